# Optimizing a Trainium2 kernel written in Bass

```python
import functools
import jax, jax.numpy as jnp
from jax import lax
import numpy as np

D_MODEL = 1024
BATCH = 8
SEQ = 4096
DEPTH = 1
DEC_BATCH = 16
DEC_SEQ = 16
PAST_LEN = 1024

CHUNK = 64
WINDOW = 128
WINDOW_CHUNKS = WINDOW // CHUNK
HEAD_DIM = 64
N_HEADS = 16
N_KV_HEADS = 4
Q_PER_KV = N_HEADS // N_KV_HEADS
D_ATTN = N_HEADS * HEAD_DIM
D_KV = N_KV_HEADS * HEAD_DIM
SCALE = HEAD_DIM ** -0.5
SSM_HEADS = 16
SSM_HEAD_DIM = 64
SSM_GROUPS = 4
HEADS_PER_GROUP = SSM_HEADS // SSM_GROUPS
SSM_STATE = 128
D_SSM = SSM_HEADS * SSM_HEAD_DIM
CONV_WIDTH = 4
CONV_DIM = D_SSM + 2 * SSM_GROUPS * SSM_STATE
D_MIX = D_ATTN + D_SSM
IN_SPLITS = (D_ATTN, D_ATTN + D_KV, D_ATTN + 2 * D_KV, D_ATTN + 2 * D_KV + D_SSM,
             D_ATTN + 2 * D_KV + D_SSM + CONV_DIM)
D_IN = D_ATTN + 2 * D_KV + D_SSM + CONV_DIM + SSM_HEADS
D_FF = -(-8 * D_MODEL // (3 * 256)) * 256
EPS = 1e-6

kernel_name = 'hymba_swa_sink_ssd_stream_step'


def rmsnorm(x, g):
    xf = x.astype(jnp.float32)
    xf = xf * lax.rsqrt(jnp.mean(xf * xf, axis=-1, keepdims=True) + EPS)
    return xf.astype(x.dtype) * g


def gated_group_rmsnorm(y, z, w):
    g = y.astype(jnp.float32) * jax.nn.silu(z.astype(jnp.float32))
    gg = g.reshape(*g.shape[:-1], SSM_GROUPS, D_SSM // SSM_GROUPS)
    gg = gg * lax.rsqrt(jnp.mean(gg * gg, axis=-1, keepdims=True) + EPS)
    return gg.reshape(g.shape) * w.astype(jnp.float32)


def sink_softmax(scores, sinks):
    sink = sinks.astype(jnp.float32).reshape(N_KV_HEADS, Q_PER_KV, 1, 1)
    m = jnp.maximum(scores.max(axis=-1, keepdims=True), sink)
    p = jnp.exp(scores - m)
    return p / (p.sum(axis=-1, keepdims=True) + jnp.exp(sink - m))


def window_attention_prompt(q, k, v, sinks):
    b, s = q.shape[:2]
    nc = s // CHUNK
    qc = q.reshape(b, nc, CHUNK, N_KV_HEADS, Q_PER_KV, HEAD_DIM)
    pad = ((0, 0), (WINDOW, 0), (0, 0), (0, 0))
    kp = jnp.pad(k, pad).reshape(b, nc + WINDOW_CHUNKS, CHUNK, N_KV_HEADS, HEAD_DIM)
    vp = jnp.pad(v, pad).reshape(b, nc + WINDOW_CHUNKS, CHUNK, N_KV_HEADS, HEAD_DIM)
    kb = jnp.concatenate([kp[:, j:j + nc] for j in range(WINDOW_CHUNKS + 1)], axis=2)
    vb = jnp.concatenate([vp[:, j:j + nc] for j in range(WINDOW_CHUNKS + 1)], axis=2)
    key_chunk = jnp.arange(nc)[:, None] + jnp.arange(-WINDOW_CHUNKS, 1)[None, :]
    valid = jnp.repeat(key_chunk >= 0, CHUNK, axis=1)
    scores = jnp.einsum('bcqkgd,bcskd->bckgqs', qc, kb, preferred_element_type=jnp.float32) * SCALE
    scores = jnp.where(valid[None, :, None, None, None, :], scores, -jnp.inf)
    p = sink_softmax(scores, sinks).astype(vb.dtype)
    o = jnp.einsum('bckgqs,bcskd->bcqkgd', p, vb)
    return o.reshape(b, s, D_ATTN)


def window_attention_sample(q, k, v, past_k, past_v, sinks):
    b, t = q.shape[:2]
    kk = jnp.concatenate([past_k.astype(k.dtype), k], axis=1)
    vv = jnp.concatenate([past_v.astype(v.dtype), v], axis=1)
    qg = q.reshape(b, t, N_KV_HEADS, Q_PER_KV, HEAD_DIM)
    scores = jnp.einsum('btkgd,bskd->bkgts', qg, kk, preferred_element_type=jnp.float32) * SCALE
    p = sink_softmax(scores, sinks).astype(vv.dtype)
    o = jnp.einsum('bkgts,bskd->btkgd', p, vv)
    return o.reshape(b, t, D_ATTN)


def causal_conv(xbc, buf, conv_w, conv_b):
    t = xbc.shape[1]
    xx = jnp.concatenate([buf.astype(xbc.dtype), xbc], axis=1)
    y = sum((xx[:, i:i + t] * conv_w[i] for i in range(CONV_WIDTH)), conv_b)
    return jax.nn.silu(y), xx[:, t:]


def ssd_scan(x, dt, a_log, bm, cm, h0, chunk):
    b, L = x.shape[:2]
    nc = L // chunk
    R = HEADS_PER_GROUP
    a = -jnp.exp(a_log.astype(jnp.float32))
    da = (dt * a).reshape(b, nc, chunk, SSM_GROUPS, R)
    xd = (x.astype(jnp.float32) * dt[..., None]).reshape(b, nc, chunk, SSM_GROUPS, R, SSM_HEAD_DIM)
    bc = bm.astype(jnp.float32).reshape(b, nc, chunk, SSM_GROUPS, SSM_STATE)
    cc = cm.astype(jnp.float32).reshape(b, nc, chunk, SSM_GROUPS, SSM_STATE)
    cs = jnp.cumsum(da, axis=2)
    causal = jnp.tril(jnp.ones((chunk, chunk), dtype=bool))[:, :, None, None]
    lmat = jnp.exp(jnp.where(causal, cs[:, :, :, None] - cs[:, :, None, :], -jnp.inf))
    cb = jnp.einsum('bclgn,bcsgn->bclsg', cc, bc)
    y_diag = jnp.einsum('bclsgr,bcsgrp->bclgrp', cb[..., None] * lmat, xd)
    decay = jnp.exp(cs[:, :, -1:] - cs)
    states = jnp.einsum('bclgn,bclgrp->bcgrpn', bc, xd * decay[..., None])
    chunk_decay = jnp.exp(cs[:, :, -1])

    def step(h, inp):
        s_c, d_c = inp
        return d_c[..., None, None] * h + s_c, h

    h_init = h0.astype(jnp.float32).reshape(b, SSM_GROUPS, R, SSM_HEAD_DIM, SSM_STATE)
    h_last, h_prev = lax.scan(step, h_init, (jnp.moveaxis(states, 1, 0), jnp.moveaxis(chunk_decay, 1, 0)))
    h_prev = jnp.moveaxis(h_prev, 0, 1)
    y_off = jnp.einsum('bclgn,bcgrpn->bclgrp', cc, h_prev) * jnp.exp(cs)[..., None]
    y = (y_diag + y_off).reshape(b, L, SSM_HEADS, SSM_HEAD_DIM)
    return y, h_last.reshape(b, SSM_HEADS, SSM_HEAD_DIM, SSM_STATE)


def hybrid_mixer(u, past_k, past_v, conv_buf, h0, w_in, conv_w, conv_b, dt_bias, a_log, d_skip,
                 ssm_norm_w, sinks, w_out):
    b, t, _ = u.shape
    q, k, v, z, xbc, dt_raw = jnp.split(u @ w_in, IN_SPLITS, axis=-1)
    q = q.reshape(b, t, N_HEADS, HEAD_DIM)
    k = k.reshape(b, t, N_KV_HEADS, HEAD_DIM)
    v = v.reshape(b, t, N_KV_HEADS, HEAD_DIM)
    if past_k is None:
        attn = window_attention_prompt(q, k, v, sinks)
        new_k, new_v = k[:, -WINDOW:], v[:, -WINDOW:]
        conv_buf = jnp.zeros((b, CONV_WIDTH - 1, CONV_DIM), u.dtype)
        h0 = jnp.zeros((b, SSM_HEADS, SSM_HEAD_DIM, SSM_STATE), jnp.float32)
        ssd_chunk = CHUNK
    else:
        attn = window_attention_sample(q, k, v, past_k, past_v, sinks)
        new_k, new_v = k, v
        ssd_chunk = t
    xbc, new_conv = causal_conv(xbc, conv_buf, conv_w, conv_b)
    xs, bm, cm = jnp.split(xbc, [D_SSM, D_SSM + SSM_GROUPS * SSM_STATE], axis=-1)
    xs = xs.reshape(b, t, SSM_HEADS, SSM_HEAD_DIM)
    bm = bm.reshape(b, t, SSM_GROUPS, SSM_STATE)
    cm = cm.reshape(b, t, SSM_GROUPS, SSM_STATE)
    dt = jax.nn.softplus(dt_raw.astype(jnp.float32) + dt_bias.astype(jnp.float32))
    y, h_last = ssd_scan(xs, dt, a_log, bm, cm, h0, ssd_chunk)
    y = y + xs.astype(jnp.float32) * d_skip.astype(jnp.float32)[:, None]
    y = gated_group_rmsnorm(y.reshape(b, t, D_SSM), z, ssm_norm_w).astype(u.dtype)
    out = jnp.concatenate([attn.astype(u.dtype), y], axis=-1) @ w_out
    return out, new_k, new_v, new_conv, h_last.astype(u.dtype)


def swiglu(u, w_gate, w_up, w_down):
    return (jax.nn.silu(u @ w_gate) * (u @ w_up)) @ w_down


def setup_inputs(seed: int = 0) -> dict:
    key = jax.random.key(seed)
    ks = jax.random.split(key, 24)
    f32 = jnp.float32
    nrm = lambda k, shape, s: jax.random.normal(k, shape, f32) * s
    dt0 = jnp.exp(jax.random.uniform(ks[10], (DEPTH, SSM_HEADS), f32, np.log(1e-3), np.log(1e-1)))
    return {
        'x_prompt': nrm(ks[0], (BATCH, SEQ, D_MODEL), 1.0),
        'x_sample': nrm(ks[1], (DEC_BATCH, DEC_SEQ, D_MODEL), 1.0),
        'cache_k': nrm(ks[2], (DEPTH, DEC_BATCH, WINDOW, N_KV_HEADS, HEAD_DIM), 1.0),
        'cache_v': nrm(ks[3], (DEPTH, DEC_BATCH, WINDOW, N_KV_HEADS, HEAD_DIM), 1.0),
        'state_conv': nrm(ks[4], (DEPTH, DEC_BATCH, CONV_WIDTH - 1, CONV_DIM), 1.0),
        'state_ssm': nrm(ks[5], (DEPTH, DEC_BATCH, SSM_HEADS, SSM_HEAD_DIM, SSM_STATE), 0.5),
        'ln1_g': 1.0 + nrm(ks[6], (DEPTH, D_MODEL), 0.01),
        'w_in': nrm(ks[7], (DEPTH, D_MODEL, D_IN), D_MODEL ** -0.5),
        'conv_w': nrm(ks[8], (DEPTH, CONV_WIDTH, CONV_DIM), CONV_WIDTH ** -0.5),
        'conv_b': nrm(ks[9], (DEPTH, CONV_DIM), 0.01),
        'dt_bias': dt0 + jnp.log(-jnp.expm1(-dt0)),
        'a_log': jnp.log(jax.random.uniform(ks[11], (DEPTH, SSM_HEADS), f32, 1.0, 16.0)),
        'd_skip': 1.0 + nrm(ks[12], (DEPTH, SSM_HEADS), 0.01),
        'ssm_norm_w': 1.0 + nrm(ks[13], (DEPTH, D_SSM), 0.01),
        'sinks': nrm(ks[14], (DEPTH, N_HEADS), 0.5),
        'w_out': nrm(ks[15], (DEPTH, D_MIX, D_MODEL), D_MIX ** -0.5),
        'ln2_g': 1.0 + nrm(ks[16], (DEPTH, D_MODEL), 0.01),
        'w_gate': nrm(ks[17], (DEPTH, D_MODEL, D_FF), D_MODEL ** -0.5),
        'w_up': nrm(ks[18], (DEPTH, D_MODEL, D_FF), D_MODEL ** -0.5),
        'w_down': nrm(ks[19], (DEPTH, D_FF, D_MODEL), D_FF ** -0.5),
        'final_g': 1.0 + nrm(ks[20], (D_MODEL,), 0.01),
    }


def reference(x_prompt, x_sample, cache_k, cache_v, state_conv, state_ssm, ln1_g, w_in, conv_w, conv_b,
              dt_bias, a_log, d_skip, ssm_norm_w, sinks, w_out, ln2_g, w_gate, w_up, w_down, final_g):
    hp, hs = x_prompt, x_sample
    kp_l, vp_l, cp_l, sp_l, ks_l, vs_l, cs_l, ss_l = [], [], [], [], [], [], [], []
    for l in range(DEPTH):
        mixer = functools.partial(hybrid_mixer, w_in=w_in[l], conv_w=conv_w[l], conv_b=conv_b[l],
                                  dt_bias=dt_bias[l], a_log=a_log[l], d_skip=d_skip[l],
                                  ssm_norm_w=ssm_norm_w[l], sinks=sinks[l], w_out=w_out[l])
        mp, kp, vp, cp, sp = mixer(rmsnorm(hp, ln1_g[l]), None, None, None, None)
        ms, kn, vn, cn, sn = mixer(rmsnorm(hs, ln1_g[l]), cache_k[l], cache_v[l], state_conv[l], state_ssm[l])
        hp = hp + mp
        hs = hs + ms
        hp = hp + swiglu(rmsnorm(hp, ln2_g[l]), w_gate[l], w_up[l], w_down[l])
        hs = hs + swiglu(rmsnorm(hs, ln2_g[l]), w_gate[l], w_up[l], w_down[l])
        kp_l.append(kp); vp_l.append(vp); cp_l.append(cp); sp_l.append(sp)
        ks_l.append(kn); vs_l.append(vn); cs_l.append(cn); ss_l.append(sn)
    y_prompt = rmsnorm(hp, final_g)
    y_sample = rmsnorm(hs, final_g)
    return (y_prompt, y_sample, jnp.stack(kp_l), jnp.stack(vp_l), jnp.stack(cp_l), jnp.stack(sp_l),
            jnp.stack(ks_l), jnp.stack(vs_l), jnp.stack(cs_l), jnp.stack(ss_l))
```

```python
import contextlib
import numpy as np
import concourse.bass as bass
import concourse.mybir as mybir
from concourse.bass_utils import run_bass_kernel_spmd

F32 = mybir.dt.float32
BF16 = mybir.dt.bfloat16
AF = mybir.ActivationFunctionType
ALU = mybir.AluOpType

ENGS = ("pe", "act", "dve", "pool", "sp")
NCORES = 8
SEQ = 4096
D = 1024
NPASS = 8
SP = 512
EPS = 1e-6
RING = 3


class _Op:
    __slots__ = ("eng", "idx", "emit", "deps", "is_dma", "dma_sem", "dma_val", "signal", "sigval")

    def __init__(self, eng, idx, emit, is_dma):
        self.eng = eng
        self.idx = idx
        self.emit = emit
        self.deps = []
        self.is_dma = is_dma
        self.dma_sem = None
        self.dma_val = 0
        self.signal = False
        self.sigval = 0


class Prog:
    NDMA_SEMS = 8

    def __init__(self, nc):
        self.nc = nc
        self.ops = {e: [] for e in ENGS}
        self.last_w = {}
        self.readers = {}
        self.dma_rr = {e: 0 for e in ENGS}
        self.dma_last = {}
        self.dma_cnt = {}

    def add(self, eng, emit, reads=(), writes=(), dma=False):
        lst = self.ops[eng]
        op = _Op(eng, len(lst), emit, dma)
        lst.append(op)
        deps = []
        bkeys = [k for k in list(reads) + list(writes) if isinstance(k, tuple) and k and k[0] == "bk"]
        if bkeys:
            reads = [k for k in reads if not (isinstance(k, tuple) and k and k[0] == "bk")]
            writes = [k for k in writes if not (isinstance(k, tuple) and k and k[0] == "bk")]
            for k in set(bkeys):
                w = self.last_w.get(k)
                if w is not None and w.eng != eng:
                    deps.append(w)
                self.last_w[k] = op
        for r in reads:
            w = self.last_w.get(r)
            if w is not None:
                deps.append(w)
        for w_ in writes:
            w = self.last_w.get(w_)
            if w is not None:
                deps.append(w)
            deps.extend(self.readers.get(w_, ()))
        if dma:
            j = self.dma_rr[eng] % self.NDMA_SEMS
            self.dma_rr[eng] += 1
            prev = self.dma_last.get((eng, j))
            if prev is not None:
                deps.append(prev)
            self.dma_last[(eng, j)] = op
            c = self.dma_cnt.get((eng, j), 0) + 1
            self.dma_cnt[(eng, j)] = c
            op.dma_sem = (eng, j)
            op.dma_val = 16 * c
        seen = set()
        best = {}
        for d in deps:
            if d is op or id(d) in seen:
                continue
            seen.add(id(d))
            if d.is_dma:
                op.deps.append(d)
                continue
            b = best.get(d.eng)
            if b is None or d.idx > b.idx:
                best[d.eng] = d
        for d in best.values():
            if d.eng == eng and eng == "pe":
                continue
            d.signal = True
            op.deps.append(d)
        for r in reads:
            lst2 = self.readers.setdefault(r, [])
            if not dma:
                for k_ in range(len(lst2)):
                    if (not lst2[k_].is_dma) and lst2[k_].eng == eng:
                        lst2[k_] = op
                        break
                else:
                    lst2.append(op)
            else:
                lst2.append(op)
        for w_ in writes:
            self.last_w[w_] = op
            self.readers[w_] = []
        return op

    def emit_all(self):
        nc = self.nc
        with contextlib.ExitStack() as es:
            esem = {e: es.enter_context(nc.semaphore("s_" + e)) for e in ENGS}
            dsem = {}
            for e in ENGS:
                for j in range(min(self.NDMA_SEMS, self.dma_rr[e])):
                    dsem[(e, j)] = es.enter_context(nc.semaphore("d_%s%d" % (e, j)))
            for e in ENGS:
                c = 0
                for op in self.ops[e]:
                    if op.signal and not op.is_dma:
                        c += 1
                        op.sigval = c
            block = es.enter_context(nc.Block())

            def run(eng_name, eng):
                waited = {}
                for op in self.ops[eng_name]:
                    for d in op.deps:
                        if d.is_dma:
                            key = ("d",) + d.dma_sem
                            sem = dsem[d.dma_sem]
                            val = d.dma_val
                        else:
                            key = ("e", d.eng)
                            sem = esem[d.eng]
                            val = d.sigval
                        if waited.get(key, 0) >= val:
                            continue
                        waited[key] = val
                        eng.wait_ge(sem, val)
                    ins = op.emit(eng)
                    if op.is_dma:
                        ins.then_inc(dsem[op.dma_sem], 16)
                    elif op.signal:
                        ins.then_inc(esem[eng_name], 1)
                for (e, j), lastop in self.dma_last.items():
                    if e == eng_name and waited.get(("d", e, j), 0) < lastop.dma_val:
                        eng.wait_ge(dsem[(e, j)], lastop.dma_val)

            @block.sync
            def _(eng):
                run("sp", eng)

            @block.tensor
            def _(eng):
                run("pe", eng)

            @block.scalar
            def _(eng):
                run("act", eng)

            @block.vector
            def _(eng):
                run("dve", eng)

            @block.gpsimd
            def _(eng):
                run("pool", eng)


def _head_pairs():
    pairs = []
    for j in range(8):
        if j < 4:
            pairs.append((j, 4 + j))
        else:
            pairs.append((8 + (j - 4), 12 + (j - 4)))
    return pairs


def _slotify(w, cols):
    sub = w[:, cols]
    K = sub.shape[0]
    return np.ascontiguousarray(sub.reshape(K // 128, 128, len(cols)).transpose(1, 0, 2).reshape(128, -1))


def _host_layout(inp):
    f = np.float32
    pairs = _head_pairs()
    w_in = np.asarray(inp["w_in"][0], f)
    qcols = []
    for (a, b) in pairs:
        qcols += list(range(a * 64, a * 64 + 64)) + list(range(b * 64, b * 64 + 64))
    qcols = np.array(qcols)
    slots = [qcols[0:512], qcols[512:1024], np.arange(1024, 1536), np.arange(1536, 2048), np.arange(2048, 2560)]
    for s in range(4):
        slots.append(np.arange(2560 + s * 512, 2560 + (s + 1) * 512))
    win = np.stack([_slotify(w_in, c) for c in slots])
    wdt = _slotify(w_in, np.arange(4608, 4624))
    w_out = np.asarray(inp["w_out"][0], f)
    rowperm = np.concatenate([qcols, np.arange(1024, 2048)])
    wo = w_out[rowperm]
    wout = np.stack([_slotify(wo[kh * 1024:(kh + 1) * 1024], np.arange(ch * 512, (ch + 1) * 512))
                     for ch in range(2) for kh in range(2)])
    wg = np.asarray(inp["w_gate"][0], f)
    wu = np.asarray(inp["w_up"][0], f)
    wgu = np.stack([np.concatenate([_slotify(wg, np.arange(s * 256, (s + 1) * 256)).reshape(128, 8, 256),
                                    _slotify(wu, np.arange(s * 256, (s + 1) * 256)).reshape(128, 8, 256)],
                                   axis=2).reshape(128, 4096) for s in range(11)])
    wd = np.asarray(inp["w_down"][0], f)
    wdp = np.zeros((3072, 1024), f)
    wdp[:2816] = wd
    wdn = np.stack([_slotify(wdp[kp * 1024:(kp + 1) * 1024], np.arange(ch * 512, (ch + 1) * 512))
                    for ch in range(2) for kp in range(3)])

    def colT(v):
        v = np.asarray(v, f)
        return np.ascontiguousarray(v.reshape(-1, 128).T)

    def rep(v):
        v = np.asarray(v, f)
        return np.ascontiguousarray(np.broadcast_to(v[None, :], (128, v.shape[0])))

    cw = np.asarray(inp["conv_w"][0], f)
    convw = np.ascontiguousarray(cw.reshape(4, 16, 128).transpose(2, 0, 1).reshape(128, 64))
    dsk = np.asarray(inp["d_skip"][0], f)
    sinks = np.asarray(inp["sinks"][0], f)
    dskT = np.zeros((128, 8), f)
    sinkT = np.zeros((128, 8), f)
    for j in range(8):
        dskT[:64, j] = dsk[2 * j]
        dskT[64:, j] = dsk[2 * j + 1]
        sinkT[:64, j] = sinks[pairs[j][0]]
        sinkT[64:, j] = sinks[pairs[j][1]]
    small = np.concatenate([
        colT(inp["ln1_g"][0]), colT(inp["ln2_g"][0]), convw, colT(inp["conv_b"][0]),
        rep(inp["dt_bias"][0]), rep(inp["a_log"][0]), dskT, colT(inp["ssm_norm_w"][0]), sinkT], axis=1)
    shared = {"win": win, "wdt": wdt, "wout": wout, "wgu": wgu, "wdn": wdn,
              "small": np.ascontiguousarray(small), "fg": rep(inp["final_g"])}
    return shared


class Ctx:
    pass


def build_program():
    nc = bass.Bass("TRN2", target_bir_lowering=False)

    def din(name, shape):
        return nc.dram_tensor(name, shape, F32, kind="ExternalInput").ap()

    def dout(name, shape):
        return nc.dram_tensor(name, shape, F32, kind="ExternalOutput").ap()

    xp_d = din("xp", [SEQ, D])
    xs_d = din("xs", [2, 16, D])
    ck_d = din("ck", [2, 128, 256])
    cv_d = din("cv", [2, 128, 256])
    sconv_d = din("sconv", [2, 3, 2048])
    sssm_d = din("sssm", [2, 1024, 128])
    win_d = din("win", [9, 128, 4096])
    wdt_d = din("wdt", [128, 128])
    wout_d = din("wout", [4, 128, 4096])
    wgu_d = din("wgu", [11, 128, 4096])
    wdn_d = din("wdn", [6, 128, 4096])
    small_d = din("small", [128, 152])
    fg_d = din("fg", [128, 1024])
    yp_d = dout("yp", [SEQ, D])
    ys_d = dout("ys", [2, 16, D])
    kp_d = dout("kp", [128, 256])
    vp_d = dout("vp", [128, 256])
    cp_d = dout("cp", [3, 2048])
    sp_d = dout("spo", [1024, 128])
    ksn_d = dout("ksn", [2, 16, 256])
    vsn_d = dout("vsn", [2, 16, 256])
    csn_d = dout("csn", [2, 3, 2048])
    ssn_d = dout("ssn", [2, 1024, 128])

    es = contextlib.ExitStack()
    with es:
        def sb(name, shape, dt=F32):
            return es.enter_context(nc.sbuf_tensor("sb_" + name, shape, dt))

        P = Prog(nc)
        bank = [es.enter_context(nc.psum_tensor("bk%d" % i, [128, 512], F32)) for i in range(8)]

        def BK(i):
            return ("bk", i)

        def bk3(i, n, T, rows=128, inner=128):
            return bank[i][:, 0:n * inner].rearrange("p (a b) -> p a b", b=inner)[0:rows, :, 0:T]

        def bkbf(i):
            return bank[i][:, :].bitcast(BF16)

        def MM(out, lhsT, rhs, start, stop, reads, writes):
            P.add("pe", lambda e: e.matmul(out, lhsT=lhsT, rhs=rhs, start=start, stop=stop), reads, writes)

        def TR(out, in_, ident, reads, writes):
            P.add("pe", lambda e: e.transpose(out=out, in_=in_, identity=ident), reads, writes)

        def ACT(out, in_, func, reads, writes, bias=None, scale=None, accum=None):
            kw = {}
            if bias is not None:
                kw["bias"] = bias
            if scale is not None:
                kw["scale"] = scale
            if accum is not None:
                kw["accum_out"] = accum
            P.add("act", lambda e: e.activation(out=out, in_=in_, func=func, **kw), reads, writes)

        def TT(eng, out, in0, in1, op, reads, writes):
            P.add(eng, lambda e: e.tensor_tensor(out=out, in0=in0, in1=in1, op=op), reads, writes)

        def TS(eng, out, in0, s1, s2, op0, op1, reads, writes):
            if s2 is None:
                P.add(eng, lambda e: e.tensor_scalar(out=out, in0=in0, scalar1=s1, scalar2=None, op0=op0), reads, writes)
            else:
                P.add(eng, lambda e: e.tensor_scalar(out=out, in0=in0, scalar1=s1, scalar2=s2, op0=op0, op1=op1),
                      reads, writes)

        def STT(eng, out, in0, scalar, in1, op0, op1, reads, writes):
            P.add(eng, lambda e: e.scalar_tensor_tensor(out=out, in0=in0, scalar=scalar, in1=in1, op0=op0, op1=op1),
                  reads, writes)

        def CP(eng, out, in_, reads, writes):
            if eng == "act":
                P.add("act", lambda e: e.copy(out=out, in_=in_), reads, writes)
            else:
                P.add(eng, lambda e: e.tensor_copy(out=out, in_=in_), reads, writes)

        def MS(eng, ap, val, writes, reads=()):
            P.add(eng, lambda e: e.memset(ap, val), reads, writes)

        def DMA(eng, out, in_, reads, writes):
            P.add(eng, lambda e: e.dma_start(out=out, in_=in_), reads, writes, dma=True)

        evac_rr = [0]

        def EV(out, in_, reads, writes):
            evac_rr[0] += 1
            CP("act" if evac_rr[0] % 2 else "dve", out, in_, reads, writes)

        small = sb("small", [128, 152])
        fg = sb("fg", [128, 1024])
        ident_f = sb("ident_f", [128, 128])
        ident_b = sb("ident_b", [128, 128], BF16)
        ones_b = sb("ones_b", [128, 128], BF16)
        Uincl = sb("Uincl", [128, 128])
        Blk = sb("Blk", [128, 128])
        Sel0 = sb("Sel0", [128, 128])
        Sel1 = sb("Sel1", [128, 128])
        maskneg = sb("maskneg", [128, 128])
        negh = sb("negh", [128, 4])
        mbias = sb("mbias", [128, 2])
        aneg = sb("aneg", [128, 16])
        esink = sb("esink", [128, 8])
        diagW = sb("diagW", [128, 64, 128], BF16)
        wdt = sb("wdt", [128, 8, 16], BF16)
        g1T = small[:, 0:8]
        g2T = small[:, 8:16]
        convb = small[:, 80:96]
        dtb = small[:, 96:112]
        dskT = small[:, 128:136]
        normwT = small[:, 136:144]

        DMA("sp", small[:, :], small_d, [], ["small"])
        DMA("sp", fg[:, :], fg_d, [], ["fg"])
        DMA("pool", wdt[:, :, :], wdt_d.rearrange("p (k c) -> p k c", c=16), [], ["wdt"])
        MS("pool", ident_f[:, :], 1.0, ["ident_f"])
        P.add("pool", lambda e: e.affine_select(out=ident_f[:, :], in_=ident_f[:, :], pattern=[[-1, 128]],
                                                 compare_op=ALU.is_equal, fill=0.0, base=0, channel_multiplier=1),
              ["ident_f"], ["ident_f"])
        CP("dve", ident_b[:, :], ident_f[:, :], ["ident_f"], ["ident_b"])
        MS("dve", ones_b[:, :], 1.0, ["ones_b"])
        MS("pool", Uincl[:, :], 1.0, ["Uincl"])
        P.add("pool", lambda e: e.affine_select(out=Uincl[:, :], in_=Uincl[:, :], pattern=[[1, 128]],
                                                 compare_op=ALU.is_ge, fill=0.0, base=0, channel_multiplier=-1),
              ["Uincl"], ["Uincl"])
        MS("pool", Uincl[0:64, 64:128], 0.0, ["Uincl"], ["Uincl"])
        MS("dve", Blk[:, :], 0.0, ["Blk"])
        MS("dve", Blk[0:64, 0:64], 1.0, ["Blk"], ["Blk"])
        MS("dve", Blk[64:128, 64:128], 1.0, ["Blk"], ["Blk"])
        MS("dve", Sel0[:, :], 0.0, ["Sel0"])
        MS("dve", Sel0[0:64, :], 1.0, ["Sel0"], ["Sel0"])
        MS("dve", Sel1[:, :], 0.0, ["Sel1"])
        MS("dve", Sel1[64:128, :], 1.0, ["Sel1"], ["Sel1"])
        TS("dve", maskneg[:, :], Uincl[:, :], 1.0, 30000.0, ALU.subtract, ALU.mult, ["Uincl"], ["maskneg"])
        MS("pool", negh[:, :], -0.5, ["negh"])
        MS("dve", mbias[:, :], 0.0, ["mbias"])
        MS("dve", mbias[0:64, 0:1], -30000.0, ["mbias"], ["mbias"])
        MS("dve", mbias[64:128, 1:2], -30000.0, ["mbias"], ["mbias"])
        ACT(aneg[:, :], small[:, 112:128], AF.Exp, ["small"], ["aneg"])
        TS("dve", aneg[:, :], aneg[:, :], -1.0, None, ALU.mult, None, ["aneg"], ["aneg"])
        ACT(esink[:, :], small[:, 144:152], AF.Exp, ["small"], ["esink"])
        for ic in range(64):
            TS("dve" if ic % 2 else "pool", diagW[:, ic, :], ident_f[:, :], small[:, 16 + ic:17 + ic], None, ALU.mult, None,
               ["ident_f", "small"], [("diagW", ic)])

        un2 = [sb("un%d" % i, [128, 1024], BF16) for i in range(2)]
        ss4 = sb("ss4", [128, 4])
        vv4 = sb("vv4", [128, 4])
        rstd4 = sb("rstd4", [128, 4])
        PT = [[sb("PT%d%d" % (a, b), [128, 512], BF16) for b in range(2)] for a in range(2)]
        rd = sb("rd", [128, 512])
        ost = sb("ost", [128, 512])
        yo = sb("yo", [128, 1024])
        sgt = [sb("sgt%d" % i, [128, 512], BF16) for i in range(2)]
        d2 = sb("d2", [128, 16])
        cs_sb = sb("cs_sb", [128, 16])
        dd = sb("dd", [128, 16])
        decay = sb("decay", [128, 16])
        dtd = sb("dtd", [128, 16])
        dB = [sb("dB%d" % i, [128, 16]) for i in range(2)]
        csT_hi = sb("csT_hi", [16, 128], BF16)
        csT_lo = sb("csT_lo", [16, 128], BF16)
        xd = sb("xd", [128, 1024], BF16)
        xdd = sb("xdd", [128, 1024], BF16)
        Btok = sb("Btok", [128, 512], BF16)
        tmp2 = [sb("tmp%d" % i, [128, 512]) for i in range(2)]
        lm2 = [sb("lm%d" % i, [128, 512], BF16) for i in range(2)]
        MT2 = [sb("MT%d" % i, [128, 512], BF16) for i in range(2)]
        ecs2 = [sb("ecs%d" % i, [128, 512], BF16) for i in range(2)]
        Cs2 = [sb("Cs%d" % i, [128, 512], BF16) for i in range(2)]
        yf = sb("yf", [128, 1024])
        t1 = yf
        sq = sb("sq", [128, 1024], BF16)
        rv = sb("rv", [128, 512])

        ring = [sb("ring%d" % i, [128, 8, 512], BF16) for i in range(RING)]

        def mkctx(name, S, NT, T, nchunk, CL, sample):
            c = Ctx()
            c.name, c.S, c.NT, c.T, c.nchunk, c.CL, c.sample = name, S, NT, T, nchunk, CL, sample
            c.HK = 128
            c.xh = sb(name + "xh", [128, NT, 1024])
            c.uT = sb(name + "uT", [128, 8, S], BF16)
            c.qT = sb(name + "qT", [128, 8, S], BF16)
            c.kT = sb(name + "kT", [128, 2, 128 + S], BF16)
            c.vx = sb(name + "vx", [128, NT + 1, 256], BF16)
            c.szT = sb(name + "szT", [128, 8, S], BF16)
            if sample:
                c.rawT = sb(name + "rawT", [128, 16, S + 3], BF16)
                c.xbcT = sb(name + "xbcT", [128, 16, S], BF16)
                c.hmid = sb(name + "hmid", [128, 22, S], BF16)
            else:
                big = sb(name + "big", [128, 16 * (S + 3) + 16 * S], BF16)
                c.rawT = big[:, 0:16 * (S + 3)].rearrange("p (c n) -> p c n", n=S + 3)
                c.xbcT = big[:, 16 * (S + 3):].rearrange("p (c n) -> p c n", n=S)
                c.hmid = big[:, 0:22 * S].rearrange("p (c n) -> p c n", n=S)
            c.alias = not sample
            c.chist = sb(name + "chist", [128, 16, 3], BF16)
            c.h = sb(name + "h", [128, 1024])
            c.d1 = sb(name + "d1", [128, NT, 16])
            c.dtt = sb(name + "dtt", [128, NT, 16])
            c.da = sb(name + "da", [128, NT, 16])
            c.nhbf = 2 if sample else 3
            c.hbf = [sb(name + "hbf%d" % i, [128, 1024], BF16) for i in range(c.nhbf)]
            c.hb = 0
            return c

        pc = mkctx("p", SP, 4, 128, 2, 64, False)
        sc = mkctx("s", 16, 1, 16, 1, 16, True)
        sconv_sb = sb("sconv_sb", [3, 2048])
        sst = yo[:, :].rearrange("p (a b) -> p a b", b=128)

        def K(c, *a):
            return (c.name,) + a

        def bigR(c):
            return [K(c, "bigR")] if c.alias else []

        def bigW(c):
            return [K(c, "bigW")] if c.alias else []

        wseq = []
        for st in range(NPASS):
            for s in range(9):
                wseq.append((win_d[s], 8))
            for s in range(4):
                wseq.append((wout_d[s], 8))
            for s in range(11):
                wseq.append((wgu_d[s], 8))
            for s in range(6):
                wseq.append((wdn_d[s], 6 if s % 3 == 2 else 8))
        wstate = {"issued": 0, "cur": -1}

        def wprefetch(upto):
            while wstate["issued"] <= min(upto, len(wseq) - 1):
                n = wstate["issued"]
                src, nk = wseq[n]
                r = n % RING
                DMA("pool", ring[r][:, 0:nk, :], src[:, 0:nk * 512].rearrange("p (k c) -> p k c", c=512),
                    [], [("ring", r)])
                wstate["issued"] += 1

        def wnext():
            wstate["cur"] += 1
            n = wstate["cur"]
            wprefetch(n + RING - 1)
            return ring[n % RING], ("ring", n % RING)

        def tcols(c, t):
            return slice(t * c.T, (t + 1) * c.T)

        def stats4(c):
            T = c.T
            sk = [("ss4", t) for t in range(c.NT)]
            MS("dve", ss4[:T, 0:c.NT], 0.0, sk)
            for t in range(c.NT):
                ACT(un2[t % 2][:T, :], c.xh[:T, t, :], AF.Square, [K(c, "xh", t)], [("un", t % 2), ("ss4", t)],
                    accum=ss4[:T, t:t + 1])
            TS("dve", vv4[:T, 0:c.NT], ss4[:T, 0:c.NT], 1.0 / D, EPS, ALU.mult, ALU.add, sk, ["vv4"])
            TT("pool", rstd4[:T, 0:c.NT], vv4[:T, 0:c.NT], negh[:T, 0:c.NT], ALU.pow, ["vv4", "negh"], ["rstd4"])

        def norm_stage(c, gT):
            T = c.T
            stats4(c)

            def scale(t):
                TS("dve", un2[t % 2][:T, :], c.xh[:T, t, :], rstd4[:T, t:t + 1], None, ALU.mult, None,
                   [K(c, "xh", t), "rstd4"], [("un", t % 2)])
                bv = bkbf(t % 2)
                for kc in range(8):
                    TR(bv[:, kc * 128:kc * 128 + T], un2[t % 2][:T, kc * 128:(kc + 1) * 128], ident_b[:T, :T],
                       [("un", t % 2), "ident_b"], [BK(t % 2)])

            def evac(t):
                bv = bkbf(t % 2)
                TT("dve", c.uT[:, :, tcols(c, t)], bv[:, 0:1024].rearrange("p (a b) -> p a b", b=128)[:, :, 0:T],
                   gT.unsqueeze(2).broadcast_to([128, 8, T]), ALU.mult, [BK(t % 2), "small"], [K(c, "uT")])

            scale(0)
            for t in range(1, c.NT):
                scale(t)
                evac(t - 1)
            evac(c.NT - 1)

        fb = [0]

        fb_mod = [4]

        def fbank():
            fb[0] = (fb[0] + 1) % fb_mod[0]
            return fb[0]

        def feat_chunk(c, W, wk, col0, dst, dkeys_r, dkeys_w, func=None, bias=None):
            b = fbank()
            S = c.S
            for kc in range(8):
                MM(bank[b][:, 0:S], W[:, kc, col0:col0 + 128], c.uT[:, kc, 0:S], kc == 0, kc == 7,
                   [wk, K(c, "uT")], [BK(b)])
            if func is None:
                EV(dst, bank[b][:, 0:S], [BK(b)] + dkeys_r, dkeys_w)
            else:
                ACT(dst, bank[b][:, 0:S], func, [BK(b)] + dkeys_r, dkeys_w, bias=bias)

        def run_streams(items):
            items = list(items)
            while items:
                for item in list(items):
                    g_, rep = item
                    for _ in range(rep):
                        try:
                            next(g_)
                        except StopIteration:
                            items.remove(item)
                            break

        def in_slot(ctxs, st, s):
            W, wk = wnext()
            for c in ctxs:
                S, T = c.S, c.T
                last = c.sample or st == NPASS - 1
                if s < 2:
                    for cc in range(4):
                        j = 4 * s + cc
                        feat_chunk(c, W, wk, cc * 128, c.qT[:, j, 0:S], [],
                                   [K(c, "qT", t, j // 4) for t in range(c.NT)])
                elif s == 2:
                    for cc in range(2):
                        feat_chunk(c, W, wk, cc * 128, c.kT[:, cc, 128:128 + S], [], [K(c, "kT")])
                    for t in range(c.NT):
                        b = 4 + t % 4
                        for kc in range(8):
                            MM(bank[b][:T, 0:512], c.uT[:, kc, tcols(c, t)], W[:, kc, :], kc == 0, kc == 7,
                               [wk, K(c, "uT")], [BK(b)])
                        CP("dve", c.vx[:T, 1 + t, :], bank[b][:T, 256:512], [BK(b)], [K(c, "vx")])
                        if last and t == c.NT - 1:
                            CP("act", ost[:T, :], bank[b][:T, :], [BK(b)], ["ost"])
                            if c.sample:
                                DMA("sp", ksn_d[c.seq], ost[:T, 0:256], ["ost"], [])
                                DMA("sp", vsn_d[c.seq], ost[:T, 256:512], ["ost"], [])
                            else:
                                DMA("sp", kp_d, ost[:T, 0:256], ["ost"], [])
                                DMA("sp", vp_d, ost[:T, 256:512], ["ost"], [])
                    for t in range(c.NT):
                        b = 4 + t % 4
                        for kc in range(8):
                            MM(bank[b][:T, 0:16], c.uT[:, kc, tcols(c, t)], wdt[:, kc, :], kc == 0, kc == 7,
                               ["wdt", K(c, "uT")], [BK(b)])
                        TT("dve", c.d1[:T, t, :], bank[b][:T, 0:16], dtb[:T, :], ALU.add, [BK(b), "small"], [K(c, "d1")])
                elif s < 5:
                    for cc in range(4):
                        j = 4 * (s - 3) + cc
                        feat_chunk(c, W, wk, cc * 128, c.szT[:, j, 0:S], [],
                                   [K(c, "szT", t) for t in range(c.NT)], func=AF.Silu)
                else:
                    for cc in range(4):
                        ch = 4 * (s - 5) + cc
                        feat_chunk(c, W, wk, cc * 128, c.rawT[:, ch, 3:3 + S], bigR(c),
                                   [K(c, "rawT", ch)] + bigW(c))
                        yield
                    if last:
                        t = c.NT - 1
                        b = fbank()
                        for kc in range(8):
                            MM(bank[b][:T, 0:512], c.uT[:, kc, tcols(c, t)], W[:, kc, :], kc == 0, kc == 7,
                               [wk, K(c, "uT")], [BK(b)])
                        CP("act", ost[:T, :], bank[b][:T, :], [BK(b)], ["ost"])
                        dst = csn_d[c.seq] if c.sample else cp_d
                        DMA("sp", dst[:, (s - 5) * 512:(s - 4) * 512], ost[T - 3:T, :], ["ost"], [])

        def in_proj(ctxs, st):
            for s in range(5):
                for _ in in_slot(ctxs, st, s):
                    pass

            def xbc_stream():
                for s in range(5, 9):
                    yield from in_slot(ctxs, st, s)

            def attn_stream(c):
                for t in range(c.NT):
                    has_prev = c.sample or not (st == 0 and t == 0)
                    yield from attention(c, t, has_prev)

            fb_mod[0] = 2
            run_streams([(xbc_stream(), 2)] + [(attn_stream(c), 1) for c in ctxs])
            fb_mod[0] = 4
            for c in ctxs:
                if not c.sample:
                    CP("dve", c.kT[:, :, 0:128], c.kT[:, :, c.S:c.S + 128], [K(c, "kT")], [K(c, "kT")])
                    CP("dve", c.vx[:, 0, :], c.vx[:, c.NT, :], [K(c, "vx")], [K(c, "vx")])

        def conv_stage(c):
            S = c.S
            if not c.sample:
                CP("dve", c.rawT[:, :, 0:3], c.chist[:, :, :], [K(c, "chist")] + bigR(c),
                   [K(c, "rawT", ch) for ch in range(16)] + bigW(c))
            for ch in range(16):
                b = fbank()
                for i in range(4):
                    MM(bank[b][:, 0:S], diagW[:, i * 16 + ch, :], c.rawT[:, ch, i:i + S], i == 0, i == 3,
                       [("diagW", i * 16 + ch), K(c, "rawT", ch)] + bigR(c), [BK(b)])
                ACT(c.xbcT[:, ch, 0:S], bank[b][:, 0:S], AF.Silu, [BK(b), "small"] + bigR(c),
                    [K(c, "xbcT", ch)] + bigW(c), bias=convb[:, ch:ch + 1])
            if not c.sample:
                CP("dve", c.chist[:, :, :], c.rawT[:, :, S:S + 3],
                   [K(c, "rawT", ch) for ch in range(16)] + bigR(c), [K(c, "chist")])

        def softplus_stage(c):
            T = c.T
            for t in range(c.NT):
                ACT(d2[:T, :], c.d1[:T, t, :], AF.Exp, [K(c, "d1")], ["d2"])
                ACT(c.dtt[:T, t, :], d2[:T, :], AF.Ln, ["d2"], [K(c, "dtt")], bias=1.0)
                TT("dve", c.da[:T, t, :], c.dtt[:T, t, :], aneg[:T, :], ALU.mult, [K(c, "dtt"), "aneg"], [K(c, "da")])

        def attention(c, t, has_prev):
            T = c.T
            N = 4 * T
            qc = tcols(c, t)
            prevc = slice(t * 128, t * 128 + 128)
            curc = slice(128 + t * T, 128 + t * T + T)
            bO, bD = 6, 7

            def pview(ap_, lo, hi):
                return ap_[0:128, 0:4 * T].rearrange("p (a b) -> p a b", b=T)[:, :, lo:hi]

            for kc in range(2):
                j0 = 4 * kc
                qk = K(c, "qT", t, kc)
                gs = (2 * kc, 2 * kc + 1)
                for g in gs:
                    hh = g % 2
                    ps = slice(hh * 64, hh * 64 + 64)
                    bA, bB = (4, 5) if hh == 0 else (2, 3)
                    rq = c.qT[ps, j0:j0 + 4, qc]
                    if has_prev:
                        MM(bank[bA][:, 0:N], c.kT[ps, kc, prevc], rq, True, True, [K(c, "kT"), qk], [BK(bA)])
                    MM(bank[bB][:T, 0:N], c.kT[ps, kc, curc], rq, True, True, [K(c, "kT"), qk], [BK(bB)])
                for g in gs:
                    hh = g % 2
                    bA, bB = (4, 5) if hh == 0 else (2, 3)
                    pa, pb = PT[hh]
                    if has_prev:
                        if c.sample:
                            ACT(pa[:, 0:N], bank[bA][:, 0:N], AF.Exp, [BK(bA)], [("PT", hh, 0, "a"), ("PT", hh, 0, "b")],
                                scale=0.125)
                        else:
                            ACT(pview(pa, 0, 64), pview(bank[bA], 0, 64), AF.Exp, [BK(bA)], [("PT", hh, 0, "a")], scale=0.125)
                            ACT(pview(pa, 64, 128), pview(bank[bA], 64, 128), AF.Exp, [BK(bA), "mbias"],
                                [("PT", hh, 0, "b")], scale=0.125, bias=mbias[:, 0:1])
                    if c.sample:
                        ACT(pb[:T, 0:N], bank[bB][:T, 0:N], AF.Exp, [BK(bB)], [("PT", hh, 1, "a"), ("PT", hh, 1, "b")],
                            scale=0.125)
                    else:
                        ACT(pview(pb, 0, 64), pview(bank[bB], 0, 64), AF.Exp, [BK(bB), "mbias"], [("PT", hh, 1, "a")],
                            scale=0.125, bias=mbias[:, 1:2])
                        ACT(pview(pb, 64, 128), pview(bank[bB], 64, 128), AF.Exp, [BK(bB)], [("PT", hh, 1, "b")], scale=0.125)
                for g in gs:
                    hh = g % 2
                    ps = slice(hh * 64, hh * 64 + 64)
                    pa, pb = PT[hh]
                    vs = slice(g * 64, g * 64 + 64)
                    ka = [("PT", hh, 0, "a"), ("PT", hh, 0, "b")]
                    kb = [("PT", hh, 1, "a"), ("PT", hh, 1, "b")]
                    if has_prev:
                        MM(bank[bO][ps, 0:N], c.vx[:, t, vs], pa[:, 0:N], True, False, [K(c, "vx")] + ka, [BK(bO)])
                    MM(bank[bO][ps, 0:N], c.vx[:T, t + 1, vs], pb[:T, 0:N], not has_prev, True, [K(c, "vx")] + kb, [BK(bO)])
                    if has_prev:
                        MM(bank[bD][ps, 0:N], ones_b[:, 0:64], pa[:, 0:N], True, False, ["ones_b"] + ka, [BK(bD)])
                    MM(bank[bD][ps, 0:N], ones_b[:T, 0:64], pb[:T, 0:N], not has_prev, True, ["ones_b"] + kb, [BK(bD)])
                rdv = rd[:, 0:N].rearrange("p (a b) -> p a b", b=T)
                TT("dve", rdv, bank[bD][:, 0:N].rearrange("p (a b) -> p a b", b=T),
                   esink[:, j0:j0 + 4].unsqueeze(2).broadcast_to([128, 4, T]), ALU.add, [BK(bD), "esink"], ["rd"])
                ACT(rd[:, 0:N], rd[:, 0:N], AF.Ln, ["rd"], ["rd"])
                ACT(rd[:, 0:N], rd[:, 0:N], AF.Exp, ["rd"], ["rd"], scale=-1.0)
                TT("dve", c.qT[:, j0:j0 + 4, qc], bank[bO][:, 0:N].rearrange("p (a b) -> p a b", b=T), rdv,
                   ALU.mult, [BK(bO), "rd"], [qk])
                yield

        def ssd(c, t):
            T, CL = c.T, c.CL
            tc_ = tcols(c, t)
            xk = [K(c, "xbcT", ch) for ch in range(16)]
            dav = c.da[:T, t, :]
            MM(bank[0][:T, 0:16], Uincl[:T, :T], dav, True, True, ["Uincl", K(c, "da")], [BK(0)])
            MM(bank[0][:T, 16:32], Blk[:T, :T], dav, True, True, ["Blk", K(c, "da")], [BK(0)])
            MM(bank[0][:, 32:48], Sel0[:T, :], dav, True, True, ["Sel0", K(c, "da")], [BK(0)])
            if c.nchunk == 2:
                MM(bank[0][:, 48:64], Sel1[:T, :], dav, True, True, ["Sel1", K(c, "da")], [BK(0)])
            MM(bank[0][0:16, 64:64 + T], dav, Uincl[:T, :T], True, True, ["Uincl", K(c, "da")], [BK(0)])
            b1v = bkbf(1)
            b2v = bkbf(2)
            for j in range(8):
                TR(b1v[:T, j * 128:(j + 1) * 128], c.xbcT[:, j, tc_], ident_b[:, :], xk + ["ident_b"], [BK(1)])
            for g in range(4):
                TR(b2v[:T, g * 128:(g + 1) * 128], c.xbcT[:, 8 + g, tc_], ident_b[:, :], xk + ["ident_b"], [BK(2)])
            CP("dve", cs_sb[:T, :], bank[0][:T, 0:16], [BK(0)], ["cs_sb"])
            TT("dve", dd[:T, :], bank[0][:T, 16:32], cs_sb[:T, :], ALU.subtract, [BK(0), "cs_sb"], ["dd"])
            ACT(decay[:T, :], dd[:T, :], AF.Exp, ["dd"], ["decay"])
            ACT(dB[0][:, :], bank[0][:, 32:48], AF.Exp, [BK(0)], [("dB", 0)])
            if c.nchunk == 2:
                ACT(dB[1][:, :], bank[0][:, 48:64], AF.Exp, [BK(0)], [("dB", 1)])
            CP("act", csT_hi[:, :T], bank[0][0:16, 64:64 + T], [BK(0)], ["csT_hi"])
            TT("dve", csT_lo[:, :T], bank[0][0:16, 64:64 + T], csT_hi[:, :T], ALU.subtract, [BK(0), "csT_hi"], ["csT_lo"])
            TT("dve", dtd[:T, :], c.dtt[:T, t, :], decay[:T, :], ALU.mult, [K(c, "dtt"), "decay"], ["dtd"])
            xv = b1v[:T, 0:1024].rearrange("p (a b) -> p a b", b=64)
            TT("dve", xdd[:T, :].rearrange("p (a b) -> p a b", b=64), xv,
               dtd[:T, :].unsqueeze(2).broadcast_to([T, 16, 64]), ALU.mult, [BK(1), "dtd"], ["xdd"])
            CP("act", Btok[:T, :], b2v[:T, 0:512], [BK(2)], ["Btok"])
            TT("dve", xd[:T, :].rearrange("p (a b) -> p a b", b=64), xv,
               c.dtt[:T, t, :].unsqueeze(2).broadcast_to([T, 16, 64]), ALU.mult, [BK(1), K(c, "dtt")], ["xd"])

            def pe_group(g):
                bc, bs = (4, 5) if g % 2 == 0 else (6, 7)
                MM(bank[bc][:T, 0:T], c.xbcT[:, 8 + g, tc_], c.xbcT[:, 12 + g, tc_], True, True, xk, [BK(bc)])
                for r in range(4):
                    h = 4 * g + r
                    MM(bank[bs][:, r * 128:r * 128 + T], ident_b[0:16, h:h + 1].broadcast_to([16, 128]),
                       csT_hi[0:16, :T], True, False, ["ident_b", "csT_hi"], [BK(bs)])
                    MM(bank[bs][:, r * 128:r * 128 + T], ident_b[0:16, h:h + 1].broadcast_to([16, 128]),
                       csT_lo[0:16, :T], False, True, ["ident_b", "csT_lo"], [BK(bs)])

            def stt_group(g):
                bc, bs = (4, 5) if g % 2 == 0 else (6, 7)
                p = g % 2
                for r in range(4):
                    h = 4 * g + r
                    STT("dve", tmp2[p][:T, r * T:(r + 1) * T], bank[bs][:T, r * 128:r * 128 + T], cs_sb[:T, h:h + 1],
                        maskneg[:T, :T], ALU.subtract, ALU.add, [BK(bs), "cs_sb", "maskneg"], [("tmp", p)])

            pe_group(0)
            pe_group(1)
            hb_idx = [c.hb]
            for ci in range(c.nchunk):
                r0 = ci * 64
                for g in range(4):
                    bS = 2 + g // 2
                    MM(bank[bS][:, (g % 2) * 256:(g % 2) * 256 + 256], Btok[r0:r0 + CL, g * 128:(g + 1) * 128],
                       xdd[r0:r0 + CL, g * 256:(g + 1) * 256], True, True, ["Btok", "xdd"], [BK(bS)])
                TT("dve", t1[:, :].rearrange("p (a b) -> p a b", b=64), c.h[:, :].rearrange("p (a b) -> p a b", b=64),
                   dB[ci][:, :].unsqueeze(2).broadcast_to([128, 16, 64]), ALU.mult, [K(c, "h"), ("dB", ci)], ["yf"])
                TT("dve", c.h[:, 0:512], t1[:, 0:512], bank[2][:, :], ALU.add, ["yf", BK(2)], [K(c, "h")])
                TT("dve", c.h[:, 512:1024], t1[:, 512:1024], bank[3][:, :], ALU.add, ["yf", BK(3)], [K(c, "h")])
                nb = (hb_idx[-1] + 1) % c.nhbf
                CP("act", c.hbf[nb][:, :], c.h[:, :], [K(c, "h")], [K(c, "hbf", nb)])
                hb_idx.append(nb)
                if ci == 0:
                    stt_group(0)
            c.hb = hb_idx[-1]
            if c.nchunk == 1:
                pass
            for g in range(4):
                bc, bs = (4, 5) if g % 2 == 0 else (6, 7)
                p = g % 2
                ACT(lm2[p][:T, 0:4 * T], tmp2[p][:T, 0:4 * T], AF.Exp, [("tmp", p)], [("lm", p)])
                ACT(ecs2[p][:, 0:4 * T].rearrange("p (a b) -> p a b", b=T), bk3(bs, 4, T), AF.Exp, [BK(bs)], [("ecs", p)])
                if g + 1 < 4:
                    stt_group(g + 1)
                TT("dve", MT2[p][:T, 0:4 * T].rearrange("p (a b) -> p a b", b=T),
                   lm2[p][:T, 0:4 * T].rearrange("p (a b) -> p a b", b=T),
                   bank[bc][:T, 0:T].unsqueeze(1).broadcast_to([T, 4, T]), ALU.mult, [("lm", p), BK(bc)], [("MT", p)])
                TT("dve", Cs2[p][:, 0:4 * T].rearrange("p (a b) -> p a b", b=T),
                   ecs2[p][:, 0:4 * T].rearrange("p (a b) -> p a b", b=T),
                   c.xbcT[:, 12 + g, tc_].unsqueeze(1).broadcast_to([128, 4, T]), ALU.mult, [("ecs", p)] + xk, [("Cs", p)])
                for r in range(4):
                    h = 4 * g + r
                    j, half = h // 2, h % 2
                    by = 2 if j < 4 else 3
                    o = bank[by][half * 64:half * 64 + 64, (j % 4) * 128:(j % 4) * 128 + T]
                    MM(o, xd[:T, h * 64:(h + 1) * 64], MT2[p][:T, r * T:(r + 1) * T], True, False, ["xd", ("MT", p)], [BK(by)])
                    for ci in range(c.nchunk):
                        r0 = ci * 64
                        hbk = hb_idx[ci]
                        MM(bank[by][half * 64:half * 64 + 64, (j % 4) * 128 + r0:(j % 4) * 128 + r0 + CL],
                           c.hbf[hbk][:, h * 64:(h + 1) * 64], Cs2[p][:, r * T + r0:r * T + r0 + CL], False,
                           ci == c.nchunk - 1, [K(c, "hbf", hbk), ("Cs", p)], [BK(by)])
                if g + 2 < 4:
                    pe_group(g + 2)
            yv = yf[:, 0:8 * T].rearrange("p (a b) -> p a b", b=T)
            TT("dve", yv, c.xbcT[:, 0:8, tc_], dskT.unsqueeze(2).broadcast_to([128, 8, T]), ALU.mult,
               xk + ["small"], ["yf"])
            TT("dve", yv[:, 0:4, :], bk3(2, 4, T), yv[:, 0:4, :], ALU.add, [BK(2), "yf"], ["yf"])
            TT("dve", yv[:, 4:8, :], bk3(3, 4, T), yv[:, 4:8, :], ALU.add, [BK(3), "yf"], ["yf"])
            TT("dve", yv, yv, c.szT[:, :, tc_], ALU.mult, ["yf", K(c, "szT", t)], ["yf"])
            ACT(sq[:, 0:8 * T], yf[:, 0:8 * T], AF.Square, ["yf"], ["sq"])
            for g in range(4):
                for jj in range(2):
                    MM(bank[0][:, g * 128:g * 128 + T], ones_b[:, :], sq[:, (2 * g + jj) * T:(2 * g + jj + 1) * T],
                       jj == 0, jj == 1, ["ones_b", "sq"], [BK(0)])
            rvv = rv[:, 0:4 * T].rearrange("p (a b) -> p a b", b=T)
            ACT(rvv, bk3(0, 4, T), AF.Ln, [BK(0)], ["rv"], scale=1.0 / 256.0, bias=EPS)
            ACT(rv[:, 0:4 * T], rv[:, 0:4 * T], AF.Exp, ["rv"], ["rv"], scale=-0.5)
            for j in range(8):
                STT("dve", c.szT[:, j, tc_], yv[:, j, :], normwT[:, j:j + 1], rvv[:, j // 2, :], ALU.mult, ALU.mult,
                    ["yf", "rv", "small"], [K(c, "szT", t)])

        def mix_stage(c, st):
            softplus_stage(c)
            for t in range(c.NT):
                ssd(c, t)

        def acc_banks(c, ch):
            if c.sample:
                return [4 * ((ch + 1) % 2)]
            return [4 * (ch % 2) + t for t in range(4)]

        def out_proj(ctxs):
            for ch in range(2):
                for kh in range(2):
                    W, wk = wnext()
                    for c in ctxs:
                        T = c.T
                        bks = acc_banks(c, ch)
                        for t in range(c.NT):
                            b = bks[t]
                            for kc in range(8):
                                if kh == 0:
                                    lhs = c.qT[:, kc, tcols(c, t)]
                                    rk = K(c, "qT", t, kc // 4)
                                else:
                                    lhs = c.szT[:, kc, tcols(c, t)]
                                    rk = K(c, "szT", t)
                                MM(bank[b][:T, :], lhs, W[:, kc, :], kh == 0 and kc == 0, kh == 1 and kc == 7,
                                   [wk, rk], [BK(b)])
                            if kh == 1:
                                xs_ = c.xh[:T, t, ch * 512:(ch + 1) * 512]
                                TT("dve", xs_, bank[b][:T, :], xs_, ALU.add, [BK(b), K(c, "xh", t)], [K(c, "xh", t)])

        def ffn(ctxs):
            gb = [0]
            for s in range(11):
                W, wk = wnext()
                for c in ctxs:
                    S = c.S
                    for mm in range(2):
                        m = 2 * s + mm
                        gb[0] = (gb[0] + 1) % 4
                        bG, bU = 2 * gb[0], 2 * gb[0] + 1
                        for kc in range(8):
                            MM(bank[bG][:, 0:S], W[:, kc, mm * 128:(mm + 1) * 128], c.uT[:, kc, 0:S], kc == 0, kc == 7,
                               [wk, K(c, "uT")], [BK(bG)])
                        for kc in range(8):
                            MM(bank[bU][:, 0:S], W[:, kc, 256 + mm * 128:256 + (mm + 1) * 128], c.uT[:, kc, 0:S],
                               kc == 0, kc == 7, [wk, K(c, "uT")], [BK(bU)])
                        sg = sgt[m % 2]
                        ACT(sg[:, 0:S], bank[bG][:, 0:S], AF.Silu, [BK(bG)], [("sgt", m % 2)])
                        TT("dve", c.hmid[:, m, 0:S], sg[:, 0:S], bank[bU][:, 0:S], ALU.mult,
                           [("sgt", m % 2), BK(bU)], [K(c, "hmid", m)] + ([K(c, "bigR")] if c.alias else []))
            for ch in range(2):
                for kp in range(3):
                    W, wk = wnext()
                    nk = 6 if kp == 2 else 8
                    for c in ctxs:
                        T = c.T
                        bks = acc_banks(c, ch)
                        for t in range(c.NT):
                            b = bks[t]
                            for kc in range(nk):
                                m = kp * 8 + kc
                                MM(bank[b][:T, :], c.hmid[:, m, tcols(c, t)], W[:, kc, :], kp == 0 and kc == 0,
                                   kp == 2 and kc == nk - 1, [wk, K(c, "hmid", m)] + bigW(c), [BK(b)])
                            if kp == 2:
                                xs_ = c.xh[:T, t, ch * 512:(ch + 1) * 512]
                                TT("dve", xs_, bank[b][:T, :], xs_, ALU.add, [BK(b), K(c, "xh", t)], [K(c, "xh", t)])

        def final_stage(c, st):
            T = c.T
            stats4(c)
            for t in range(c.NT):
                STT("dve", yo[:T, :], c.xh[:T, t, :], rstd4[:T, t:t + 1], fg[:T, :], ALU.mult, ALU.mult,
                    [K(c, "xh", t), "rstd4", "fg"], ["yo"])
                if c.sample:
                    DMA("sp", ys_d[c.seq], yo[:T, :], ["yo"], [])
                else:
                    r0 = (st * 4 + t) * 128
                    DMA("sp", yp_d[r0:r0 + 128, :], yo[:T, :], ["yo"], [])

        def state_out(c, dst):
            for half in range(2):
                b = 2 + half
                for q in range(4):
                    blk = half * 4 + q
                    TR(bank[b][:, q * 128:(q + 1) * 128], c.h[:, blk * 128:(blk + 1) * 128], ident_f[:, :],
                       [K(c, "h"), "ident_f"], [BK(b)])
                CP("act" if half else "dve", sst[:, half * 4:half * 4 + 4, :],
                   bank[b][:, :].rearrange("p (a b) -> p a b", b=128), [BK(b)], ["yo"])
            DMA("sp", dst.rearrange("(b p) n -> p b n", p=128), sst[:, :, :], ["yo"], [])

        def sample_init(c, seq):
            c.seq = seq
            DMA("sp", c.xh[:16, 0, :], xs_d[seq], [], [K(c, "xh", 0)])
            DMA("pool", xd[:, 0:256], ck_d[seq], [], ["xd"])
            bv = bkbf(0)
            for kc in range(2):
                TR(bv[:, kc * 128:(kc + 1) * 128], xd[:, kc * 128:(kc + 1) * 128], ident_b[:, :], ["xd", "ident_b"], [BK(0)])
            CP("dve", c.kT[:, :, 0:128], bv[:, 0:256].rearrange("p (a b) -> p a b", b=128), [BK(0)], [K(c, "kT")])
            DMA("pool", c.vx[:, 0, :], cv_d[seq], [], [K(c, "vx")])
            DMA("sp", sconv_sb[:, :], sconv_d[seq], [], ["sconv_sb"])
            for ch in range(16):
                TR(bank[1][:, ch * 4:ch * 4 + 3], sconv_sb[0:3, ch * 128:(ch + 1) * 128], ident_f[0:3, 0:3],
                   ["sconv_sb", "ident_f"], [BK(1)])
            CP("dve", c.rawT[:, :, 0:3], bank[1][:, 0:64].rearrange("p (a b) -> p a b", b=4)[:, :, 0:3], [BK(1)],
               [K(c, "rawT", ch) for ch in range(16)])
            DMA("sp", sst[:, :, :], sssm_d[seq].rearrange("(b p) n -> p b n", p=128), [], ["yo"])
            for half in range(2):
                b = 2 + half
                for q in range(4):
                    blk = half * 4 + q
                    TR(bank[b][:, q * 128:(q + 1) * 128], sst[:, blk, :], ident_f[:, :], ["yo", "ident_f"], [BK(b)])
                CP("dve", c.h[:, half * 512:(half + 1) * 512], bank[b][:, :], [BK(b)], [K(c, "h")])
            c.hb = 0
            CP("act", c.hbf[0][:, :], c.h[:, :], [K(c, "h")], [K(c, "hbf", 0)])

        MS("dve", pc.h[:, :], 0.0, [K(pc, "h")])
        pc.hb = 0
        MS("pool", pc.hbf[0][:, :], 0.0, [K(pc, "hbf", 0)])
        MS("pool", pc.chist[:, :, :], 0.0, [K(pc, "chist")])

        for st in range(NPASS):
            ctxs = [pc]
            import os as _os2
            NOS = bool(_os2.environ.get("DEBUG_NOSAMPLE"))
            if st < 2 and not NOS:
                ctxs.append(sc)
                sample_init(sc, st)
            for t in range(4):
                r0 = (st * 4 + t) * 128
                DMA("sp", pc.xh[:, t, :], xp_d[r0:r0 + 128, :], [], [K(pc, "xh", t)])
            LIM = int(_os2.environ.get("STAGE_LIMIT", "99"))
            if LIM >= 1:
                for c in ctxs:
                    norm_stage(c, g1T)
            if LIM >= 2:
                in_proj(ctxs, st)
            if LIM >= 3:
                for c in ctxs:
                    conv_stage(c)
            if LIM >= 4:
                for c in ctxs:
                    mix_stage(c, st)
            if LIM >= 5:
                out_proj(ctxs)
            if LIM >= 6:
                for c in ctxs:
                    norm_stage(c, g2T)
            if LIM >= 7:
                ffn(ctxs)
            if LIM >= 8:
                for c in ctxs:
                    final_stage(c, st)
            if st < 2 and not NOS and LIM >= 9:
                state_out(sc, ssn_d[st])
        if LIM >= 9:
            state_out(pc, sp_d)
        P.emit_all()
    return nc


_CACHE = {}


def kernel(**inputs):
    f = np.float32
    shared = _host_layout(inputs)
    xp = np.asarray(inputs["x_prompt"], f)
    xs = np.asarray(inputs["x_sample"], f)
    ck = np.asarray(inputs["cache_k"], f)[0].reshape(16, 128, 256)
    cv = np.asarray(inputs["cache_v"], f)[0].reshape(16, 128, 256)
    sconv = np.asarray(inputs["state_conv"], f)[0]
    sssm = np.asarray(inputs["state_ssm"], f)[0].reshape(16, 1024, 128)
    in_maps = []
    for i in range(NCORES):
        m = dict(shared)
        m["xp"] = np.ascontiguousarray(xp[i])
        m["xs"] = np.ascontiguousarray(xs[2 * i:2 * i + 2])
        m["ck"] = np.ascontiguousarray(ck[2 * i:2 * i + 2])
        m["cv"] = np.ascontiguousarray(cv[2 * i:2 * i + 2])
        m["sconv"] = np.ascontiguousarray(sconv[2 * i:2 * i + 2])
        m["sssm"] = np.ascontiguousarray(sssm[2 * i:2 * i + 2])
        in_maps.append(m)
    if "nc" not in _CACHE:
        _CACHE["nc"] = build_program()
    nc = _CACHE["nc"]
    res = run_bass_kernel_spmd(nc, in_maps, core_ids=list(range(NCORES)))
    R = res.results
    y_prompt = np.stack([R[i]["yp"] for i in range(NCORES)]).astype(f)
    y_sample = np.concatenate([R[i]["ys"] for i in range(NCORES)]).astype(f)
    k_prompt = np.stack([R[i]["kp"] for i in range(NCORES)]).reshape(1, 8, 128, 4, 64).astype(f)
    v_prompt = np.stack([R[i]["vp"] for i in range(NCORES)]).reshape(1, 8, 128, 4, 64).astype(f)
    conv_prompt = np.stack([R[i]["cp"] for i in range(NCORES)]).reshape(1, 8, 3, 2048).astype(f)
    ssm_prompt = np.stack([R[i]["spo"] for i in range(NCORES)]).reshape(1, 8, 16, 64, 128).astype(f)
    k_sample = np.concatenate([R[i]["ksn"] for i in range(NCORES)]).reshape(1, 16, 16, 4, 64).astype(f)
    v_sample = np.concatenate([R[i]["vsn"] for i in range(NCORES)]).reshape(1, 16, 16, 4, 64).astype(f)
    conv_sample = np.concatenate([R[i]["csn"] for i in range(NCORES)]).reshape(1, 16, 3, 2048).astype(f)
    ssm_sample = np.concatenate([R[i]["ssn"] for i in range(NCORES)]).reshape(1, 16, 16, 64, 128).astype(f)
    return (y_prompt, y_sample, k_prompt, v_prompt, conv_prompt, ssm_prompt,
            k_sample, v_sample, conv_sample, ssm_sample)
```

```python
import contextlib
import numpy as np
import concourse.bass as bass
import concourse.mybir as mybir
from concourse.bass_utils import run_bass_kernel_spmd

F32 = mybir.dt.float32
BF16 = mybir.dt.bfloat16
AF = mybir.ActivationFunctionType
ALU = mybir.AluOpType

ENGS = ("pe", "act", "dve", "pool", "sp")
NCORES = 8
SEQ = 4096
D = 1024
NPASS = 8
SP = 512
EPS = 1e-6
RING = 3


class _Op:
    __slots__ = ("eng", "idx", "emit", "deps", "is_dma", "dma_sem", "dma_val", "signal", "sigval")

    def __init__(self, eng, idx, emit, is_dma):
        self.eng = eng
        self.idx = idx
        self.emit = emit
        self.deps = []
        self.is_dma = is_dma
        self.dma_sem = None
        self.dma_val = 0
        self.signal = False
        self.sigval = 0


class Prog:
    NDMA_SEMS = 8

    def __init__(self, nc):
        self.nc = nc
        self.ops = {e: [] for e in ENGS}
        self.last_w = {}
        self.readers = {}
        self.dma_rr = {e: 0 for e in ENGS}
        self.dma_last = {}
        self.dma_cnt = {}

    def add(self, eng, emit, reads=(), writes=(), dma=False):
        lst = self.ops[eng]
        op = _Op(eng, len(lst), emit, dma)
        lst.append(op)
        deps = []
        bkeys = [k for k in list(reads) + list(writes) if isinstance(k, tuple) and k and k[0] == "bk"]
        if bkeys:
            reads = [k for k in reads if not (isinstance(k, tuple) and k and k[0] == "bk")]
            writes = [k for k in writes if not (isinstance(k, tuple) and k and k[0] == "bk")]
            for k in set(bkeys):
                w = self.last_w.get(k)
                if w is not None and w.eng != eng:
                    deps.append(w)
                self.last_w[k] = op
        for r in reads:
            w = self.last_w.get(r)
            if w is not None:
                deps.append(w)
        for w_ in writes:
            w = self.last_w.get(w_)
            if w is not None:
                deps.append(w)
            deps.extend(self.readers.get(w_, ()))
        if dma:
            j = self.dma_rr[eng] % self.NDMA_SEMS
            self.dma_rr[eng] += 1
            prev = self.dma_last.get((eng, j))
            if prev is not None:
                deps.append(prev)
            self.dma_last[(eng, j)] = op
            c = self.dma_cnt.get((eng, j), 0) + 1
            self.dma_cnt[(eng, j)] = c
            op.dma_sem = (eng, j)
            op.dma_val = 16 * c
        seen = set()
        best = {}
        for d in deps:
            if d is op or id(d) in seen:
                continue
            seen.add(id(d))
            if d.is_dma:
                op.deps.append(d)
                continue
            b = best.get(d.eng)
            if b is None or d.idx > b.idx:
                best[d.eng] = d
        for d in best.values():
            if d.eng == eng and eng == "pe":
                continue
            d.signal = True
            op.deps.append(d)
        for r in reads:
            lst2 = self.readers.setdefault(r, [])
            if not dma:
                for k_ in range(len(lst2)):
                    if (not lst2[k_].is_dma) and lst2[k_].eng == eng:
                        lst2[k_] = op
                        break
                else:
                    lst2.append(op)
            else:
                lst2.append(op)
        for w_ in writes:
            self.last_w[w_] = op
            self.readers[w_] = []
        return op

    def emit_all(self):
        nc = self.nc
        with contextlib.ExitStack() as es:
            esem = {e: es.enter_context(nc.semaphore("s_" + e)) for e in ENGS}
            dsem = {}
            for e in ENGS:
                for j in range(min(self.NDMA_SEMS, self.dma_rr[e])):
                    dsem[(e, j)] = es.enter_context(nc.semaphore("d_%s%d" % (e, j)))
            for e in ENGS:
                c = 0
                for op in self.ops[e]:
                    if op.signal and not op.is_dma:
                        c += 1
                        op.sigval = c
            block = es.enter_context(nc.Block())

            def run(eng_name, eng):
                waited = {}
                for op in self.ops[eng_name]:
                    for d in op.deps:
                        if d.is_dma:
                            key = ("d",) + d.dma_sem
                            sem = dsem[d.dma_sem]
                            val = d.dma_val
                        else:
                            key = ("e", d.eng)
                            sem = esem[d.eng]
                            val = d.sigval
                        if waited.get(key, 0) >= val:
                            continue
                        waited[key] = val
                        eng.wait_ge(sem, val)
                    ins = op.emit(eng)
                    if op.is_dma:
                        ins.then_inc(dsem[op.dma_sem], 16)
                    elif op.signal:
                        ins.then_inc(esem[eng_name], 1)
                for (e, j), lastop in self.dma_last.items():
                    if e == eng_name and waited.get(("d", e, j), 0) < lastop.dma_val:
                        eng.wait_ge(dsem[(e, j)], lastop.dma_val)

            @block.sync
            def _(eng):
                run("sp", eng)

            @block.tensor
            def _(eng):
                run("pe", eng)

            @block.scalar
            def _(eng):
                run("act", eng)

            @block.vector
            def _(eng):
                run("dve", eng)

            @block.gpsimd
            def _(eng):
                run("pool", eng)


def _head_pairs():
    pairs = []
    for j in range(8):
        if j < 4:
            pairs.append((j, 4 + j))
        else:
            pairs.append((8 + (j - 4), 12 + (j - 4)))
    return pairs


def _slotify(w, cols):
    sub = w[:, cols]
    K = sub.shape[0]
    return np.ascontiguousarray(sub.reshape(K // 128, 128, len(cols)).transpose(1, 0, 2).reshape(128, -1))


def _host_layout(inp):
    f = np.float32
    pairs = _head_pairs()
    w_in = np.asarray(inp["w_in"][0], f)
    qcols = []
    for (a, b) in pairs:
        qcols += list(range(a * 64, a * 64 + 64)) + list(range(b * 64, b * 64 + 64))
    qcols = np.array(qcols)
    slots = [qcols[0:512], qcols[512:1024], np.arange(1024, 1536), np.arange(1536, 2048), np.arange(2048, 2560)]
    for s in range(4):
        slots.append(np.arange(2560 + s * 512, 2560 + (s + 1) * 512))
    win = np.stack([_slotify(w_in, c) for c in slots])
    wdt = _slotify(w_in, np.arange(4608, 4624))
    w_out = np.asarray(inp["w_out"][0], f)
    rowperm = np.concatenate([qcols, np.arange(1024, 2048)])
    wo = w_out[rowperm]
    wout = np.stack([_slotify(wo[kh * 1024:(kh + 1) * 1024], np.arange(ch * 512, (ch + 1) * 512))
                     for ch in range(2) for kh in range(2)])
    wg = np.asarray(inp["w_gate"][0], f)
    wu = np.asarray(inp["w_up"][0], f)
    wgu = np.stack([np.concatenate([_slotify(wg, np.arange(s * 256, (s + 1) * 256)).reshape(128, 8, 256),
                                    _slotify(wu, np.arange(s * 256, (s + 1) * 256)).reshape(128, 8, 256)],
                                   axis=2).reshape(128, 4096) for s in range(11)])
    wd = np.asarray(inp["w_down"][0], f)
    wdp = np.zeros((3072, 1024), f)
    wdp[:2816] = wd
    wdn = np.stack([_slotify(wdp[kp * 1024:(kp + 1) * 1024], np.arange(ch * 512, (ch + 1) * 512))
                    for ch in range(2) for kp in range(3)])

    def colT(v):
        v = np.asarray(v, f)
        return np.ascontiguousarray(v.reshape(-1, 128).T)

    def rep(v):
        v = np.asarray(v, f)
        return np.ascontiguousarray(np.broadcast_to(v[None, :], (128, v.shape[0])))

    cw = np.asarray(inp["conv_w"][0], f)
    convw = np.ascontiguousarray(cw.reshape(4, 16, 128).transpose(2, 0, 1).reshape(128, 64))
    dsk = np.asarray(inp["d_skip"][0], f)
    sinks = np.asarray(inp["sinks"][0], f)
    dskT = np.zeros((128, 8), f)
    sinkT = np.zeros((128, 8), f)
    for j in range(8):
        dskT[:64, j] = dsk[2 * j]
        dskT[64:, j] = dsk[2 * j + 1]
        sinkT[:64, j] = sinks[pairs[j][0]]
        sinkT[64:, j] = sinks[pairs[j][1]]
    small = np.concatenate([
        colT(inp["ln1_g"][0]), colT(inp["ln2_g"][0]), convw, colT(inp["conv_b"][0]),
        rep(inp["dt_bias"][0]), rep(inp["a_log"][0]), dskT, colT(inp["ssm_norm_w"][0]), sinkT], axis=1)
    shared = {"win": win, "wdt": wdt, "wout": wout, "wgu": wgu, "wdn": wdn,
              "small": np.ascontiguousarray(small), "fg": rep(inp["final_g"])}
    return shared


class Ctx:
    pass


def build_program():
    nc = bass.Bass("TRN2", target_bir_lowering=False)

    def din(name, shape):
        return nc.dram_tensor(name, shape, F32, kind="ExternalInput").ap()

    def dout(name, shape):
        return nc.dram_tensor(name, shape, F32, kind="ExternalOutput").ap()

    xp_d = din("xp", [SEQ, D])
    xs_d = din("xs", [2, 16, D])
    ck_d = din("ck", [2, 128, 256])
    cv_d = din("cv", [2, 128, 256])
    sconv_d = din("sconv", [2, 3, 2048])
    sssm_d = din("sssm", [2, 1024, 128])
    win_d = din("win", [9, 128, 4096])
    wdt_d = din("wdt", [128, 128])
    wout_d = din("wout", [4, 128, 4096])
    wgu_d = din("wgu", [11, 128, 4096])
    wdn_d = din("wdn", [6, 128, 4096])
    small_d = din("small", [128, 152])
    fg_d = din("fg", [128, 1024])
    yp_d = dout("yp", [SEQ, D])
    ys_d = dout("ys", [2, 16, D])
    kp_d = dout("kp", [128, 256])
    vp_d = dout("vp", [128, 256])
    cp_d = dout("cp", [3, 2048])
    sp_d = dout("spo", [1024, 128])
    ksn_d = dout("ksn", [2, 16, 256])
    vsn_d = dout("vsn", [2, 16, 256])
    csn_d = dout("csn", [2, 3, 2048])
    ssn_d = dout("ssn", [2, 1024, 128])

    es = contextlib.ExitStack()
    with es:
        def sb(name, shape, dt=F32):
            return es.enter_context(nc.sbuf_tensor("sb_" + name, shape, dt))

        P = Prog(nc)
        bank = [es.enter_context(nc.psum_tensor("bk%d" % i, [128, 512], F32)) for i in range(8)]

        def BK(i):
            return ("bk", i)

        def bk3(i, n, T, rows=128, inner=128):
            return bank[i][:, 0:n * inner].rearrange("p (a b) -> p a b", b=inner)[0:rows, :, 0:T]

        def bkbf(i):
            return bank[i][:, :].bitcast(BF16)

        def MM(out, lhsT, rhs, start, stop, reads, writes):
            P.add("pe", lambda e: e.matmul(out, lhsT=lhsT, rhs=rhs, start=start, stop=stop), reads, writes)

        def TR(out, in_, ident, reads, writes):
            P.add("pe", lambda e: e.transpose(out=out, in_=in_, identity=ident), reads, writes)

        def ACT(out, in_, func, reads, writes, bias=None, scale=None, accum=None):
            kw = {}
            if bias is not None:
                kw["bias"] = bias
            if scale is not None:
                kw["scale"] = scale
            if accum is not None:
                kw["accum_out"] = accum
            P.add("act", lambda e: e.activation(out=out, in_=in_, func=func, **kw), reads, writes)

        def TT(eng, out, in0, in1, op, reads, writes):
            P.add(eng, lambda e: e.tensor_tensor(out=out, in0=in0, in1=in1, op=op), reads, writes)

        def TS(eng, out, in0, s1, s2, op0, op1, reads, writes):
            if s2 is None:
                P.add(eng, lambda e: e.tensor_scalar(out=out, in0=in0, scalar1=s1, scalar2=None, op0=op0), reads, writes)
            else:
                P.add(eng, lambda e: e.tensor_scalar(out=out, in0=in0, scalar1=s1, scalar2=s2, op0=op0, op1=op1),
                      reads, writes)

        def STT(eng, out, in0, scalar, in1, op0, op1, reads, writes):
            P.add(eng, lambda e: e.scalar_tensor_tensor(out=out, in0=in0, scalar=scalar, in1=in1, op0=op0, op1=op1),
                  reads, writes)

        def CP(eng, out, in_, reads, writes):
            if eng == "act":
                P.add("act", lambda e: e.copy(out=out, in_=in_), reads, writes)
            else:
                P.add(eng, lambda e: e.tensor_copy(out=out, in_=in_), reads, writes)

        def MS(eng, ap, val, writes, reads=()):
            P.add(eng, lambda e: e.memset(ap, val), reads, writes)

        def DMA(eng, out, in_, reads, writes):
            P.add(eng, lambda e: e.dma_start(out=out, in_=in_), reads, writes, dma=True)

        evac_rr = [0]

        def EV(out, in_, reads, writes):
            evac_rr[0] += 1
            CP("act" if evac_rr[0] % 2 else "dve", out, in_, reads, writes)

        small = sb("small", [128, 152])
        fg = sb("fg", [128, 1024])
        ident_f = sb("ident_f", [128, 128])
        ident_b = sb("ident_b", [128, 128], BF16)
        ones_b = sb("ones_b", [128, 128], BF16)
        Uincl = sb("Uincl", [128, 128])
        Blk = sb("Blk", [128, 128])
        Sel0 = sb("Sel0", [128, 128])
        Sel1 = sb("Sel1", [128, 128])
        maskneg = sb("maskneg", [128, 128])
        negh = sb("negh", [128, 4])
        dskD = sb("dskD", [128, 8, 128], BF16)
        mbias = sb("mbias", [128, 2])
        aneg = sb("aneg", [128, 16])
        esink = sb("esink", [128, 8])
        diagW = sb("diagW", [128, 64, 128], BF16)
        wdt = sb("wdt", [128, 8, 16], BF16)
        g1T = small[:, 0:8]
        g2T = small[:, 8:16]
        convb = small[:, 80:96]
        dtb = small[:, 96:112]
        dskT = small[:, 128:136]
        normwT = small[:, 136:144]

        DMA("sp", small[:, :], small_d, [], ["small"])
        DMA("sp", fg[:, :], fg_d, [], ["fg"])
        DMA("pool", wdt[:, :, :], wdt_d.rearrange("p (k c) -> p k c", c=16), [], ["wdt"])
        MS("pool", ident_f[:, :], 1.0, ["ident_f"])
        P.add("pool", lambda e: e.affine_select(out=ident_f[:, :], in_=ident_f[:, :], pattern=[[-1, 128]],
                                                 compare_op=ALU.is_equal, fill=0.0, base=0, channel_multiplier=1),
              ["ident_f"], ["ident_f"])
        CP("dve", ident_b[:, :], ident_f[:, :], ["ident_f"], ["ident_b"])
        MS("dve", ones_b[:, :], 1.0, ["ones_b"])
        MS("pool", Uincl[:, :], 1.0, ["Uincl"])
        P.add("pool", lambda e: e.affine_select(out=Uincl[:, :], in_=Uincl[:, :], pattern=[[1, 128]],
                                                 compare_op=ALU.is_ge, fill=0.0, base=0, channel_multiplier=-1),
              ["Uincl"], ["Uincl"])
        MS("pool", Uincl[0:64, 64:128], 0.0, ["Uincl"], ["Uincl"])
        MS("dve", Blk[:, :], 0.0, ["Blk"])
        MS("dve", Blk[0:64, 0:64], 1.0, ["Blk"], ["Blk"])
        MS("dve", Blk[64:128, 64:128], 1.0, ["Blk"], ["Blk"])
        MS("dve", Sel0[:, :], 0.0, ["Sel0"])
        MS("dve", Sel0[0:64, :], 1.0, ["Sel0"], ["Sel0"])
        MS("dve", Sel1[:, :], 0.0, ["Sel1"])
        MS("dve", Sel1[64:128, :], 1.0, ["Sel1"], ["Sel1"])
        TS("dve", maskneg[:, :], Uincl[:, :], 1.0, 30000.0, ALU.subtract, ALU.mult, ["Uincl"], ["maskneg"])
        MS("pool", negh[:, :], -0.5, ["negh"])
        for j in range(8):
            TS("dve", dskD[:, j, :], ident_f[:, :], small[:, 128 + j:129 + j], None, ALU.mult, None,
               ["ident_f", "small"], ["dskD"])
        MS("dve", mbias[:, :], 0.0, ["mbias"])
        MS("dve", mbias[0:64, 0:1], -30000.0, ["mbias"], ["mbias"])
        MS("dve", mbias[64:128, 1:2], -30000.0, ["mbias"], ["mbias"])
        ACT(aneg[:, :], small[:, 112:128], AF.Exp, ["small"], ["aneg"])
        TS("dve", aneg[:, :], aneg[:, :], -1.0, None, ALU.mult, None, ["aneg"], ["aneg"])
        ACT(esink[:, :], small[:, 144:152], AF.Exp, ["small"], ["esink"])
        for ic in range(64):
            TS("dve" if ic % 2 else "pool", diagW[:, ic, :], ident_f[:, :], small[:, 16 + ic:17 + ic], None, ALU.mult, None,
               ["ident_f", "small"], [("diagW", ic)])

        un2 = [sb("un%d" % i, [128, 1024], BF16) for i in range(2)]
        ss4 = sb("ss4", [128, 4])
        vv4 = sb("vv4", [128, 4])
        rstd4 = sb("rstd4", [128, 4])
        PT = [[sb("PT%d%d" % (a, b), [128, 512], BF16) for b in range(2)] for a in range(2)]
        rd = sb("rd", [128, 512])
        ost = sb("ost", [128, 512])
        yo = sb("yo", [128, 1024])
        sgt = [sb("sgt%d" % i, [128, 512], BF16) for i in range(2)]
        d2 = sb("d2", [128, 16])
        cs_sb = sb("cs_sb", [128, 16])
        dd = sb("dd", [128, 16])
        decay = sb("decay", [128, 16])
        dtd = sb("dtd", [128, 16])
        dB = [sb("dB%d" % i, [128, 16]) for i in range(2)]
        csT_hi = sb("csT_hi", [16, 128], BF16)
        csT_lo = sb("csT_lo", [16, 128], BF16)
        xd = sb("xd", [128, 1024], BF16)
        xdd = sb("xdd", [128, 1024], BF16)
        Btok = sb("Btok", [128, 512], BF16)
        tmp2 = [sb("tmp%d" % i, [128, 512]) for i in range(2)]
        lm2 = [sb("lm%d" % i, [128, 512], BF16) for i in range(2)]
        MT2 = [sb("MT%d" % i, [128, 512], BF16) for i in range(2)]
        ecs2 = [sb("ecs%d" % i, [128, 512], BF16) for i in range(2)]
        Cs2 = [sb("Cs%d" % i, [128, 512], BF16) for i in range(2)]
        yf = sb("yf", [128, 1024])
        t1 = yf
        sq = sb("sq", [128, 1024], BF16)
        rv = sb("rv", [128, 512])

        ring = [sb("ring%d" % i, [128, 8, 512], BF16) for i in range(RING)]

        def mkctx(name, S, NT, T, nchunk, CL, sample):
            c = Ctx()
            c.name, c.S, c.NT, c.T, c.nchunk, c.CL, c.sample = name, S, NT, T, nchunk, CL, sample
            c.HK = 128
            c.xh = sb(name + "xh", [128, NT, 1024])
            c.uT = sb(name + "uT", [128, 8, S], BF16)
            c.qT = sb(name + "qT", [128, 8, S], BF16)
            c.kT = sb(name + "kT", [128, 2, 128 + S], BF16)
            c.vx = sb(name + "vx", [128, NT + 1, 256], BF16)
            c.szT = sb(name + "szT", [128, 8, S], BF16)
            if sample:
                c.rawT = sb(name + "rawT", [128, 16, S + 3], BF16)
                c.xbcT = sb(name + "xbcT", [128, 16, S], BF16)
                c.hmid = sb(name + "hmid", [128, 22, S], BF16)
            else:
                big = sb(name + "big", [128, 16 * (S + 3) + 16 * S], BF16)
                c.rawT = big[:, 0:16 * (S + 3)].rearrange("p (c n) -> p c n", n=S + 3)
                c.xbcT = big[:, 16 * (S + 3):].rearrange("p (c n) -> p c n", n=S)
                c.hmid = big[:, 0:22 * S].rearrange("p (c n) -> p c n", n=S)
            c.alias = not sample
            c.chist = sb(name + "chist", [128, 16, 3], BF16)
            c.h = sb(name + "h", [128, 1024])
            c.d1 = sb(name + "d1", [128, NT, 16])
            c.dtt = sb(name + "dtt", [128, NT, 16])
            c.da = sb(name + "da", [128, NT, 16])
            c.nhbf = 2 if sample else 3
            c.hbf = [sb(name + "hbf%d" % i, [128, 1024], BF16) for i in range(c.nhbf)]
            c.hb = 0
            return c

        pc = mkctx("p", SP, 4, 128, 2, 64, False)
        sc = mkctx("s", 16, 1, 16, 1, 16, True)
        sconv_sb = sb("sconv_sb", [3, 2048])
        sst = yo[:, :].rearrange("p (a b) -> p a b", b=128)

        def K(c, *a):
            return (c.name,) + a

        def bigR(c):
            return [K(c, "bigR")] if c.alias else []

        def bigW(c):
            return [K(c, "bigW")] if c.alias else []

        wseq = []
        for st in range(NPASS):
            for s in range(9):
                wseq.append((win_d[s], 8))
            for s in range(4):
                wseq.append((wout_d[s], 8))
            for s in range(11):
                wseq.append((wgu_d[s], 8))
            for s in range(6):
                wseq.append((wdn_d[s], 6 if s % 3 == 2 else 8))
        wstate = {"issued": 0, "cur": -1}

        def wprefetch(upto):
            while wstate["issued"] <= min(upto, len(wseq) - 1):
                n = wstate["issued"]
                src, nk = wseq[n]
                r = n % RING
                DMA("pool", ring[r][:, 0:nk, :], src[:, 0:nk * 512].rearrange("p (k c) -> p k c", c=512),
                    [], [("ring", r)])
                wstate["issued"] += 1

        def wnext():
            wstate["cur"] += 1
            n = wstate["cur"]
            wprefetch(n + RING - 1)
            return ring[n % RING], ("ring", n % RING)

        def tcols(c, t):
            return slice(t * c.T, (t + 1) * c.T)

        def stats4(c):
            T = c.T
            sk = [("ss4", t) for t in range(c.NT)]
            MS("dve", ss4[:T, 0:c.NT], 0.0, sk)
            for t in range(c.NT):
                ACT(un2[t % 2][:T, :], c.xh[:T, t, :], AF.Square, [K(c, "xh", t)], [("un", t % 2), ("ss4", t)],
                    accum=ss4[:T, t:t + 1])
            TS("dve", vv4[:T, 0:c.NT], ss4[:T, 0:c.NT], 1.0 / D, EPS, ALU.mult, ALU.add, sk, ["vv4"])
            TT("pool", rstd4[:T, 0:c.NT], vv4[:T, 0:c.NT], negh[:T, 0:c.NT], ALU.pow, ["vv4", "negh"], ["rstd4"])

        def norm_stage(c, gT):
            T = c.T
            stats4(c)

            def scale(t):
                TS("dve", un2[t % 2][:T, :], c.xh[:T, t, :], rstd4[:T, t:t + 1], None, ALU.mult, None,
                   [K(c, "xh", t), "rstd4"], [("un", t % 2)])
                bv = bkbf(t % 2)
                for kc in range(8):
                    TR(bv[:, kc * 128:kc * 128 + T], un2[t % 2][:T, kc * 128:(kc + 1) * 128], ident_b[:T, :T],
                       [("un", t % 2), "ident_b"], [BK(t % 2)])

            def evac(t):
                bv = bkbf(t % 2)
                TT("dve", c.uT[:, :, tcols(c, t)], bv[:, 0:1024].rearrange("p (a b) -> p a b", b=128)[:, :, 0:T],
                   gT.unsqueeze(2).broadcast_to([128, 8, T]), ALU.mult, [BK(t % 2), "small"], [K(c, "uT")])

            scale(0)
            for t in range(1, c.NT):
                scale(t)
                evac(t - 1)
            evac(c.NT - 1)

        fb = [0]

        fb_mod = [4]

        def fbank():
            fb[0] = (fb[0] + 1) % fb_mod[0]
            return fb[0]

        def feat_chunk(c, W, wk, col0, dst, dkeys_r, dkeys_w, func=None, bias=None):
            b = fbank()
            S = c.S
            for kc in range(8):
                MM(bank[b][:, 0:S], W[:, kc, col0:col0 + 128], c.uT[:, kc, 0:S], kc == 0, kc == 7,
                   [wk, K(c, "uT")], [BK(b)])
            if func is None:
                EV(dst, bank[b][:, 0:S], [BK(b)] + dkeys_r, dkeys_w)
            else:
                ACT(dst, bank[b][:, 0:S], func, [BK(b)] + dkeys_r, dkeys_w, bias=bias)

        def run_streams(items):
            items = list(items)
            while items:
                for item in list(items):
                    g_, rep = item
                    for _ in range(rep):
                        try:
                            next(g_)
                        except StopIteration:
                            items.remove(item)
                            break

        def in_slot(ctxs, st, s):
            W, wk = wnext()
            for c in ctxs:
                S, T = c.S, c.T
                last = c.sample or st == NPASS - 1
                if s < 2:
                    for cc in range(4):
                        j = 4 * s + cc
                        feat_chunk(c, W, wk, cc * 128, c.qT[:, j, 0:S], [],
                                   [K(c, "qT", t, j // 4) for t in range(c.NT)])
                elif s == 2:
                    for cc in range(2):
                        feat_chunk(c, W, wk, cc * 128, c.kT[:, cc, 128:128 + S], [], [K(c, "kT")])
                    for t in range(c.NT):
                        b = 4 + t % 4
                        for kc in range(8):
                            MM(bank[b][:T, 0:512], c.uT[:, kc, tcols(c, t)], W[:, kc, :], kc == 0, kc == 7,
                               [wk, K(c, "uT")], [BK(b)])
                        CP("dve", c.vx[:T, 1 + t, :], bank[b][:T, 256:512], [BK(b)], [K(c, "vx")])
                        if last and t == c.NT - 1:
                            CP("act", ost[:T, :], bank[b][:T, :], [BK(b)], ["ost"])
                            if c.sample:
                                DMA("sp", ksn_d[c.seq], ost[:T, 0:256], ["ost"], [])
                                DMA("sp", vsn_d[c.seq], ost[:T, 256:512], ["ost"], [])
                            else:
                                DMA("sp", kp_d, ost[:T, 0:256], ["ost"], [])
                                DMA("sp", vp_d, ost[:T, 256:512], ["ost"], [])
                    for t in range(c.NT):
                        b = 4 + t % 4
                        for kc in range(8):
                            MM(bank[b][:T, 0:16], c.uT[:, kc, tcols(c, t)], wdt[:, kc, :], kc == 0, kc == 7,
                               ["wdt", K(c, "uT")], [BK(b)])
                        TT("dve", c.d1[:T, t, :], bank[b][:T, 0:16], dtb[:T, :], ALU.add, [BK(b), "small"], [K(c, "d1")])
                elif s < 5:
                    for cc in range(4):
                        j = 4 * (s - 3) + cc
                        feat_chunk(c, W, wk, cc * 128, c.szT[:, j, 0:S], [],
                                   [K(c, "szT", t) for t in range(c.NT)], func=AF.Silu)
                else:
                    for cc in range(4):
                        ch = 4 * (s - 5) + cc
                        feat_chunk(c, W, wk, cc * 128, c.rawT[:, ch, 3:3 + S], bigR(c),
                                   [K(c, "rawT", ch)] + bigW(c))
                        yield
                    if last:
                        t = c.NT - 1
                        b = fbank()
                        for kc in range(8):
                            MM(bank[b][:T, 0:512], c.uT[:, kc, tcols(c, t)], W[:, kc, :], kc == 0, kc == 7,
                               [wk, K(c, "uT")], [BK(b)])
                        CP("act", ost[:T, :], bank[b][:T, :], [BK(b)], ["ost"])
                        dst = csn_d[c.seq] if c.sample else cp_d
                        DMA("sp", dst[:, (s - 5) * 512:(s - 4) * 512], ost[T - 3:T, :], ["ost"], [])

        def in_proj(ctxs, st):
            for s in range(5):
                for _ in in_slot(ctxs, st, s):
                    pass

            def xbc_stream():
                for s in range(5, 9):
                    yield from in_slot(ctxs, st, s)

            def attn_stream(c):
                for t in range(c.NT):
                    has_prev = c.sample or not (st == 0 and t == 0)
                    yield from attention(c, t, has_prev)

            fb_mod[0] = 2
            run_streams([(xbc_stream(), 2)] + [(attn_stream(c), 1) for c in ctxs])
            fb_mod[0] = 4
            for c in ctxs:
                if not c.sample:
                    CP("dve", c.kT[:, :, 0:128], c.kT[:, :, c.S:c.S + 128], [K(c, "kT")], [K(c, "kT")])
                    CP("dve", c.vx[:, 0, :], c.vx[:, c.NT, :], [K(c, "vx")], [K(c, "vx")])

        def conv_stage(c):
            S = c.S
            if not c.sample:
                CP("dve", c.rawT[:, :, 0:3], c.chist[:, :, :], [K(c, "chist")] + bigR(c),
                   [K(c, "rawT", ch) for ch in range(16)] + bigW(c))
            for ch in range(16):
                b = fbank()
                for i in range(4):
                    MM(bank[b][:, 0:S], diagW[:, i * 16 + ch, :], c.rawT[:, ch, i:i + S], i == 0, i == 3,
                       [("diagW", i * 16 + ch), K(c, "rawT", ch)] + bigR(c), [BK(b)])
                ACT(c.xbcT[:, ch, 0:S], bank[b][:, 0:S], AF.Silu, [BK(b), "small"] + bigR(c),
                    [K(c, "xbcT", ch)] + bigW(c), bias=convb[:, ch:ch + 1])
            if not c.sample:
                CP("dve", c.chist[:, :, :], c.rawT[:, :, S:S + 3],
                   [K(c, "rawT", ch) for ch in range(16)] + bigR(c), [K(c, "chist")])

        def softplus_stage(c):
            T = c.T
            for t in range(c.NT):
                ACT(d2[:T, :], c.d1[:T, t, :], AF.Exp, [K(c, "d1")], ["d2"])
                ACT(c.dtt[:T, t, :], d2[:T, :], AF.Ln, ["d2"], [K(c, "dtt")], bias=1.0)
                TT("dve", c.da[:T, t, :], c.dtt[:T, t, :], aneg[:T, :], ALU.mult, [K(c, "dtt"), "aneg"], [K(c, "da")])

        def attention(c, t, has_prev):
            T = c.T
            N = 4 * T
            qc = tcols(c, t)
            prevc = slice(t * 128, t * 128 + 128)
            curc = slice(128 + t * T, 128 + t * T + T)
            bO, bD = 6, 7

            def pview(ap_, lo, hi):
                return ap_[0:128, 0:4 * T].rearrange("p (a b) -> p a b", b=T)[:, :, lo:hi]

            for kc in range(2):
                j0 = 4 * kc
                qk = K(c, "qT", t, kc)
                gs = (2 * kc, 2 * kc + 1)
                for g in gs:
                    hh = g % 2
                    ps = slice(hh * 64, hh * 64 + 64)
                    bA, bB = (4, 5) if hh == 0 else (2, 3)
                    rq = c.qT[ps, j0:j0 + 4, qc]
                    if has_prev:
                        MM(bank[bA][:, 0:N], c.kT[ps, kc, prevc], rq, True, True, [K(c, "kT"), qk], [BK(bA)])
                    MM(bank[bB][:T, 0:N], c.kT[ps, kc, curc], rq, True, True, [K(c, "kT"), qk], [BK(bB)])
                for g in gs:
                    hh = g % 2
                    bA, bB = (4, 5) if hh == 0 else (2, 3)
                    pa, pb = PT[hh]
                    if has_prev:
                        if c.sample:
                            ACT(pa[:, 0:N], bank[bA][:, 0:N], AF.Exp, [BK(bA)], [("PT", hh, 0, "a"), ("PT", hh, 0, "b")],
                                scale=0.125)
                        else:
                            ACT(pview(pa, 0, 64), pview(bank[bA], 0, 64), AF.Exp, [BK(bA)], [("PT", hh, 0, "a")], scale=0.125)
                            ACT(pview(pa, 64, 128), pview(bank[bA], 64, 128), AF.Exp, [BK(bA), "mbias"],
                                [("PT", hh, 0, "b")], scale=0.125, bias=mbias[:, 0:1])
                    if c.sample:
                        ACT(pb[:T, 0:N], bank[bB][:T, 0:N], AF.Exp, [BK(bB)], [("PT", hh, 1, "a"), ("PT", hh, 1, "b")],
                            scale=0.125)
                    else:
                        ACT(pview(pb, 0, 64), pview(bank[bB], 0, 64), AF.Exp, [BK(bB), "mbias"], [("PT", hh, 1, "a")],
                            scale=0.125, bias=mbias[:, 1:2])
                        ACT(pview(pb, 64, 128), pview(bank[bB], 64, 128), AF.Exp, [BK(bB)], [("PT", hh, 1, "b")], scale=0.125)
                for g in gs:
                    hh = g % 2
                    ps = slice(hh * 64, hh * 64 + 64)
                    pa, pb = PT[hh]
                    vs = slice(g * 64, g * 64 + 64)
                    ka = [("PT", hh, 0, "a"), ("PT", hh, 0, "b")]
                    kb = [("PT", hh, 1, "a"), ("PT", hh, 1, "b")]
                    if has_prev:
                        MM(bank[bO][ps, 0:N], c.vx[:, t, vs], pa[:, 0:N], True, False, [K(c, "vx")] + ka, [BK(bO)])
                    MM(bank[bO][ps, 0:N], c.vx[:T, t + 1, vs], pb[:T, 0:N], not has_prev, True, [K(c, "vx")] + kb, [BK(bO)])
                    if has_prev:
                        MM(bank[bD][ps, 0:N], ones_b[:, 0:64], pa[:, 0:N], True, False, ["ones_b"] + ka, [BK(bD)])
                    MM(bank[bD][ps, 0:N], ones_b[:T, 0:64], pb[:T, 0:N], not has_prev, True, ["ones_b"] + kb, [BK(bD)])
                rdv = rd[:, 0:N].rearrange("p (a b) -> p a b", b=T)
                TT("dve", rdv, bank[bD][:, 0:N].rearrange("p (a b) -> p a b", b=T),
                   esink[:, j0:j0 + 4].unsqueeze(2).broadcast_to([128, 4, T]), ALU.add, [BK(bD), "esink"], ["rd"])
                ACT(rd[:, 0:N], rd[:, 0:N], AF.Ln, ["rd"], ["rd"])
                ACT(rd[:, 0:N], rd[:, 0:N], AF.Exp, ["rd"], ["rd"], scale=-1.0)
                TT("dve", c.qT[:, j0:j0 + 4, qc], bank[bO][:, 0:N].rearrange("p (a b) -> p a b", b=T), rdv,
                   ALU.mult, [BK(bO), "rd"], [qk])
                yield

        def ssd(c, t):
            T, CL = c.T, c.CL
            tc_ = tcols(c, t)
            xk = [K(c, "xbcT", ch) for ch in range(16)]
            dav = c.da[:T, t, :]
            MM(bank[0][:T, 0:16], Uincl[:T, :T], dav, True, True, ["Uincl", K(c, "da")], [BK(0)])
            MM(bank[0][:T, 16:32], Blk[:T, :T], dav, True, True, ["Blk", K(c, "da")], [BK(0)])
            MM(bank[0][:, 32:48], Sel0[:T, :], dav, True, True, ["Sel0", K(c, "da")], [BK(0)])
            if c.nchunk == 2:
                MM(bank[0][:, 48:64], Sel1[:T, :], dav, True, True, ["Sel1", K(c, "da")], [BK(0)])
            MM(bank[0][0:16, 64:64 + T], dav, Uincl[:T, :T], True, True, ["Uincl", K(c, "da")], [BK(0)])
            b1v = bkbf(1)
            b2v = bkbf(2)
            for j in range(8):
                TR(b1v[:T, j * 128:(j + 1) * 128], c.xbcT[:, j, tc_], ident_b[:, :], xk + ["ident_b"], [BK(1)])
            for g in range(4):
                TR(b2v[:T, g * 128:(g + 1) * 128], c.xbcT[:, 8 + g, tc_], ident_b[:, :], xk + ["ident_b"], [BK(2)])
            CP("dve", cs_sb[:T, :], bank[0][:T, 0:16], [BK(0)], ["cs_sb"])
            TT("dve", dd[:T, :], bank[0][:T, 16:32], cs_sb[:T, :], ALU.subtract, [BK(0), "cs_sb"], ["dd"])
            ACT(decay[:T, :], dd[:T, :], AF.Exp, ["dd"], ["decay"])
            ACT(dB[0][:, :], bank[0][:, 32:48], AF.Exp, [BK(0)], [("dB", 0)])
            if c.nchunk == 2:
                ACT(dB[1][:, :], bank[0][:, 48:64], AF.Exp, [BK(0)], [("dB", 1)])
            CP("act", csT_hi[:, :T], bank[0][0:16, 64:64 + T], [BK(0)], ["csT_hi"])
            TT("dve", csT_lo[:, :T], bank[0][0:16, 64:64 + T], csT_hi[:, :T], ALU.subtract, [BK(0), "csT_hi"], ["csT_lo"])
            TT("dve", dtd[:T, :], c.dtt[:T, t, :], decay[:T, :], ALU.mult, [K(c, "dtt"), "decay"], ["dtd"])
            xv = b1v[:T, 0:1024].rearrange("p (a b) -> p a b", b=64)
            TT("dve", xdd[:T, :].rearrange("p (a b) -> p a b", b=64), xv,
               dtd[:T, :].unsqueeze(2).broadcast_to([T, 16, 64]), ALU.mult, [BK(1), "dtd"], ["xdd"])
            CP("act", Btok[:T, :], b2v[:T, 0:512], [BK(2)], ["Btok"])
            TT("dve", xd[:T, :].rearrange("p (a b) -> p a b", b=64), xv,
               c.dtt[:T, t, :].unsqueeze(2).broadcast_to([T, 16, 64]), ALU.mult, [BK(1), K(c, "dtt")], ["xd"])

            def pe_group(g):
                bc, bs = (4, 5) if g % 2 == 0 else (6, 7)
                MM(bank[bc][:T, 0:T], c.xbcT[:, 8 + g, tc_], c.xbcT[:, 12 + g, tc_], True, True, xk, [BK(bc)])
                for r in range(4):
                    h = 4 * g + r
                    MM(bank[bs][:, r * 128:r * 128 + T], ident_b[0:16, h:h + 1].broadcast_to([16, 128]),
                       csT_hi[0:16, :T], True, False, ["ident_b", "csT_hi"], [BK(bs)])
                    MM(bank[bs][:, r * 128:r * 128 + T], ident_b[0:16, h:h + 1].broadcast_to([16, 128]),
                       csT_lo[0:16, :T], False, True, ["ident_b", "csT_lo"], [BK(bs)])

            def stt_group(g):
                bc, bs = (4, 5) if g % 2 == 0 else (6, 7)
                p = g % 2
                for r in range(4):
                    h = 4 * g + r
                    STT("dve", tmp2[p][:T, r * T:(r + 1) * T], bank[bs][:T, r * 128:r * 128 + T], cs_sb[:T, h:h + 1],
                        maskneg[:T, :T], ALU.subtract, ALU.add, [BK(bs), "cs_sb", "maskneg"], [("tmp", p)])

            pe_group(0)
            pe_group(1)
            hb_idx = [c.hb]
            for ci in range(c.nchunk):
                r0 = ci * 64
                for g in range(4):
                    bS = 2 + g // 2
                    MM(bank[bS][:, (g % 2) * 256:(g % 2) * 256 + 256], Btok[r0:r0 + CL, g * 128:(g + 1) * 128],
                       xdd[r0:r0 + CL, g * 256:(g + 1) * 256], True, True, ["Btok", "xdd"], [BK(bS)])
                TT("dve", t1[:, :].rearrange("p (a b) -> p a b", b=64), c.h[:, :].rearrange("p (a b) -> p a b", b=64),
                   dB[ci][:, :].unsqueeze(2).broadcast_to([128, 16, 64]), ALU.mult, [K(c, "h"), ("dB", ci)], ["yf"])
                TT("dve", c.h[:, 0:512], t1[:, 0:512], bank[2][:, :], ALU.add, ["yf", BK(2)], [K(c, "h")])
                TT("dve", c.h[:, 512:1024], t1[:, 512:1024], bank[3][:, :], ALU.add, ["yf", BK(3)], [K(c, "h")])
                nb = (hb_idx[-1] + 1) % c.nhbf
                CP("act", c.hbf[nb][:, :], c.h[:, :], [K(c, "h")], [K(c, "hbf", nb)])
                hb_idx.append(nb)
                if ci == 0:
                    stt_group(0)
            c.hb = hb_idx[-1]
            if c.nchunk == 1:
                pass
            for g in range(4):
                bc, bs = (4, 5) if g % 2 == 0 else (6, 7)
                p = g % 2
                ACT(lm2[p][:T, 0:4 * T], tmp2[p][:T, 0:4 * T], AF.Exp, [("tmp", p)], [("lm", p)])
                ACT(ecs2[p][:, 0:4 * T].rearrange("p (a b) -> p a b", b=T), bk3(bs, 4, T), AF.Exp, [BK(bs)], [("ecs", p)])
                if g + 1 < 4:
                    stt_group(g + 1)
                TT("dve", MT2[p][:T, 0:4 * T].rearrange("p (a b) -> p a b", b=T),
                   lm2[p][:T, 0:4 * T].rearrange("p (a b) -> p a b", b=T),
                   bank[bc][:T, 0:T].unsqueeze(1).broadcast_to([T, 4, T]), ALU.mult, [("lm", p), BK(bc)], [("MT", p)])
                TT("dve", Cs2[p][:, 0:4 * T].rearrange("p (a b) -> p a b", b=T),
                   ecs2[p][:, 0:4 * T].rearrange("p (a b) -> p a b", b=T),
                   c.xbcT[:, 12 + g, tc_].unsqueeze(1).broadcast_to([128, 4, T]), ALU.mult, [("ecs", p)] + xk, [("Cs", p)])
                for j in (2 * g, 2 * g + 1):
                    by = 2 if j < 4 else 3
                    MM(bank[by][:, (j % 4) * 128:(j % 4) * 128 + T], dskD[:, j, :], c.xbcT[:, j, tc_], True, False,
                       ["dskD"] + xk, [BK(by)])
                    for half in range(2):
                        h = 2 * j + half
                        r = h - 4 * g
                        o = bank[by][half * 64:half * 64 + 64, (j % 4) * 128:(j % 4) * 128 + T]
                        MM(o, xd[:T, h * 64:(h + 1) * 64], MT2[p][:T, r * T:(r + 1) * T], False, False,
                           ["xd", ("MT", p)], [BK(by)])
                        for ci in range(c.nchunk):
                            r0 = ci * 64
                            hbk = hb_idx[ci]
                            MM(bank[by][half * 64:half * 64 + 64, (j % 4) * 128 + r0:(j % 4) * 128 + r0 + CL],
                               c.hbf[hbk][:, h * 64:(h + 1) * 64], Cs2[p][:, r * T + r0:r * T + r0 + CL], False,
                               ci == c.nchunk - 1, [K(c, "hbf", hbk), ("Cs", p)], [BK(by)])
                if g + 2 < 4:
                    pe_group(g + 2)
            yv = yf[:, 0:8 * T].rearrange("p (a b) -> p a b", b=T)
            TT("dve", yv[:, 0:4, :], bk3(2, 4, T), c.szT[:, 0:4, tc_], ALU.mult, [BK(2), K(c, "szT", t)], ["yf"])
            TT("dve", yv[:, 4:8, :], bk3(3, 4, T), c.szT[:, 4:8, tc_], ALU.mult, [BK(3), K(c, "szT", t)], ["yf"])
            ACT(sq[:, 0:8 * T], yf[:, 0:8 * T], AF.Square, ["yf"], ["sq"])
            for g in range(4):
                for jj in range(2):
                    MM(bank[0][:, g * 128:g * 128 + T], ones_b[:, :], sq[:, (2 * g + jj) * T:(2 * g + jj + 1) * T],
                       jj == 0, jj == 1, ["ones_b", "sq"], [BK(0)])
            rvv = rv[:, 0:4 * T].rearrange("p (a b) -> p a b", b=T)
            ACT(rvv, bk3(0, 4, T), AF.Ln, [BK(0)], ["rv"], scale=1.0 / 256.0, bias=EPS)
            ACT(rv[:, 0:4 * T], rv[:, 0:4 * T], AF.Exp, ["rv"], ["rv"], scale=-0.5)
            for j in range(8):
                STT("dve", c.szT[:, j, tc_], yv[:, j, :], normwT[:, j:j + 1], rvv[:, j // 2, :], ALU.mult, ALU.mult,
                    ["yf", "rv", "small"], [K(c, "szT", t)])

        def mix_stage(c, st):
            softplus_stage(c)
            for t in range(c.NT):
                ssd(c, t)

        def acc_banks(c, ch):
            if c.sample:
                return [4 * ((ch + 1) % 2)]
            return [4 * (ch % 2) + t for t in range(4)]

        def out_proj(ctxs):
            for ch in range(2):
                for kh in range(2):
                    W, wk = wnext()
                    for c in ctxs:
                        T = c.T
                        bks = acc_banks(c, ch)
                        for t in range(c.NT):
                            b = bks[t]
                            for kc in range(8):
                                if kh == 0:
                                    lhs = c.qT[:, kc, tcols(c, t)]
                                    rk = K(c, "qT", t, kc // 4)
                                else:
                                    lhs = c.szT[:, kc, tcols(c, t)]
                                    rk = K(c, "szT", t)
                                MM(bank[b][:T, :], lhs, W[:, kc, :], kh == 0 and kc == 0, kh == 1 and kc == 7,
                                   [wk, rk], [BK(b)])
                            if kh == 1:
                                xs_ = c.xh[:T, t, ch * 512:(ch + 1) * 512]
                                TT("dve", xs_, bank[b][:T, :], xs_, ALU.add, [BK(b), K(c, "xh", t)], [K(c, "xh", t)])

        def ffn(ctxs):
            gb = [0]
            for s in range(11):
                W, wk = wnext()
                for c in ctxs:
                    S = c.S
                    for mm in range(2):
                        m = 2 * s + mm
                        gb[0] = (gb[0] + 1) % 4
                        bG, bU = 2 * gb[0], 2 * gb[0] + 1
                        for kc in range(8):
                            MM(bank[bG][:, 0:S], W[:, kc, mm * 128:(mm + 1) * 128], c.uT[:, kc, 0:S], kc == 0, kc == 7,
                               [wk, K(c, "uT")], [BK(bG)])
                        for kc in range(8):
                            MM(bank[bU][:, 0:S], W[:, kc, 256 + mm * 128:256 + (mm + 1) * 128], c.uT[:, kc, 0:S],
                               kc == 0, kc == 7, [wk, K(c, "uT")], [BK(bU)])
                        sg = sgt[m % 2]
                        ACT(sg[:, 0:S], bank[bG][:, 0:S], AF.Silu, [BK(bG)], [("sgt", m % 2)])
                        TT("dve", c.hmid[:, m, 0:S], sg[:, 0:S], bank[bU][:, 0:S], ALU.mult,
                           [("sgt", m % 2), BK(bU)], [K(c, "hmid", m)] + ([K(c, "bigR")] if c.alias else []))
            for ch in range(2):
                for kp in range(3):
                    W, wk = wnext()
                    nk = 6 if kp == 2 else 8
                    for c in ctxs:
                        T = c.T
                        bks = acc_banks(c, ch)
                        for t in range(c.NT):
                            b = bks[t]
                            for kc in range(nk):
                                m = kp * 8 + kc
                                MM(bank[b][:T, :], c.hmid[:, m, tcols(c, t)], W[:, kc, :], kp == 0 and kc == 0,
                                   kp == 2 and kc == nk - 1, [wk, K(c, "hmid", m)] + bigW(c), [BK(b)])
                            if kp == 2:
                                xs_ = c.xh[:T, t, ch * 512:(ch + 1) * 512]
                                TT("dve", xs_, bank[b][:T, :], xs_, ALU.add, [BK(b), K(c, "xh", t)], [K(c, "xh", t)])

        def final_stage(c, st):
            T = c.T
            stats4(c)
            for t in range(c.NT):
                STT("dve", yo[:T, :], c.xh[:T, t, :], rstd4[:T, t:t + 1], fg[:T, :], ALU.mult, ALU.mult,
                    [K(c, "xh", t), "rstd4", "fg"], ["yo"])
                if c.sample:
                    DMA("sp", ys_d[c.seq], yo[:T, :], ["yo"], [])
                else:
                    r0 = (st * 4 + t) * 128
                    DMA("sp", yp_d[r0:r0 + 128, :], yo[:T, :], ["yo"], [])

        def state_out(c, dst):
            for half in range(2):
                b = 2 + half
                for q in range(4):
                    blk = half * 4 + q
                    TR(bank[b][:, q * 128:(q + 1) * 128], c.h[:, blk * 128:(blk + 1) * 128], ident_f[:, :],
                       [K(c, "h"), "ident_f"], [BK(b)])
                CP("act" if half else "dve", sst[:, half * 4:half * 4 + 4, :],
                   bank[b][:, :].rearrange("p (a b) -> p a b", b=128), [BK(b)], ["yo"])
            DMA("sp", dst.rearrange("(b p) n -> p b n", p=128), sst[:, :, :], ["yo"], [])

        def sample_init(c, seq):
            c.seq = seq
            DMA("sp", c.xh[:16, 0, :], xs_d[seq], [], [K(c, "xh", 0)])
            DMA("pool", xd[:, 0:256], ck_d[seq], [], ["xd"])
            bv = bkbf(0)
            for kc in range(2):
                TR(bv[:, kc * 128:(kc + 1) * 128], xd[:, kc * 128:(kc + 1) * 128], ident_b[:, :], ["xd", "ident_b"], [BK(0)])
            CP("dve", c.kT[:, :, 0:128], bv[:, 0:256].rearrange("p (a b) -> p a b", b=128), [BK(0)], [K(c, "kT")])
            DMA("pool", c.vx[:, 0, :], cv_d[seq], [], [K(c, "vx")])
            DMA("sp", sconv_sb[:, :], sconv_d[seq], [], ["sconv_sb"])
            for ch in range(16):
                TR(bank[1][:, ch * 4:ch * 4 + 3], sconv_sb[0:3, ch * 128:(ch + 1) * 128], ident_f[0:3, 0:3],
                   ["sconv_sb", "ident_f"], [BK(1)])
            CP("dve", c.rawT[:, :, 0:3], bank[1][:, 0:64].rearrange("p (a b) -> p a b", b=4)[:, :, 0:3], [BK(1)],
               [K(c, "rawT", ch) for ch in range(16)])
            DMA("sp", sst[:, :, :], sssm_d[seq].rearrange("(b p) n -> p b n", p=128), [], ["yo"])
            for half in range(2):
                b = 2 + half
                for q in range(4):
                    blk = half * 4 + q
                    TR(bank[b][:, q * 128:(q + 1) * 128], sst[:, blk, :], ident_f[:, :], ["yo", "ident_f"], [BK(b)])
                CP("dve", c.h[:, half * 512:(half + 1) * 512], bank[b][:, :], [BK(b)], [K(c, "h")])
            c.hb = 0
            CP("act", c.hbf[0][:, :], c.h[:, :], [K(c, "h")], [K(c, "hbf", 0)])

        MS("dve", pc.h[:, :], 0.0, [K(pc, "h")])
        pc.hb = 0
        MS("pool", pc.hbf[0][:, :], 0.0, [K(pc, "hbf", 0)])
        MS("pool", pc.chist[:, :, :], 0.0, [K(pc, "chist")])

        for st in range(NPASS):
            ctxs = [pc]
            import os as _os2
            NOS = bool(_os2.environ.get("DEBUG_NOSAMPLE"))
            if st < 2 and not NOS:
                ctxs.append(sc)
                sample_init(sc, st)
            for t in range(4):
                r0 = (st * 4 + t) * 128
                DMA("sp", pc.xh[:, t, :], xp_d[r0:r0 + 128, :], [], [K(pc, "xh", t)])
            LIM = int(_os2.environ.get("STAGE_LIMIT", "99"))
            if LIM >= 1:
                for c in ctxs:
                    norm_stage(c, g1T)
            if LIM >= 2:
                in_proj(ctxs, st)
            if LIM >= 3:
                for c in ctxs:
                    conv_stage(c)
            if LIM >= 4:
                for c in ctxs:
                    mix_stage(c, st)
            if LIM >= 5:
                out_proj(ctxs)
            if LIM >= 6:
                for c in ctxs:
                    norm_stage(c, g2T)
            if LIM >= 7:
                ffn(ctxs)
            if LIM >= 8:
                for c in ctxs:
                    final_stage(c, st)
            if st < 2 and not NOS and LIM >= 9:
                state_out(sc, ssn_d[st])
        if LIM >= 9:
            state_out(pc, sp_d)
        P.emit_all()
    return nc


_CACHE = {}


def kernel(**inputs):
    f = np.float32
    shared = _host_layout(inputs)
    xp = np.asarray(inputs["x_prompt"], f)
    xs = np.asarray(inputs["x_sample"], f)
    ck = np.asarray(inputs["cache_k"], f)[0].reshape(16, 128, 256)
    cv = np.asarray(inputs["cache_v"], f)[0].reshape(16, 128, 256)
    sconv = np.asarray(inputs["state_conv"], f)[0]
    sssm = np.asarray(inputs["state_ssm"], f)[0].reshape(16, 1024, 128)
    in_maps = []
    for i in range(NCORES):
        m = dict(shared)
        m["xp"] = np.ascontiguousarray(xp[i])
        m["xs"] = np.ascontiguousarray(xs[2 * i:2 * i + 2])
        m["ck"] = np.ascontiguousarray(ck[2 * i:2 * i + 2])
        m["cv"] = np.ascontiguousarray(cv[2 * i:2 * i + 2])
        m["sconv"] = np.ascontiguousarray(sconv[2 * i:2 * i + 2])
        m["sssm"] = np.ascontiguousarray(sssm[2 * i:2 * i + 2])
        in_maps.append(m)
    if "nc" not in _CACHE:
        _CACHE["nc"] = build_program()
    nc = _CACHE["nc"]
    res = run_bass_kernel_spmd(nc, in_maps, core_ids=list(range(NCORES)))
    R = res.results
    y_prompt = np.stack([R[i]["yp"] for i in range(NCORES)]).astype(f)
    y_sample = np.concatenate([R[i]["ys"] for i in range(NCORES)]).astype(f)
    k_prompt = np.stack([R[i]["kp"] for i in range(NCORES)]).reshape(1, 8, 128, 4, 64).astype(f)
    v_prompt = np.stack([R[i]["vp"] for i in range(NCORES)]).reshape(1, 8, 128, 4, 64).astype(f)
    conv_prompt = np.stack([R[i]["cp"] for i in range(NCORES)]).reshape(1, 8, 3, 2048).astype(f)
    ssm_prompt = np.stack([R[i]["spo"] for i in range(NCORES)]).reshape(1, 8, 16, 64, 128).astype(f)
    k_sample = np.concatenate([R[i]["ksn"] for i in range(NCORES)]).reshape(1, 16, 16, 4, 64).astype(f)
    v_sample = np.concatenate([R[i]["vsn"] for i in range(NCORES)]).reshape(1, 16, 16, 4, 64).astype(f)
    conv_sample = np.concatenate([R[i]["csn"] for i in range(NCORES)]).reshape(1, 16, 3, 2048).astype(f)
    ssm_sample = np.concatenate([R[i]["ssn"] for i in range(NCORES)]).reshape(1, 16, 16, 64, 128).astype(f)
    return (y_prompt, y_sample, k_prompt, v_prompt, conv_prompt, ssm_prompt,
            k_sample, v_sample, conv_sample, ssm_sample)
```

```python
import contextlib
import numpy as np
import concourse.bass as bass
import concourse.mybir as mybir
from concourse.bass_utils import run_bass_kernel_spmd

F32 = mybir.dt.float32
BF16 = mybir.dt.bfloat16
AF = mybir.ActivationFunctionType
ALU = mybir.AluOpType

ENGS = ("pe", "act", "dve", "pool", "sp")
NCORES = 8
SEQ = 4096
D = 1024
NPASS = 8
SP = 512
EPS = 1e-6
RING = 3


class _Op:
    __slots__ = ("eng", "idx", "emit", "deps", "is_dma", "dma_sem", "dma_val", "signal", "sigval")

    def __init__(self, eng, idx, emit, is_dma):
        self.eng = eng
        self.idx = idx
        self.emit = emit
        self.deps = []
        self.is_dma = is_dma
        self.dma_sem = None
        self.dma_val = 0
        self.signal = False
        self.sigval = 0


class Prog:
    NDMA_SEMS = 8

    def __init__(self, nc):
        self.nc = nc
        self.ops = {e: [] for e in ENGS}
        self.last_w = {}
        self.readers = {}
        self.dma_rr = {e: 0 for e in ENGS}
        self.dma_last = {}
        self.dma_cnt = {}

    def add(self, eng, emit, reads=(), writes=(), dma=False):
        lst = self.ops[eng]
        op = _Op(eng, len(lst), emit, dma)
        lst.append(op)
        deps = []
        bkeys = [k for k in list(reads) + list(writes) if isinstance(k, tuple) and k and k[0] == "bk"]
        if bkeys:
            reads = [k for k in reads if not (isinstance(k, tuple) and k and k[0] == "bk")]
            writes = [k for k in writes if not (isinstance(k, tuple) and k and k[0] == "bk")]
            for k in set(bkeys):
                w = self.last_w.get(k)
                if w is not None and w.eng != eng:
                    deps.append(w)
                self.last_w[k] = op
        for r in reads:
            w = self.last_w.get(r)
            if w is not None:
                deps.append(w)
        for w_ in writes:
            w = self.last_w.get(w_)
            if w is not None:
                deps.append(w)
            deps.extend(self.readers.get(w_, ()))
        if dma:
            j = self.dma_rr[eng] % self.NDMA_SEMS
            self.dma_rr[eng] += 1
            prev = self.dma_last.get((eng, j))
            if prev is not None:
                deps.append(prev)
            self.dma_last[(eng, j)] = op
            c = self.dma_cnt.get((eng, j), 0) + 1
            self.dma_cnt[(eng, j)] = c
            op.dma_sem = (eng, j)
            op.dma_val = 16 * c
        seen = set()
        best = {}
        for d in deps:
            if d is op or id(d) in seen:
                continue
            seen.add(id(d))
            if d.is_dma:
                op.deps.append(d)
                continue
            b = best.get(d.eng)
            if b is None or d.idx > b.idx:
                best[d.eng] = d
        for d in best.values():
            if d.eng == eng and eng == "pe":
                continue
            d.signal = True
            op.deps.append(d)
        for r in reads:
            lst2 = self.readers.setdefault(r, [])
            if not dma:
                for k_ in range(len(lst2)):
                    if (not lst2[k_].is_dma) and lst2[k_].eng == eng:
                        lst2[k_] = op
                        break
                else:
                    lst2.append(op)
            else:
                lst2.append(op)
        for w_ in writes:
            self.last_w[w_] = op
            self.readers[w_] = []
        return op

    def emit_all(self):
        nc = self.nc
        with contextlib.ExitStack() as es:
            esem = {e: es.enter_context(nc.semaphore("s_" + e)) for e in ENGS}
            dsem = {}
            for e in ENGS:
                for j in range(min(self.NDMA_SEMS, self.dma_rr[e])):
                    dsem[(e, j)] = es.enter_context(nc.semaphore("d_%s%d" % (e, j)))
            for e in ENGS:
                c = 0
                for op in self.ops[e]:
                    if op.signal and not op.is_dma:
                        c += 1
                        op.sigval = c
            block = es.enter_context(nc.Block())

            def run(eng_name, eng):
                waited = {}
                for op in self.ops[eng_name]:
                    for d in op.deps:
                        if d.is_dma:
                            key = ("d",) + d.dma_sem
                            sem = dsem[d.dma_sem]
                            val = d.dma_val
                        else:
                            key = ("e", d.eng)
                            sem = esem[d.eng]
                            val = d.sigval
                        if waited.get(key, 0) >= val:
                            continue
                        waited[key] = val
                        eng.wait_ge(sem, val)
                    ins = op.emit(eng)
                    if op.is_dma:
                        ins.then_inc(dsem[op.dma_sem], 16)
                    elif op.signal:
                        ins.then_inc(esem[eng_name], 1)
                for (e, j), lastop in self.dma_last.items():
                    if e == eng_name and waited.get(("d", e, j), 0) < lastop.dma_val:
                        eng.wait_ge(dsem[(e, j)], lastop.dma_val)

            @block.sync
            def _(eng):
                run("sp", eng)

            @block.tensor
            def _(eng):
                run("pe", eng)

            @block.scalar
            def _(eng):
                run("act", eng)

            @block.vector
            def _(eng):
                run("dve", eng)

            @block.gpsimd
            def _(eng):
                run("pool", eng)


def _head_pairs():
    pairs = []
    for j in range(8):
        if j < 4:
            pairs.append((j, 4 + j))
        else:
            pairs.append((8 + (j - 4), 12 + (j - 4)))
    return pairs


def _slotify(w, cols):
    sub = w[:, cols]
    K = sub.shape[0]
    return np.ascontiguousarray(sub.reshape(K // 128, 128, len(cols)).transpose(1, 0, 2).reshape(128, -1))


def _host_layout(inp):
    f = np.float32
    pairs = _head_pairs()
    w_in = np.asarray(inp["w_in"][0], f)
    qcols = []
    for (a, b) in pairs:
        qcols += list(range(a * 64, a * 64 + 64)) + list(range(b * 64, b * 64 + 64))
    qcols = np.array(qcols)
    slots = [qcols[0:512], qcols[512:1024], np.arange(1024, 1536), np.arange(1536, 2048), np.arange(2048, 2560)]
    for s in range(4):
        slots.append(np.arange(2560 + s * 512, 2560 + (s + 1) * 512))
    win = np.stack([_slotify(w_in, c) for c in slots])
    wdt = _slotify(w_in, np.arange(4608, 4624))
    w_out = np.asarray(inp["w_out"][0], f)
    rowperm = np.concatenate([qcols, np.arange(1024, 2048)])
    wo = w_out[rowperm]
    wout = np.stack([_slotify(wo[kh * 1024:(kh + 1) * 1024], np.arange(ch * 512, (ch + 1) * 512))
                     for ch in range(2) for kh in range(2)])
    wg = np.asarray(inp["w_gate"][0], f)
    wu = np.asarray(inp["w_up"][0], f)
    wgu = np.stack([np.concatenate([_slotify(wg, np.arange(s * 256, (s + 1) * 256)).reshape(128, 8, 256),
                                    _slotify(wu, np.arange(s * 256, (s + 1) * 256)).reshape(128, 8, 256)],
                                   axis=2).reshape(128, 4096) for s in range(11)])
    wd = np.asarray(inp["w_down"][0], f)
    wdp = np.zeros((3072, 1024), f)
    wdp[:2816] = wd
    wdn = np.stack([_slotify(wdp[kp * 1024:(kp + 1) * 1024], np.arange(ch * 512, (ch + 1) * 512))
                    for ch in range(2) for kp in range(3)])

    def colT(v):
        v = np.asarray(v, f)
        return np.ascontiguousarray(v.reshape(-1, 128).T)

    def rep(v):
        v = np.asarray(v, f)
        return np.ascontiguousarray(np.broadcast_to(v[None, :], (128, v.shape[0])))

    cw = np.asarray(inp["conv_w"][0], f)
    convw = np.ascontiguousarray(cw.reshape(4, 16, 128).transpose(2, 0, 1).reshape(128, 64))
    dsk = np.asarray(inp["d_skip"][0], f)
    sinks = np.asarray(inp["sinks"][0], f)
    dskT = np.zeros((128, 8), f)
    sinkT = np.zeros((128, 8), f)
    for j in range(8):
        dskT[:64, j] = dsk[2 * j]
        dskT[64:, j] = dsk[2 * j + 1]
        sinkT[:64, j] = sinks[pairs[j][0]]
        sinkT[64:, j] = sinks[pairs[j][1]]
    small = np.concatenate([
        colT(inp["ln1_g"][0]), colT(inp["ln2_g"][0]), convw, colT(inp["conv_b"][0]),
        rep(inp["dt_bias"][0]), rep(inp["a_log"][0]), dskT, colT(inp["ssm_norm_w"][0]), sinkT], axis=1)
    shared = {"win": win, "wdt": wdt, "wout": wout, "wgu": wgu, "wdn": wdn,
              "small": np.ascontiguousarray(small), "fg": rep(inp["final_g"])}
    return shared


class Ctx:
    pass


def build_program():
    nc = bass.Bass("TRN2", target_bir_lowering=False)

    def din(name, shape):
        return nc.dram_tensor(name, shape, F32, kind="ExternalInput").ap()

    def dout(name, shape):
        return nc.dram_tensor(name, shape, F32, kind="ExternalOutput").ap()

    xp_d = din("xp", [SEQ, D])
    xs_d = din("xs", [2, 16, D])
    ck_d = din("ck", [2, 128, 256])
    cv_d = din("cv", [2, 128, 256])
    sconv_d = din("sconv", [2, 3, 2048])
    sssm_d = din("sssm", [2, 1024, 128])
    win_d = din("win", [9, 128, 4096])
    wdt_d = din("wdt", [128, 128])
    wout_d = din("wout", [4, 128, 4096])
    wgu_d = din("wgu", [11, 128, 4096])
    wdn_d = din("wdn", [6, 128, 4096])
    small_d = din("small", [128, 152])
    fg_d = din("fg", [128, 1024])
    yp_d = dout("yp", [SEQ, D])
    ys_d = dout("ys", [2, 16, D])
    kp_d = dout("kp", [128, 256])
    vp_d = dout("vp", [128, 256])
    cp_d = dout("cp", [3, 2048])
    sp_d = dout("spo", [1024, 128])
    ksn_d = dout("ksn", [2, 16, 256])
    vsn_d = dout("vsn", [2, 16, 256])
    csn_d = dout("csn", [2, 3, 2048])
    ssn_d = dout("ssn", [2, 1024, 128])

    es = contextlib.ExitStack()
    with es:
        def sb(name, shape, dt=F32):
            return es.enter_context(nc.sbuf_tensor("sb_" + name, shape, dt))

        P = Prog(nc)
        bank = [es.enter_context(nc.psum_tensor("bk%d" % i, [128, 512], F32)) for i in range(8)]

        def BK(i):
            return ("bk", i)

        def bk3(i, n, T, rows=128, inner=128):
            return bank[i][:, 0:n * inner].rearrange("p (a b) -> p a b", b=inner)[0:rows, :, 0:T]

        def bkbf(i):
            return bank[i][:, :].bitcast(BF16)

        def MM(out, lhsT, rhs, start, stop, reads, writes):
            P.add("pe", lambda e: e.matmul(out, lhsT=lhsT, rhs=rhs, start=start, stop=stop), reads, writes)

        def TR(out, in_, ident, reads, writes):
            P.add("pe", lambda e: e.transpose(out=out, in_=in_, identity=ident), reads, writes)

        def ACT(out, in_, func, reads, writes, bias=None, scale=None, accum=None):
            kw = {}
            if bias is not None:
                kw["bias"] = bias
            if scale is not None:
                kw["scale"] = scale
            if accum is not None:
                kw["accum_out"] = accum
            P.add("act", lambda e: e.activation(out=out, in_=in_, func=func, **kw), reads, writes)

        def TT(eng, out, in0, in1, op, reads, writes):
            P.add(eng, lambda e: e.tensor_tensor(out=out, in0=in0, in1=in1, op=op), reads, writes)

        def TS(eng, out, in0, s1, s2, op0, op1, reads, writes):
            if s2 is None:
                P.add(eng, lambda e: e.tensor_scalar(out=out, in0=in0, scalar1=s1, scalar2=None, op0=op0), reads, writes)
            else:
                P.add(eng, lambda e: e.tensor_scalar(out=out, in0=in0, scalar1=s1, scalar2=s2, op0=op0, op1=op1),
                      reads, writes)

        def STT(eng, out, in0, scalar, in1, op0, op1, reads, writes):
            P.add(eng, lambda e: e.scalar_tensor_tensor(out=out, in0=in0, scalar=scalar, in1=in1, op0=op0, op1=op1),
                  reads, writes)

        def CP(eng, out, in_, reads, writes):
            if eng == "act":
                P.add("act", lambda e: e.copy(out=out, in_=in_), reads, writes)
            else:
                P.add(eng, lambda e: e.tensor_copy(out=out, in_=in_), reads, writes)

        def MS(eng, ap, val, writes, reads=()):
            P.add(eng, lambda e: e.memset(ap, val), reads, writes)

        def DMA(eng, out, in_, reads, writes):
            P.add(eng, lambda e: e.dma_start(out=out, in_=in_), reads, writes, dma=True)

        evac_rr = [0]

        def EV(out, in_, reads, writes):
            evac_rr[0] += 1
            CP("act" if evac_rr[0] % 2 else "dve", out, in_, reads, writes)

        small = sb("small", [128, 152])
        fg = sb("fg", [128, 1024])
        ident_f = sb("ident_f", [128, 128])
        ident_b = sb("ident_b", [128, 128], BF16)
        ones_b = sb("ones_b", [128, 128], BF16)
        Uincl = sb("Uincl", [128, 128])
        Blk = sb("Blk", [128, 128])
        Sel0 = sb("Sel0", [128, 128])
        Sel1 = sb("Sel1", [128, 128])
        maskneg = sb("maskneg", [128, 128])
        negh = sb("negh", [128, 4])
        dskD = sb("dskD", [128, 8, 128], BF16)
        mbias = sb("mbias", [128, 2])
        aneg = sb("aneg", [128, 16])
        esink = sb("esink", [128, 8])
        diagW = sb("diagW", [128, 64, 128], BF16)
        wdt = sb("wdt", [128, 8, 16], BF16)
        g1T = small[:, 0:8]
        g2T = small[:, 8:16]
        convb = small[:, 80:96]
        dtb = small[:, 96:112]
        dskT = small[:, 128:136]
        normwT = small[:, 136:144]

        DMA("sp", small[:, :], small_d, [], ["small"])
        DMA("sp", fg[:, :], fg_d, [], ["fg"])
        DMA("pool", wdt[:, :, :], wdt_d.rearrange("p (k c) -> p k c", c=16), [], ["wdt"])
        MS("pool", ident_f[:, :], 1.0, ["ident_f"])
        P.add("pool", lambda e: e.affine_select(out=ident_f[:, :], in_=ident_f[:, :], pattern=[[-1, 128]],
                                                 compare_op=ALU.is_equal, fill=0.0, base=0, channel_multiplier=1),
              ["ident_f"], ["ident_f"])
        CP("dve", ident_b[:, :], ident_f[:, :], ["ident_f"], ["ident_b"])
        MS("dve", ones_b[:, :], 1.0, ["ones_b"])
        MS("pool", Uincl[:, :], 1.0, ["Uincl"])
        P.add("pool", lambda e: e.affine_select(out=Uincl[:, :], in_=Uincl[:, :], pattern=[[1, 128]],
                                                 compare_op=ALU.is_ge, fill=0.0, base=0, channel_multiplier=-1),
              ["Uincl"], ["Uincl"])
        MS("pool", Uincl[0:64, 64:128], 0.0, ["Uincl"], ["Uincl"])
        MS("dve", Blk[:, :], 0.0, ["Blk"])
        MS("dve", Blk[0:64, 0:64], 1.0, ["Blk"], ["Blk"])
        MS("dve", Blk[64:128, 64:128], 1.0, ["Blk"], ["Blk"])
        MS("dve", Sel0[:, :], 0.0, ["Sel0"])
        MS("dve", Sel0[0:64, :], 1.0, ["Sel0"], ["Sel0"])
        MS("dve", Sel1[:, :], 0.0, ["Sel1"])
        MS("dve", Sel1[64:128, :], 1.0, ["Sel1"], ["Sel1"])
        TS("dve", maskneg[:, :], Uincl[:, :], 1.0, 30000.0, ALU.subtract, ALU.mult, ["Uincl"], ["maskneg"])
        MS("pool", negh[:, :], -0.5, ["negh"])
        for j in range(8):
            TS("dve", dskD[:, j, :], ident_f[:, :], small[:, 128 + j:129 + j], None, ALU.mult, None,
               ["ident_f", "small"], ["dskD"])
        MS("dve", mbias[:, :], 0.0, ["mbias"])
        MS("dve", mbias[0:64, 0:1], -30000.0, ["mbias"], ["mbias"])
        MS("dve", mbias[64:128, 1:2], -30000.0, ["mbias"], ["mbias"])
        ACT(aneg[:, :], small[:, 112:128], AF.Exp, ["small"], ["aneg"])
        TS("dve", aneg[:, :], aneg[:, :], -1.0, None, ALU.mult, None, ["aneg"], ["aneg"])
        ACT(esink[:, :], small[:, 144:152], AF.Exp, ["small"], ["esink"])
        for ic in range(64):
            TS("dve" if ic % 2 else "pool", diagW[:, ic, :], ident_f[:, :], small[:, 16 + ic:17 + ic], None, ALU.mult, None,
               ["ident_f", "small"], [("diagW", ic)])

        un2 = [sb("un%d" % i, [128, 1024], BF16) for i in range(2)]
        ss4 = sb("ss4", [128, 4])
        vv4 = sb("vv4", [128, 4])
        rstd4 = sb("rstd4", [128, 4])
        PT = [[sb("PT%d%d" % (a, b), [128, 512], BF16) for b in range(2)] for a in range(2)]
        rd = sb("rd", [128, 512])
        ost = sb("ost", [128, 512])
        yo = sb("yo", [128, 1024])
        sgt = [sb("sgt%d" % i, [128, 512], BF16) for i in range(2)]
        d2 = sb("d2", [128, 16])
        cs_sb = sb("cs_sb", [128, 16])
        dd = sb("dd", [128, 16])
        decay = sb("decay", [128, 16])
        dtd = sb("dtd", [128, 16])
        dB = [sb("dB%d" % i, [128, 16]) for i in range(2)]
        csT_hi = sb("csT_hi", [16, 128], BF16)
        csT_lo = sb("csT_lo", [16, 128], BF16)
        xd = sb("xd", [128, 1024], BF16)
        xdd = sb("xdd", [128, 1024], BF16)
        Btok = sb("Btok", [128, 512], BF16)
        tmp2 = [sb("tmp%d" % i, [128, 512]) for i in range(2)]
        lm2 = [sb("lm%d" % i, [128, 512], BF16) for i in range(2)]
        MT2 = [sb("MT%d" % i, [128, 512], BF16) for i in range(2)]
        ecs2 = [sb("ecs%d" % i, [128, 512], BF16) for i in range(2)]
        Cs2 = [sb("Cs%d" % i, [128, 512], BF16) for i in range(2)]
        yf = sb("yf", [128, 1024])
        t1 = yf
        sq = sb("sq", [128, 1024], BF16)
        rv = sb("rv", [128, 512])

        ring = [sb("ring%d" % i, [128, 8, 512], BF16) for i in range(RING)]

        def mkctx(name, S, NT, T, nchunk, CL, sample):
            c = Ctx()
            c.name, c.S, c.NT, c.T, c.nchunk, c.CL, c.sample = name, S, NT, T, nchunk, CL, sample
            c.HK = 128
            c.xh = sb(name + "xh", [128, NT, 1024])
            c.uT = sb(name + "uT", [128, 8, S], BF16)
            c.qT = sb(name + "qT", [128, 8, S], BF16)
            c.kT = sb(name + "kT", [128, 2, 128 + S], BF16)
            c.vx = sb(name + "vx", [128, NT + 1, 256], BF16)
            c.szT = sb(name + "szT", [128, 8, S], BF16)
            if sample:
                c.rawT = sb(name + "rawT", [128, 16, S + 3], BF16)
                c.xbcT = sb(name + "xbcT", [128, 16, S], BF16)
                c.hmid = sb(name + "hmid", [128, 22, S], BF16)
            else:
                big = sb(name + "big", [128, 16 * (S + 3) + 16 * S], BF16)
                c.rawT = big[:, 0:16 * (S + 3)].rearrange("p (c n) -> p c n", n=S + 3)
                c.xbcT = big[:, 16 * (S + 3):].rearrange("p (c n) -> p c n", n=S)
                c.hmid = big[:, 0:22 * S].rearrange("p (c n) -> p c n", n=S)
            c.alias = not sample
            c.chist = sb(name + "chist", [128, 16, 3], BF16)
            c.h = sb(name + "h", [128, 1024])
            c.d1 = sb(name + "d1", [128, NT, 16])
            c.dtt = sb(name + "dtt", [128, NT, 16])
            c.da = sb(name + "da", [128, NT, 16])
            c.nhbf = 2 if sample else 3
            c.hbf = [sb(name + "hbf%d" % i, [128, 1024], BF16) for i in range(c.nhbf)]
            c.hb = 0
            return c

        pc = mkctx("p", SP, 4, 128, 2, 64, False)
        sc = mkctx("s", 16, 1, 16, 1, 16, True)
        sconv_sb = sb("sconv_sb", [3, 2048])
        sst = yo[:, :].rearrange("p (a b) -> p a b", b=128)

        def K(c, *a):
            return (c.name,) + a

        def bigR(c):
            return [K(c, "bigR")] if c.alias else []

        def bigW(c):
            return [K(c, "bigW")] if c.alias else []

        wseq = []
        for st in range(NPASS):
            for s in range(9):
                wseq.append((win_d[s], 8))
            for s in range(4):
                wseq.append((wout_d[s], 8))
            for s in range(11):
                wseq.append((wgu_d[s], 8))
            for s in range(6):
                wseq.append((wdn_d[s], 6 if s % 3 == 2 else 8))
        wstate = {"issued": 0, "cur": -1}

        def wprefetch(upto):
            while wstate["issued"] <= min(upto, len(wseq) - 1):
                n = wstate["issued"]
                src, nk = wseq[n]
                r = n % RING
                DMA("pool", ring[r][:, 0:nk, :], src[:, 0:nk * 512].rearrange("p (k c) -> p k c", c=512),
                    [], [("ring", r)])
                wstate["issued"] += 1

        def wnext():
            wstate["cur"] += 1
            n = wstate["cur"]
            wprefetch(n + RING - 1)
            return ring[n % RING], ("ring", n % RING)

        def tcols(c, t):
            return slice(t * c.T, (t + 1) * c.T)

        def stats4(c):
            T = c.T
            sk = [("ss4", t) for t in range(c.NT)]
            MS("dve", ss4[:T, 0:c.NT], 0.0, sk)
            for t in range(c.NT):
                ACT(un2[t % 2][:T, :], c.xh[:T, t, :], AF.Square, [K(c, "xh", t)], [("un", t % 2), ("ss4", t)],
                    accum=ss4[:T, t:t + 1])
            TS("dve", vv4[:T, 0:c.NT], ss4[:T, 0:c.NT], 1.0 / D, EPS, ALU.mult, ALU.add, sk, ["vv4"])
            TT("pool", rstd4[:T, 0:c.NT], vv4[:T, 0:c.NT], negh[:T, 0:c.NT], ALU.pow, ["vv4", "negh"], ["rstd4"])

        def norm_stage(c, gT):
            T = c.T
            stats4(c)

            def scale(t):
                TS("dve", un2[t % 2][:T, :], c.xh[:T, t, :], rstd4[:T, t:t + 1], None, ALU.mult, None,
                   [K(c, "xh", t), "rstd4"], [("un", t % 2)])
                bv = bkbf(t % 2)
                for kc in range(8):
                    TR(bv[:, kc * 128:kc * 128 + T], un2[t % 2][:T, kc * 128:(kc + 1) * 128], ident_b[:T, :T],
                       [("un", t % 2), "ident_b"], [BK(t % 2)])

            def evac(t):
                bv = bkbf(t % 2)
                TT("dve", c.uT[:, :, tcols(c, t)], bv[:, 0:1024].rearrange("p (a b) -> p a b", b=128)[:, :, 0:T],
                   gT.unsqueeze(2).broadcast_to([128, 8, T]), ALU.mult, [BK(t % 2), "small"], [K(c, "uT")])

            scale(0)
            for t in range(1, c.NT):
                scale(t)
                evac(t - 1)
            evac(c.NT - 1)

        fb = [0]

        fb_mod = [4]

        def fbank():
            fb[0] = (fb[0] + 1) % fb_mod[0]
            return fb[0]

        def feat_chunk(c, W, wk, col0, dst, dkeys_r, dkeys_w, func=None, bias=None):
            b = fbank()
            S = c.S
            for kc in range(8):
                MM(bank[b][:, 0:S], W[:, kc, col0:col0 + 128], c.uT[:, kc, 0:S], kc == 0, kc == 7,
                   [wk, K(c, "uT")], [BK(b)])
            if func is None:
                EV(dst, bank[b][:, 0:S], [BK(b)] + dkeys_r, dkeys_w)
            else:
                ACT(dst, bank[b][:, 0:S], func, [BK(b)] + dkeys_r, dkeys_w, bias=bias)

        def run_streams(items):
            items = list(items)
            while items:
                for item in list(items):
                    g_, rep = item
                    for _ in range(rep):
                        try:
                            next(g_)
                        except StopIteration:
                            items.remove(item)
                            break

        def in_slot(ctxs, st, s):
            W, wk = wnext()
            for c in ctxs:
                S, T = c.S, c.T
                last = c.sample or st == NPASS - 1
                if s < 2:
                    for cc in range(4):
                        j = 4 * s + cc
                        feat_chunk(c, W, wk, cc * 128, c.qT[:, j, 0:S], [],
                                   [K(c, "qT", t, j // 4) for t in range(c.NT)])
                elif s == 2:
                    for cc in range(2):
                        feat_chunk(c, W, wk, cc * 128, c.kT[:, cc, 128:128 + S], [], [K(c, "kT")])
                    for t in range(c.NT):
                        b = 4 + t % 4
                        for kc in range(8):
                            MM(bank[b][:T, 0:512], c.uT[:, kc, tcols(c, t)], W[:, kc, :], kc == 0, kc == 7,
                               [wk, K(c, "uT")], [BK(b)])
                        CP("dve", c.vx[:T, 1 + t, :], bank[b][:T, 256:512], [BK(b)], [K(c, "vx")])
                        if last and t == c.NT - 1:
                            CP("act", ost[:T, :], bank[b][:T, :], [BK(b)], ["ost"])
                            if c.sample:
                                DMA("sp", ksn_d[c.seq], ost[:T, 0:256], ["ost"], [])
                                DMA("sp", vsn_d[c.seq], ost[:T, 256:512], ["ost"], [])
                            else:
                                DMA("sp", kp_d, ost[:T, 0:256], ["ost"], [])
                                DMA("sp", vp_d, ost[:T, 256:512], ["ost"], [])
                    for t in range(c.NT):
                        b = 4 + t % 4
                        for kc in range(8):
                            MM(bank[b][:T, 0:16], c.uT[:, kc, tcols(c, t)], wdt[:, kc, :], kc == 0, kc == 7,
                               ["wdt", K(c, "uT")], [BK(b)])
                        TT("dve", c.d1[:T, t, :], bank[b][:T, 0:16], dtb[:T, :], ALU.add, [BK(b), "small"], [K(c, "d1")])
                elif s < 5:
                    for cc in range(4):
                        j = 4 * (s - 3) + cc
                        feat_chunk(c, W, wk, cc * 128, c.szT[:, j, 0:S], [],
                                   [K(c, "szT", t) for t in range(c.NT)], func=AF.Silu)
                else:
                    for cc in range(4):
                        ch = 4 * (s - 5) + cc
                        feat_chunk(c, W, wk, cc * 128, c.rawT[:, ch, 3:3 + S], bigR(c),
                                   [K(c, "rawT", ch)] + bigW(c))
                        yield
                    if last:
                        t = c.NT - 1
                        b = fbank()
                        for kc in range(8):
                            MM(bank[b][:T, 0:512], c.uT[:, kc, tcols(c, t)], W[:, kc, :], kc == 0, kc == 7,
                               [wk, K(c, "uT")], [BK(b)])
                        CP("act", ost[:T, :], bank[b][:T, :], [BK(b)], ["ost"])
                        dst = csn_d[c.seq] if c.sample else cp_d
                        DMA("sp", dst[:, (s - 5) * 512:(s - 4) * 512], ost[T - 3:T, :], ["ost"], [])

        def in_proj(ctxs, st):
            for s in range(5):
                for _ in in_slot(ctxs, st, s):
                    pass

            def xbc_stream():
                for s in range(5, 9):
                    yield from in_slot(ctxs, st, s)

            def attn_stream(c):
                for t in range(c.NT):
                    has_prev = c.sample or not (st == 0 and t == 0)
                    yield from attention(c, t, has_prev)

            fb_mod[0] = 2
            run_streams([(xbc_stream(), 2)] + [(attn_stream(c), 1) for c in ctxs])
            fb_mod[0] = 4
            for c in ctxs:
                if not c.sample:
                    CP("dve", c.kT[:, :, 0:128], c.kT[:, :, c.S:c.S + 128], [K(c, "kT")], [K(c, "kT")])
                    CP("dve", c.vx[:, 0, :], c.vx[:, c.NT, :], [K(c, "vx")], [K(c, "vx")])

        def conv_stage(c):
            S = c.S
            if not c.sample:
                CP("dve", c.rawT[:, :, 0:3], c.chist[:, :, :], [K(c, "chist")] + bigR(c),
                   [K(c, "rawT", ch) for ch in range(16)] + bigW(c))
            for ch in range(16):
                b = fbank()
                for i in range(4):
                    MM(bank[b][:, 0:S], diagW[:, i * 16 + ch, :], c.rawT[:, ch, i:i + S], i == 0, i == 3,
                       [("diagW", i * 16 + ch), K(c, "rawT", ch)] + bigR(c), [BK(b)])
                ACT(c.xbcT[:, ch, 0:S], bank[b][:, 0:S], AF.Silu, [BK(b), "small"] + bigR(c),
                    [K(c, "xbcT", ch)] + bigW(c), bias=convb[:, ch:ch + 1])
            if not c.sample:
                CP("dve", c.chist[:, :, :], c.rawT[:, :, S:S + 3],
                   [K(c, "rawT", ch) for ch in range(16)] + bigR(c), [K(c, "chist")])

        def softplus_stage(c):
            T = c.T
            for t in range(c.NT):
                ACT(d2[:T, :], c.d1[:T, t, :], AF.Exp, [K(c, "d1")], ["d2"])
                ACT(c.dtt[:T, t, :], d2[:T, :], AF.Ln, ["d2"], [K(c, "dtt")], bias=1.0)
                TT("dve", c.da[:T, t, :], c.dtt[:T, t, :], aneg[:T, :], ALU.mult, [K(c, "dtt"), "aneg"], [K(c, "da")])

        def attention(c, t, has_prev):
            T = c.T
            N = 4 * T
            qc = tcols(c, t)
            prevc = slice(t * 128, t * 128 + 128)
            curc = slice(128 + t * T, 128 + t * T + T)
            bO, bD = 6, 7

            def pview(ap_, lo, hi):
                return ap_[0:128, 0:4 * T].rearrange("p (a b) -> p a b", b=T)[:, :, lo:hi]

            for kc in range(2):
                j0 = 4 * kc
                qk = K(c, "qT", t, kc)
                gs = (2 * kc, 2 * kc + 1)
                for g in gs:
                    hh = g % 2
                    ps = slice(hh * 64, hh * 64 + 64)
                    bA, bB = (4, 5) if hh == 0 else (2, 3)
                    rq = c.qT[ps, j0:j0 + 4, qc]
                    if has_prev:
                        MM(bank[bA][:, 0:N], c.kT[ps, kc, prevc], rq, True, True, [K(c, "kT"), qk], [BK(bA)])
                    MM(bank[bB][:T, 0:N], c.kT[ps, kc, curc], rq, True, True, [K(c, "kT"), qk], [BK(bB)])
                for g in gs:
                    hh = g % 2
                    bA, bB = (4, 5) if hh == 0 else (2, 3)
                    pa, pb = PT[hh]
                    if has_prev:
                        if c.sample:
                            ACT(pa[:, 0:N], bank[bA][:, 0:N], AF.Exp, [BK(bA)], [("PT", hh, 0, "a"), ("PT", hh, 0, "b")],
                                scale=0.125)
                        else:
                            ACT(pview(pa, 0, 64), pview(bank[bA], 0, 64), AF.Exp, [BK(bA)], [("PT", hh, 0, "a")], scale=0.125)
                            ACT(pview(pa, 64, 128), pview(bank[bA], 64, 128), AF.Exp, [BK(bA), "mbias"],
                                [("PT", hh, 0, "b")], scale=0.125, bias=mbias[:, 0:1])
                    if c.sample:
                        ACT(pb[:T, 0:N], bank[bB][:T, 0:N], AF.Exp, [BK(bB)], [("PT", hh, 1, "a"), ("PT", hh, 1, "b")],
                            scale=0.125)
                    else:
                        ACT(pview(pb, 0, 64), pview(bank[bB], 0, 64), AF.Exp, [BK(bB), "mbias"], [("PT", hh, 1, "a")],
                            scale=0.125, bias=mbias[:, 1:2])
                        ACT(pview(pb, 64, 128), pview(bank[bB], 64, 128), AF.Exp, [BK(bB)], [("PT", hh, 1, "b")], scale=0.125)
                for g in gs:
                    hh = g % 2
                    ps = slice(hh * 64, hh * 64 + 64)
                    pa, pb = PT[hh]
                    vs = slice(g * 64, g * 64 + 64)
                    ka = [("PT", hh, 0, "a"), ("PT", hh, 0, "b")]
                    kb = [("PT", hh, 1, "a"), ("PT", hh, 1, "b")]
                    if has_prev:
                        MM(bank[bO][ps, 0:N], c.vx[:, t, vs], pa[:, 0:N], True, False, [K(c, "vx")] + ka, [BK(bO)])
                    MM(bank[bO][ps, 0:N], c.vx[:T, t + 1, vs], pb[:T, 0:N], not has_prev, True, [K(c, "vx")] + kb, [BK(bO)])
                    if has_prev:
                        MM(bank[bD][ps, 0:N], ones_b[:, 0:64], pa[:, 0:N], True, False, ["ones_b"] + ka, [BK(bD)])
                    MM(bank[bD][ps, 0:N], ones_b[:T, 0:64], pb[:T, 0:N], not has_prev, True, ["ones_b"] + kb, [BK(bD)])
                rdv = rd[:, 0:N].rearrange("p (a b) -> p a b", b=T)
                TT("dve", rdv, bank[bD][:, 0:N].rearrange("p (a b) -> p a b", b=T),
                   esink[:, j0:j0 + 4].unsqueeze(2).broadcast_to([128, 4, T]), ALU.add, [BK(bD), "esink"], ["rd"])
                ACT(rd[:, 0:N], rd[:, 0:N], AF.Ln, ["rd"], ["rd"])
                ACT(rd[:, 0:N], rd[:, 0:N], AF.Exp, ["rd"], ["rd"], scale=-1.0)
                TT("dve", c.qT[:, j0:j0 + 4, qc], bank[bO][:, 0:N].rearrange("p (a b) -> p a b", b=T), rdv,
                   ALU.mult, [BK(bO), "rd"], [qk])
                yield

        def ssd(c, t):
            T, CL = c.T, c.CL
            tc_ = tcols(c, t)
            xk = [K(c, "xbcT", ch) for ch in range(16)]
            dav = c.da[:T, t, :]
            MM(bank[0][:T, 0:16], Uincl[:T, :T], dav, True, True, ["Uincl", K(c, "da")], [BK(0)])
            MM(bank[0][:T, 16:32], Blk[:T, :T], dav, True, True, ["Blk", K(c, "da")], [BK(0)])
            MM(bank[0][:, 32:48], Sel0[:T, :], dav, True, True, ["Sel0", K(c, "da")], [BK(0)])
            if c.nchunk == 2:
                MM(bank[0][:, 48:64], Sel1[:T, :], dav, True, True, ["Sel1", K(c, "da")], [BK(0)])
            MM(bank[0][0:16, 64:64 + T], dav, Uincl[:T, :T], True, True, ["Uincl", K(c, "da")], [BK(0)])
            b1v = bkbf(1)
            b2v = bkbf(2)
            for j in range(8):
                TR(b1v[:T, j * 128:(j + 1) * 128], c.xbcT[:, j, tc_], ident_b[:, :], xk + ["ident_b"], [BK(1)])
            for g in range(4):
                TR(b2v[:T, g * 128:(g + 1) * 128], c.xbcT[:, 8 + g, tc_], ident_b[:, :], xk + ["ident_b"], [BK(2)])
            CP("dve", cs_sb[:T, :], bank[0][:T, 0:16], [BK(0)], ["cs_sb"])
            TT("dve", dd[:T, :], bank[0][:T, 16:32], cs_sb[:T, :], ALU.subtract, [BK(0), "cs_sb"], ["dd"])
            ACT(decay[:T, :], dd[:T, :], AF.Exp, ["dd"], ["decay"])
            ACT(dB[0][:, :], bank[0][:, 32:48], AF.Exp, [BK(0)], [("dB", 0)])
            if c.nchunk == 2:
                ACT(dB[1][:, :], bank[0][:, 48:64], AF.Exp, [BK(0)], [("dB", 1)])
            CP("act", csT_hi[:, :T], bank[0][0:16, 64:64 + T], [BK(0)], ["csT_hi"])
            TT("dve", csT_lo[:, :T], bank[0][0:16, 64:64 + T], csT_hi[:, :T], ALU.subtract, [BK(0), "csT_hi"], ["csT_lo"])
            TT("dve", dtd[:T, :], c.dtt[:T, t, :], decay[:T, :], ALU.mult, [K(c, "dtt"), "decay"], ["dtd"])
            xv = b1v[:T, 0:1024].rearrange("p (a b) -> p a b", b=64)
            TT("dve", xdd[:T, :].rearrange("p (a b) -> p a b", b=64), xv,
               dtd[:T, :].unsqueeze(2).broadcast_to([T, 16, 64]), ALU.mult, [BK(1), "dtd"], ["xdd"])
            CP("act", Btok[:T, :], b2v[:T, 0:512], [BK(2)], ["Btok"])
            TT("dve", xd[:T, :].rearrange("p (a b) -> p a b", b=64), xv,
               c.dtt[:T, t, :].unsqueeze(2).broadcast_to([T, 16, 64]), ALU.mult, [BK(1), K(c, "dtt")], ["xd"])

            def pe_group(g):
                bc, bs = (4, 5) if g % 2 == 0 else (6, 7)
                MM(bank[bc][:T, 0:T], c.xbcT[:, 8 + g, tc_], c.xbcT[:, 12 + g, tc_], True, True, xk, [BK(bc)])
                for r in range(4):
                    h = 4 * g + r
                    MM(bank[bs][:, r * 128:r * 128 + T], ident_b[0:16, h:h + 1].broadcast_to([16, 128]),
                       csT_hi[0:16, :T], True, False, ["ident_b", "csT_hi"], [BK(bs)])
                    MM(bank[bs][:, r * 128:r * 128 + T], ident_b[0:16, h:h + 1].broadcast_to([16, 128]),
                       csT_lo[0:16, :T], False, True, ["ident_b", "csT_lo"], [BK(bs)])

            def stt_group(g):
                bc, bs = (4, 5) if g % 2 == 0 else (6, 7)
                p = g % 2
                for r in range(4):
                    h = 4 * g + r
                    STT("dve", tmp2[p][:T, r * T:(r + 1) * T], bank[bs][:T, r * 128:r * 128 + T], cs_sb[:T, h:h + 1],
                        maskneg[:T, :T], ALU.subtract, ALU.add, [BK(bs), "cs_sb", "maskneg"], [("tmp", p)])

            pe_group(0)
            pe_group(1)
            hb_idx = [c.hb]
            for ci in range(c.nchunk):
                r0 = ci * 64
                for g in range(4):
                    bS = 2 + g // 2
                    MM(bank[bS][:, (g % 2) * 256:(g % 2) * 256 + 256], Btok[r0:r0 + CL, g * 128:(g + 1) * 128],
                       xdd[r0:r0 + CL, g * 256:(g + 1) * 256], True, True, ["Btok", "xdd"], [BK(bS)])
                TT("dve", t1[:, :].rearrange("p (a b) -> p a b", b=64), c.h[:, :].rearrange("p (a b) -> p a b", b=64),
                   dB[ci][:, :].unsqueeze(2).broadcast_to([128, 16, 64]), ALU.mult, [K(c, "h"), ("dB", ci)], ["yf"])
                TT("dve", c.h[:, 0:512], t1[:, 0:512], bank[2][:, :], ALU.add, ["yf", BK(2)], [K(c, "h")])
                TT("dve", c.h[:, 512:1024], t1[:, 512:1024], bank[3][:, :], ALU.add, ["yf", BK(3)], [K(c, "h")])
                nb = (hb_idx[-1] + 1) % c.nhbf
                CP("act", c.hbf[nb][:, :], c.h[:, :], [K(c, "h")], [K(c, "hbf", nb)])
                hb_idx.append(nb)
                if ci == 0:
                    stt_group(0)
            c.hb = hb_idx[-1]
            if c.nchunk == 1:
                pass
            for g in range(4):
                bc, bs = (4, 5) if g % 2 == 0 else (6, 7)
                p = g % 2
                ACT(lm2[p][:T, 0:4 * T], tmp2[p][:T, 0:4 * T], AF.Exp, [("tmp", p)], [("lm", p)])
                ACT(ecs2[p][:, 0:4 * T].rearrange("p (a b) -> p a b", b=T), bk3(bs, 4, T), AF.Exp, [BK(bs)], [("ecs", p)])
                if g + 1 < 4:
                    stt_group(g + 1)
                TT("dve", MT2[p][:T, 0:4 * T].rearrange("p (a b) -> p a b", b=T),
                   lm2[p][:T, 0:4 * T].rearrange("p (a b) -> p a b", b=T),
                   bank[bc][:T, 0:T].unsqueeze(1).broadcast_to([T, 4, T]), ALU.mult, [("lm", p), BK(bc)], [("MT", p)])
                TT("dve", Cs2[p][:, 0:4 * T].rearrange("p (a b) -> p a b", b=T),
                   ecs2[p][:, 0:4 * T].rearrange("p (a b) -> p a b", b=T),
                   c.xbcT[:, 12 + g, tc_].unsqueeze(1).broadcast_to([128, 4, T]), ALU.mult, [("ecs", p)] + xk, [("Cs", p)])
                for j in (2 * g, 2 * g + 1):
                    by = 2 if j < 4 else 3
                    MM(bank[by][:, (j % 4) * 128:(j % 4) * 128 + T], dskD[:, j, :], c.xbcT[:, j, tc_], True, False,
                       ["dskD"] + xk, [BK(by)])
                    for half in range(2):
                        h = 2 * j + half
                        r = h - 4 * g
                        o = bank[by][half * 64:half * 64 + 64, (j % 4) * 128:(j % 4) * 128 + T]
                        MM(o, xd[:T, h * 64:(h + 1) * 64], MT2[p][:T, r * T:(r + 1) * T], False, False,
                           ["xd", ("MT", p)], [BK(by)])
                        for ci in range(c.nchunk):
                            r0 = ci * 64
                            hbk = hb_idx[ci]
                            MM(bank[by][half * 64:half * 64 + 64, (j % 4) * 128 + r0:(j % 4) * 128 + r0 + CL],
                               c.hbf[hbk][:, h * 64:(h + 1) * 64], Cs2[p][:, r * T + r0:r * T + r0 + CL], False,
                               ci == c.nchunk - 1, [K(c, "hbf", hbk), ("Cs", p)], [BK(by)])
                if g + 2 < 4:
                    pe_group(g + 2)
            yv = yf[:, 0:8 * T].rearrange("p (a b) -> p a b", b=T)
            TT("dve", yv[:, 0:4, :], bk3(2, 4, T), c.szT[:, 0:4, tc_], ALU.mult, [BK(2), K(c, "szT", t)], ["yf"])
            TT("dve", yv[:, 4:8, :], bk3(3, 4, T), c.szT[:, 4:8, tc_], ALU.mult, [BK(3), K(c, "szT", t)], ["yf"])
            ACT(sq[:, 0:8 * T], yf[:, 0:8 * T], AF.Square, ["yf"], ["sq"])
            for g in range(4):
                for jj in range(2):
                    MM(bank[0][:, g * 128:g * 128 + T], ones_b[:, :], sq[:, (2 * g + jj) * T:(2 * g + jj + 1) * T],
                       jj == 0, jj == 1, ["ones_b", "sq"], [BK(0)])
            rvv = rv[:, 0:4 * T].rearrange("p (a b) -> p a b", b=T)
            ACT(rvv, bk3(0, 4, T), AF.Ln, [BK(0)], ["rv"], scale=1.0 / 256.0, bias=EPS)
            ACT(rv[:, 0:4 * T], rv[:, 0:4 * T], AF.Exp, ["rv"], ["rv"], scale=-0.5)
            for j in range(8):
                STT("dve", c.szT[:, j, tc_], yv[:, j, :], normwT[:, j:j + 1], rvv[:, j // 2, :], ALU.mult, ALU.mult,
                    ["yf", "rv", "small"], [K(c, "szT", t)])

        def mix_stage(c, st):
            softplus_stage(c)
            for t in range(c.NT):
                ssd(c, t)

        def acc_banks(c, ch):
            if c.sample:
                return [4 * ((ch + 1) % 2)]
            return [4 * (ch % 2) + t for t in range(4)]

        def out_proj(ctxs):
            for ch in range(2):
                for kh in range(2):
                    W, wk = wnext()
                    for c in ctxs:
                        T = c.T
                        bks = acc_banks(c, ch)
                        for t in range(c.NT):
                            b = bks[t]
                            for kc in range(8):
                                if kh == 0:
                                    lhs = c.qT[:, kc, tcols(c, t)]
                                    rk = K(c, "qT", t, kc // 4)
                                else:
                                    lhs = c.szT[:, kc, tcols(c, t)]
                                    rk = K(c, "szT", t)
                                MM(bank[b][:T, :], lhs, W[:, kc, :], kh == 0 and kc == 0, kh == 1 and kc == 7,
                                   [wk, rk], [BK(b)])
                            if kh == 1:
                                xs_ = c.xh[:T, t, ch * 512:(ch + 1) * 512]
                                TT("dve", xs_, bank[b][:T, :], xs_, ALU.add, [BK(b), K(c, "xh", t)], [K(c, "xh", t)])

        def ffn(ctxs):
            gb = [0]
            for s in range(11):
                W, wk = wnext()
                for c in ctxs:
                    S = c.S
                    for mm in range(2):
                        m = 2 * s + mm
                        gb[0] = (gb[0] + 1) % 4
                        bG, bU = 2 * gb[0], 2 * gb[0] + 1
                        for kc in range(8):
                            MM(bank[bG][:, 0:S], W[:, kc, mm * 128:(mm + 1) * 128], c.uT[:, kc, 0:S], kc == 0, kc == 7,
                               [wk, K(c, "uT")], [BK(bG)])
                        for kc in range(8):
                            MM(bank[bU][:, 0:S], W[:, kc, 256 + mm * 128:256 + (mm + 1) * 128], c.uT[:, kc, 0:S],
                               kc == 0, kc == 7, [wk, K(c, "uT")], [BK(bU)])
                        sg = sgt[m % 2]
                        ACT(sg[:, 0:S], bank[bG][:, 0:S], AF.Silu, [BK(bG)], [("sgt", m % 2)])
                        TT("dve", c.hmid[:, m, 0:S], sg[:, 0:S], bank[bU][:, 0:S], ALU.mult,
                           [("sgt", m % 2), BK(bU)], [K(c, "hmid", m)] + ([K(c, "bigR")] if c.alias else []))
            for ch in range(2):
                for kp in range(3):
                    W, wk = wnext()
                    nk = 6 if kp == 2 else 8
                    for c in ctxs:
                        T = c.T
                        bks = acc_banks(c, ch)
                        for t in range(c.NT):
                            b = bks[t]
                            for kc in range(nk):
                                m = kp * 8 + kc
                                MM(bank[b][:T, :], c.hmid[:, m, tcols(c, t)], W[:, kc, :], kp == 0 and kc == 0,
                                   kp == 2 and kc == nk - 1, [wk, K(c, "hmid", m)] + bigW(c), [BK(b)])
                            if kp == 2:
                                xs_ = c.xh[:T, t, ch * 512:(ch + 1) * 512]
                                TT("dve", xs_, bank[b][:T, :], xs_, ALU.add, [BK(b), K(c, "xh", t)], [K(c, "xh", t)])

        def final_stage(c, st):
            T = c.T
            stats4(c)
            for t in range(c.NT):
                ob, okey = (yo, "yo") if t % 2 == 0 else (yf, "yf")
                STT("dve", ob[:T, :], c.xh[:T, t, :], rstd4[:T, t:t + 1], fg[:T, :], ALU.mult, ALU.mult,
                    [K(c, "xh", t), "rstd4", "fg"], [okey])
                if c.sample:
                    DMA("sp", ys_d[c.seq], ob[:T, :], [okey], [])
                else:
                    r0 = (st * 4 + t) * 128
                    DMA("sp", yp_d[r0:r0 + 128, :], ob[:T, :], [okey], [])

        def state_out(c, dst):
            for half in range(2):
                b = 2 + half
                for q in range(4):
                    blk = half * 4 + q
                    TR(bank[b][:, q * 128:(q + 1) * 128], c.h[:, blk * 128:(blk + 1) * 128], ident_f[:, :],
                       [K(c, "h"), "ident_f"], [BK(b)])
                CP("act" if half else "dve", sst[:, half * 4:half * 4 + 4, :],
                   bank[b][:, :].rearrange("p (a b) -> p a b", b=128), [BK(b)], ["yo"])
            DMA("sp", dst.rearrange("(b p) n -> p b n", p=128), sst[:, :, :], ["yo"], [])

        def sample_init(c, seq):
            c.seq = seq
            DMA("sp", c.xh[:16, 0, :], xs_d[seq], [], [K(c, "xh", 0)])
            DMA("pool", xd[:, 0:256], ck_d[seq], [], ["xd"])
            bv = bkbf(0)
            for kc in range(2):
                TR(bv[:, kc * 128:(kc + 1) * 128], xd[:, kc * 128:(kc + 1) * 128], ident_b[:, :], ["xd", "ident_b"], [BK(0)])
            CP("dve", c.kT[:, :, 0:128], bv[:, 0:256].rearrange("p (a b) -> p a b", b=128), [BK(0)], [K(c, "kT")])
            DMA("pool", c.vx[:, 0, :], cv_d[seq], [], [K(c, "vx")])
            DMA("sp", sconv_sb[:, :], sconv_d[seq], [], ["sconv_sb"])
            for ch in range(16):
                TR(bank[1][:, ch * 4:ch * 4 + 3], sconv_sb[0:3, ch * 128:(ch + 1) * 128], ident_f[0:3, 0:3],
                   ["sconv_sb", "ident_f"], [BK(1)])
            CP("dve", c.rawT[:, :, 0:3], bank[1][:, 0:64].rearrange("p (a b) -> p a b", b=4)[:, :, 0:3], [BK(1)],
               [K(c, "rawT", ch) for ch in range(16)])
            DMA("sp", sst[:, :, :], sssm_d[seq].rearrange("(b p) n -> p b n", p=128), [], ["yo"])
            for half in range(2):
                b = 2 + half
                for q in range(4):
                    blk = half * 4 + q
                    TR(bank[b][:, q * 128:(q + 1) * 128], sst[:, blk, :], ident_f[:, :], ["yo", "ident_f"], [BK(b)])
                CP("dve", c.h[:, half * 512:(half + 1) * 512], bank[b][:, :], [BK(b)], [K(c, "h")])
            c.hb = 0
            CP("act", c.hbf[0][:, :], c.h[:, :], [K(c, "h")], [K(c, "hbf", 0)])

        MS("dve", pc.h[:, :], 0.0, [K(pc, "h")])
        pc.hb = 0
        MS("pool", pc.hbf[0][:, :], 0.0, [K(pc, "hbf", 0)])
        MS("pool", pc.chist[:, :, :], 0.0, [K(pc, "chist")])

        for st in range(NPASS):
            ctxs = [pc]
            import os as _os2
            NOS = bool(_os2.environ.get("DEBUG_NOSAMPLE"))
            if st < 2 and not NOS:
                ctxs.append(sc)
                sample_init(sc, st)
            for t in range(4):
                r0 = (st * 4 + t) * 128
                DMA("sp", pc.xh[:, t, :], xp_d[r0:r0 + 128, :], [], [K(pc, "xh", t)])
            LIM = int(_os2.environ.get("STAGE_LIMIT", "99"))
            if LIM >= 1:
                for c in ctxs:
                    norm_stage(c, g1T)
            if LIM >= 2:
                in_proj(ctxs, st)
            if LIM >= 3:
                for c in ctxs:
                    conv_stage(c)
            if LIM >= 4:
                for c in ctxs:
                    mix_stage(c, st)
            if LIM >= 5:
                out_proj(ctxs)
            if LIM >= 6:
                for c in ctxs:
                    norm_stage(c, g2T)
            if LIM >= 7:
                ffn(ctxs)
            if LIM >= 8:
                for c in ctxs:
                    final_stage(c, st)
            if st < 2 and not NOS and LIM >= 9:
                state_out(sc, ssn_d[st])
        if LIM >= 9:
            state_out(pc, sp_d)
        P.emit_all()
    return nc


_CACHE = {}


def kernel(**inputs):
    f = np.float32
    shared = _host_layout(inputs)
    xp = np.asarray(inputs["x_prompt"], f)
    xs = np.asarray(inputs["x_sample"], f)
    ck = np.asarray(inputs["cache_k"], f)[0].reshape(16, 128, 256)
    cv = np.asarray(inputs["cache_v"], f)[0].reshape(16, 128, 256)
    sconv = np.asarray(inputs["state_conv"], f)[0]
    sssm = np.asarray(inputs["state_ssm"], f)[0].reshape(16, 1024, 128)
    in_maps = []
    for i in range(NCORES):
        m = dict(shared)
        m["xp"] = np.ascontiguousarray(xp[i])
        m["xs"] = np.ascontiguousarray(xs[2 * i:2 * i + 2])
        m["ck"] = np.ascontiguousarray(ck[2 * i:2 * i + 2])
        m["cv"] = np.ascontiguousarray(cv[2 * i:2 * i + 2])
        m["sconv"] = np.ascontiguousarray(sconv[2 * i:2 * i + 2])
        m["sssm"] = np.ascontiguousarray(sssm[2 * i:2 * i + 2])
        in_maps.append(m)
    if "nc" not in _CACHE:
        _CACHE["nc"] = build_program()
    nc = _CACHE["nc"]
    res = run_bass_kernel_spmd(nc, in_maps, core_ids=list(range(NCORES)))
    R = res.results
    y_prompt = np.stack([R[i]["yp"] for i in range(NCORES)]).astype(f)
    y_sample = np.concatenate([R[i]["ys"] for i in range(NCORES)]).astype(f)
    k_prompt = np.stack([R[i]["kp"] for i in range(NCORES)]).reshape(1, 8, 128, 4, 64).astype(f)
    v_prompt = np.stack([R[i]["vp"] for i in range(NCORES)]).reshape(1, 8, 128, 4, 64).astype(f)
    conv_prompt = np.stack([R[i]["cp"] for i in range(NCORES)]).reshape(1, 8, 3, 2048).astype(f)
    ssm_prompt = np.stack([R[i]["spo"] for i in range(NCORES)]).reshape(1, 8, 16, 64, 128).astype(f)
    k_sample = np.concatenate([R[i]["ksn"] for i in range(NCORES)]).reshape(1, 16, 16, 4, 64).astype(f)
    v_sample = np.concatenate([R[i]["vsn"] for i in range(NCORES)]).reshape(1, 16, 16, 4, 64).astype(f)
    conv_sample = np.concatenate([R[i]["csn"] for i in range(NCORES)]).reshape(1, 16, 3, 2048).astype(f)
    ssm_sample = np.concatenate([R[i]["ssn"] for i in range(NCORES)]).reshape(1, 16, 16, 64, 128).astype(f)
    return (y_prompt, y_sample, k_prompt, v_prompt, conv_prompt, ssm_prompt,
            k_sample, v_sample, conv_sample, ssm_sample)
```

```python
import contextlib
import numpy as np
import concourse.bass as bass
import concourse.mybir as mybir
from concourse.bass_utils import run_bass_kernel_spmd

F32 = mybir.dt.float32
BF16 = mybir.dt.bfloat16
AF = mybir.ActivationFunctionType
ALU = mybir.AluOpType

ENGS = ("pe", "act", "dve", "pool", "sp")
NCORES = 8
SEQ = 4096
D = 1024
NPASS = 8
SP = 512
EPS = 1e-6
RING = 3


class _Op:
    __slots__ = ("eng", "idx", "emit", "deps", "is_dma", "dma_sem", "dma_val", "signal", "sigval")

    def __init__(self, eng, idx, emit, is_dma):
        self.eng = eng
        self.idx = idx
        self.emit = emit
        self.deps = []
        self.is_dma = is_dma
        self.dma_sem = None
        self.dma_val = 0
        self.signal = False
        self.sigval = 0


class Prog:
    NDMA_SEMS = 8

    def __init__(self, nc):
        self.nc = nc
        self.ops = {e: [] for e in ENGS}
        self.last_w = {}
        self.readers = {}
        self.dma_rr = {e: 0 for e in ENGS}
        self.dma_last = {}
        self.dma_cnt = {}

    def add(self, eng, emit, reads=(), writes=(), dma=False):
        lst = self.ops[eng]
        op = _Op(eng, len(lst), emit, dma)
        lst.append(op)
        deps = []
        bkeys = [k for k in list(reads) + list(writes) if isinstance(k, tuple) and k and k[0] == "bk"]
        if bkeys:
            reads = [k for k in reads if not (isinstance(k, tuple) and k and k[0] == "bk")]
            writes = [k for k in writes if not (isinstance(k, tuple) and k and k[0] == "bk")]
            for k in set(bkeys):
                w = self.last_w.get(k)
                if w is not None and w.eng != eng:
                    deps.append(w)
                self.last_w[k] = op
        for r in reads:
            w = self.last_w.get(r)
            if w is not None:
                deps.append(w)
        for w_ in writes:
            w = self.last_w.get(w_)
            if w is not None:
                deps.append(w)
            deps.extend(self.readers.get(w_, ()))
        if dma:
            j = self.dma_rr[eng] % self.NDMA_SEMS
            self.dma_rr[eng] += 1
            prev = self.dma_last.get((eng, j))
            if prev is not None:
                deps.append(prev)
            self.dma_last[(eng, j)] = op
            c = self.dma_cnt.get((eng, j), 0) + 1
            self.dma_cnt[(eng, j)] = c
            op.dma_sem = (eng, j)
            op.dma_val = 16 * c
        seen = set()
        best = {}
        for d in deps:
            if d is op or id(d) in seen:
                continue
            seen.add(id(d))
            if d.is_dma:
                op.deps.append(d)
                continue
            b = best.get(d.eng)
            if b is None or d.idx > b.idx:
                best[d.eng] = d
        for d in best.values():
            if d.eng == eng and eng == "pe":
                continue
            d.signal = True
            op.deps.append(d)
        for r in reads:
            lst2 = self.readers.setdefault(r, [])
            if not dma:
                for k_ in range(len(lst2)):
                    if (not lst2[k_].is_dma) and lst2[k_].eng == eng:
                        lst2[k_] = op
                        break
                else:
                    lst2.append(op)
            else:
                lst2.append(op)
        for w_ in writes:
            self.last_w[w_] = op
            self.readers[w_] = []
        return op

    def emit_all(self):
        nc = self.nc
        with contextlib.ExitStack() as es:
            esem = {e: es.enter_context(nc.semaphore("s_" + e)) for e in ENGS}
            dsem = {}
            for e in ENGS:
                for j in range(min(self.NDMA_SEMS, self.dma_rr[e])):
                    dsem[(e, j)] = es.enter_context(nc.semaphore("d_%s%d" % (e, j)))
            for e in ENGS:
                c = 0
                for op in self.ops[e]:
                    if op.signal and not op.is_dma:
                        c += 1
                        op.sigval = c
            block = es.enter_context(nc.Block())

            def run(eng_name, eng):
                waited = {}
                for op in self.ops[eng_name]:
                    for d in op.deps:
                        if d.is_dma:
                            key = ("d",) + d.dma_sem
                            sem = dsem[d.dma_sem]
                            val = d.dma_val
                        else:
                            key = ("e", d.eng)
                            sem = esem[d.eng]
                            val = d.sigval
                        if waited.get(key, 0) >= val:
                            continue
                        waited[key] = val
                        eng.wait_ge(sem, val)
                    ins = op.emit(eng)
                    if op.is_dma:
                        ins.then_inc(dsem[op.dma_sem], 16)
                    elif op.signal:
                        ins.then_inc(esem[eng_name], 1)
                for (e, j), lastop in self.dma_last.items():
                    if e == eng_name and waited.get(("d", e, j), 0) < lastop.dma_val:
                        eng.wait_ge(dsem[(e, j)], lastop.dma_val)

            @block.sync
            def _(eng):
                run("sp", eng)

            @block.tensor
            def _(eng):
                run("pe", eng)

            @block.scalar
            def _(eng):
                run("act", eng)

            @block.vector
            def _(eng):
                run("dve", eng)

            @block.gpsimd
            def _(eng):
                run("pool", eng)


def _head_pairs():
    pairs = []
    for j in range(8):
        if j < 4:
            pairs.append((j, 4 + j))
        else:
            pairs.append((8 + (j - 4), 12 + (j - 4)))
    return pairs


def _slotify(w, cols):
    sub = w[:, cols]
    K = sub.shape[0]
    return np.ascontiguousarray(sub.reshape(K // 128, 128, len(cols)).transpose(1, 0, 2).reshape(128, -1))


def _host_layout(inp):
    f = np.float32
    pairs = _head_pairs()
    w_in = np.asarray(inp["w_in"][0], f)
    qcols = []
    for (a, b) in pairs:
        qcols += list(range(a * 64, a * 64 + 64)) + list(range(b * 64, b * 64 + 64))
    qcols = np.array(qcols)
    slots = [qcols[0:512], qcols[512:1024], np.arange(1024, 1536), np.arange(1536, 2048), np.arange(2048, 2560)]
    for s in range(4):
        slots.append(np.arange(2560 + s * 512, 2560 + (s + 1) * 512))
    win = np.stack([_slotify(w_in, c) for c in slots])
    wdt = _slotify(w_in, np.arange(4608, 4624))
    w_out = np.asarray(inp["w_out"][0], f)
    rowperm = np.concatenate([qcols, np.arange(1024, 2048)])
    wo = w_out[rowperm]
    wout = np.stack([_slotify(wo[kh * 1024:(kh + 1) * 1024], np.arange(ch * 512, (ch + 1) * 512))
                     for ch in range(2) for kh in range(2)])
    wg = np.asarray(inp["w_gate"][0], f)
    wu = np.asarray(inp["w_up"][0], f)
    wgu = np.stack([np.concatenate([_slotify(wg, np.arange(s * 256, (s + 1) * 256)).reshape(128, 8, 256),
                                    _slotify(wu, np.arange(s * 256, (s + 1) * 256)).reshape(128, 8, 256)],
                                   axis=2).reshape(128, 4096) for s in range(11)])
    wd = np.asarray(inp["w_down"][0], f)
    wdp = np.zeros((3072, 1024), f)
    wdp[:2816] = wd
    wdn = np.stack([_slotify(wdp[kp * 1024:(kp + 1) * 1024], np.arange(ch * 512, (ch + 1) * 512))
                    for ch in range(2) for kp in range(3)])

    def colT(v):
        v = np.asarray(v, f)
        return np.ascontiguousarray(v.reshape(-1, 128).T)

    def rep(v):
        v = np.asarray(v, f)
        return np.ascontiguousarray(np.broadcast_to(v[None, :], (128, v.shape[0])))

    cw = np.asarray(inp["conv_w"][0], f)
    convw = np.ascontiguousarray(cw.reshape(4, 16, 128).transpose(2, 0, 1).reshape(128, 64))
    dsk = np.asarray(inp["d_skip"][0], f)
    sinks = np.asarray(inp["sinks"][0], f)
    dskT = np.zeros((128, 8), f)
    sinkT = np.zeros((128, 8), f)
    for j in range(8):
        dskT[:64, j] = dsk[2 * j]
        dskT[64:, j] = dsk[2 * j + 1]
        sinkT[:64, j] = sinks[pairs[j][0]]
        sinkT[64:, j] = sinks[pairs[j][1]]
    small = np.concatenate([
        colT(inp["ln1_g"][0]), colT(inp["ln2_g"][0]), convw, colT(inp["conv_b"][0]),
        rep(inp["dt_bias"][0]), rep(inp["a_log"][0]), dskT, colT(inp["ssm_norm_w"][0]), sinkT], axis=1)
    shared = {"win": win, "wdt": wdt, "wout": wout, "wgu": wgu, "wdn": wdn,
              "small": np.ascontiguousarray(small), "fg": rep(inp["final_g"])}
    return shared


class Ctx:
    pass


def build_program():
    nc = bass.Bass("TRN2", target_bir_lowering=False)

    def din(name, shape):
        return nc.dram_tensor(name, shape, F32, kind="ExternalInput").ap()

    def dout(name, shape):
        return nc.dram_tensor(name, shape, F32, kind="ExternalOutput").ap()

    xp_d = din("xp", [SEQ, D])
    xs_d = din("xs", [2, 16, D])
    ck_d = din("ck", [2, 128, 256])
    cv_d = din("cv", [2, 128, 256])
    sconv_d = din("sconv", [2, 3, 2048])
    sssm_d = din("sssm", [2, 1024, 128])
    win_d = din("win", [9, 128, 4096])
    wdt_d = din("wdt", [128, 128])
    wout_d = din("wout", [4, 128, 4096])
    wgu_d = din("wgu", [11, 128, 4096])
    wdn_d = din("wdn", [6, 128, 4096])
    small_d = din("small", [128, 152])
    fg_d = din("fg", [128, 1024])
    yp_d = dout("yp", [SEQ, D])
    ys_d = dout("ys", [2, 16, D])
    kp_d = dout("kp", [128, 256])
    vp_d = dout("vp", [128, 256])
    cp_d = dout("cp", [3, 2048])
    sp_d = dout("spo", [1024, 128])
    ksn_d = dout("ksn", [2, 16, 256])
    vsn_d = dout("vsn", [2, 16, 256])
    csn_d = dout("csn", [2, 3, 2048])
    ssn_d = dout("ssn", [2, 1024, 128])

    es = contextlib.ExitStack()
    with es:
        def sb(name, shape, dt=F32):
            return es.enter_context(nc.sbuf_tensor("sb_" + name, shape, dt))

        P = Prog(nc)
        bank = [es.enter_context(nc.psum_tensor("bk%d" % i, [128, 512], F32)) for i in range(8)]

        def BK(i):
            return ("bk", i)

        def bk3(i, n, T, rows=128, inner=128):
            return bank[i][:, 0:n * inner].rearrange("p (a b) -> p a b", b=inner)[0:rows, :, 0:T]

        def bkbf(i):
            return bank[i][:, :].bitcast(BF16)

        def MM(out, lhsT, rhs, start, stop, reads, writes):
            P.add("pe", lambda e: e.matmul(out, lhsT=lhsT, rhs=rhs, start=start, stop=stop), reads, writes)

        def TR(out, in_, ident, reads, writes):
            P.add("pe", lambda e: e.transpose(out=out, in_=in_, identity=ident), reads, writes)

        def ACT(out, in_, func, reads, writes, bias=None, scale=None, accum=None):
            kw = {}
            if bias is not None:
                kw["bias"] = bias
            if scale is not None:
                kw["scale"] = scale
            if accum is not None:
                kw["accum_out"] = accum
            P.add("act", lambda e: e.activation(out=out, in_=in_, func=func, **kw), reads, writes)

        def TT(eng, out, in0, in1, op, reads, writes):
            P.add(eng, lambda e: e.tensor_tensor(out=out, in0=in0, in1=in1, op=op), reads, writes)

        def TS(eng, out, in0, s1, s2, op0, op1, reads, writes):
            if s2 is None:
                P.add(eng, lambda e: e.tensor_scalar(out=out, in0=in0, scalar1=s1, scalar2=None, op0=op0), reads, writes)
            else:
                P.add(eng, lambda e: e.tensor_scalar(out=out, in0=in0, scalar1=s1, scalar2=s2, op0=op0, op1=op1),
                      reads, writes)

        def STT(eng, out, in0, scalar, in1, op0, op1, reads, writes):
            P.add(eng, lambda e: e.scalar_tensor_tensor(out=out, in0=in0, scalar=scalar, in1=in1, op0=op0, op1=op1),
                  reads, writes)

        def CP(eng, out, in_, reads, writes):
            if eng == "act":
                P.add("act", lambda e: e.copy(out=out, in_=in_), reads, writes)
            else:
                P.add(eng, lambda e: e.tensor_copy(out=out, in_=in_), reads, writes)

        def MS(eng, ap, val, writes, reads=()):
            P.add(eng, lambda e: e.memset(ap, val), reads, writes)

        def DMA(eng, out, in_, reads, writes):
            P.add(eng, lambda e: e.dma_start(out=out, in_=in_), reads, writes, dma=True)

        evac_rr = [0]

        def EV(out, in_, reads, writes):
            evac_rr[0] += 1
            CP("act" if evac_rr[0] % 2 else "dve", out, in_, reads, writes)

        small = sb("small", [128, 152])
        fg = sb("fg", [128, 1024])
        ident_f = sb("ident_f", [128, 128])
        ident_b = sb("ident_b", [128, 128], BF16)
        ones_b = sb("ones_b", [128, 128], BF16)
        Uincl = sb("Uincl", [128, 128])
        Blk = sb("Blk", [128, 128])
        Sel0 = sb("Sel0", [128, 128])
        Sel1 = sb("Sel1", [128, 128])
        maskneg = sb("maskneg", [128, 128])
        negh = sb("negh", [128, 4])
        dskD = sb("dskD", [128, 8, 128], BF16)
        mbias = sb("mbias", [128, 2])
        aneg = sb("aneg", [128, 16])
        esink = sb("esink", [128, 8])
        diagW = sb("diagW", [128, 64, 128], BF16)
        wdt = sb("wdt", [128, 8, 16], BF16)
        g1T = small[:, 0:8]
        g2T = small[:, 8:16]
        convb = small[:, 80:96]
        dtb = small[:, 96:112]
        dskT = small[:, 128:136]
        normwT = small[:, 136:144]

        DMA("sp", small[:, :], small_d, [], ["small"])
        DMA("sp", fg[:, :], fg_d, [], ["fg"])
        DMA("pool", wdt[:, :, :], wdt_d.rearrange("p (k c) -> p k c", c=16), [], ["wdt"])
        MS("pool", ident_f[:, :], 1.0, ["ident_f"])
        P.add("pool", lambda e: e.affine_select(out=ident_f[:, :], in_=ident_f[:, :], pattern=[[-1, 128]],
                                                 compare_op=ALU.is_equal, fill=0.0, base=0, channel_multiplier=1),
              ["ident_f"], ["ident_f"])
        CP("dve", ident_b[:, :], ident_f[:, :], ["ident_f"], ["ident_b"])
        MS("dve", ones_b[:, :], 1.0, ["ones_b"])
        MS("pool", Uincl[:, :], 1.0, ["Uincl"])
        P.add("pool", lambda e: e.affine_select(out=Uincl[:, :], in_=Uincl[:, :], pattern=[[1, 128]],
                                                 compare_op=ALU.is_ge, fill=0.0, base=0, channel_multiplier=-1),
              ["Uincl"], ["Uincl"])
        MS("pool", Uincl[0:64, 64:128], 0.0, ["Uincl"], ["Uincl"])
        MS("dve", Blk[:, :], 0.0, ["Blk"])
        MS("dve", Blk[0:64, 0:64], 1.0, ["Blk"], ["Blk"])
        MS("dve", Blk[64:128, 64:128], 1.0, ["Blk"], ["Blk"])
        MS("dve", Sel0[:, :], 0.0, ["Sel0"])
        MS("dve", Sel0[0:64, :], 1.0, ["Sel0"], ["Sel0"])
        MS("dve", Sel1[:, :], 0.0, ["Sel1"])
        MS("dve", Sel1[64:128, :], 1.0, ["Sel1"], ["Sel1"])
        TS("dve", maskneg[:, :], Uincl[:, :], 1.0, 30000.0, ALU.subtract, ALU.mult, ["Uincl"], ["maskneg"])
        MS("pool", negh[:, :], -0.5, ["negh"])
        for j in range(8):
            TS("dve", dskD[:, j, :], ident_f[:, :], small[:, 128 + j:129 + j], None, ALU.mult, None,
               ["ident_f", "small"], ["dskD"])
        MS("dve", mbias[:, :], 0.0, ["mbias"])
        MS("dve", mbias[0:64, 0:1], -30000.0, ["mbias"], ["mbias"])
        MS("dve", mbias[64:128, 1:2], -30000.0, ["mbias"], ["mbias"])
        ACT(aneg[:, :], small[:, 112:128], AF.Exp, ["small"], ["aneg"])
        TS("dve", aneg[:, :], aneg[:, :], -1.0, None, ALU.mult, None, ["aneg"], ["aneg"])
        ACT(esink[:, :], small[:, 144:152], AF.Exp, ["small"], ["esink"])
        for ic in range(64):
            TS("dve" if ic % 2 else "pool", diagW[:, ic, :], ident_f[:, :], small[:, 16 + ic:17 + ic], None, ALU.mult, None,
               ["ident_f", "small"], [("diagW", ic)])

        un2 = [sb("un%d" % i, [128, 1024], BF16) for i in range(2)]
        ss4 = sb("ss4", [128, 4])
        vv4 = sb("vv4", [128, 4])
        rstd4 = sb("rstd4", [128, 4])
        PT = [[sb("PT%d%d" % (a, b), [128, 512], BF16) for b in range(2)] for a in range(2)]
        rd = sb("rd", [128, 512])
        ost = sb("ost", [128, 512])
        yo = sb("yo", [128, 1024])
        sgt = [sb("sgt%d" % i, [128, 512], BF16) for i in range(2)]
        d2 = sb("d2", [128, 16])
        cs_sb = sb("cs_sb", [128, 16])
        dd = sb("dd", [128, 16])
        decay = sb("decay", [128, 16])
        dtd = sb("dtd", [128, 16])
        dB = [sb("dB%d" % i, [128, 16]) for i in range(2)]
        csT_hi = sb("csT_hi", [16, 128], BF16)
        csT_lo = sb("csT_lo", [16, 128], BF16)
        xd = sb("xd", [128, 1024], BF16)
        xdd = sb("xdd", [128, 1024], BF16)
        Btok = sb("Btok", [128, 512], BF16)
        tmp2 = [sb("tmp%d" % i, [128, 512]) for i in range(2)]
        lm2 = [sb("lm%d" % i, [128, 512], BF16) for i in range(2)]
        MT2 = [sb("MT%d" % i, [128, 512], BF16) for i in range(2)]
        ecs2 = [sb("ecs%d" % i, [128, 512], BF16) for i in range(2)]
        Cs2 = [sb("Cs%d" % i, [128, 512], BF16) for i in range(2)]
        yf = sb("yf", [128, 1024])
        t1 = yf
        sq = sb("sq", [128, 1024], BF16)
        rv = sb("rv", [128, 512])

        ring = [sb("ring%d" % i, [128, 8, 512], BF16) for i in range(RING)]

        def mkctx(name, S, NT, T, nchunk, CL, sample):
            c = Ctx()
            c.name, c.S, c.NT, c.T, c.nchunk, c.CL, c.sample = name, S, NT, T, nchunk, CL, sample
            c.HK = 128
            c.xh = sb(name + "xh", [128, NT, 1024])
            c.uT = sb(name + "uT", [128, 8, S], BF16)
            c.qT = sb(name + "qT", [128, 8, S], BF16)
            c.kT = sb(name + "kT", [128, 2, 128 + S], BF16)
            c.vx = sb(name + "vx", [128, NT + 1, 256], BF16)
            c.szT = sb(name + "szT", [128, 8, S], BF16)
            if sample:
                c.rawT = sb(name + "rawT", [128, 16, S + 3], BF16)
                c.xbcT = sb(name + "xbcT", [128, 16, S], BF16)
                c.hmid = sb(name + "hmid", [128, 22, S], BF16)
            else:
                big = sb(name + "big", [128, 16 * (S + 3) + 16 * S], BF16)
                c.rawT = big[:, 0:16 * (S + 3)].rearrange("p (c n) -> p c n", n=S + 3)
                c.xbcT = big[:, 16 * (S + 3):].rearrange("p (c n) -> p c n", n=S)
                c.hmid = big[:, 0:22 * S].rearrange("p (c n) -> p c n", n=S)
            c.alias = not sample
            c.chist = sb(name + "chist", [128, 16, 3], BF16)
            c.h = sb(name + "h", [128, 1024])
            c.d1 = sb(name + "d1", [128, NT, 16])
            c.dtt = sb(name + "dtt", [128, NT, 16])
            c.da = sb(name + "da", [128, NT, 16])
            c.nhbf = 2 if sample else 3
            c.hbf = [sb(name + "hbf%d" % i, [128, 1024], BF16) for i in range(c.nhbf)]
            c.hb = 0
            return c

        pc = mkctx("p", SP, 4, 128, 2, 64, False)
        sc = mkctx("s", 16, 1, 16, 1, 16, True)
        sconv_sb = sb("sconv_sb", [3, 2048])
        sst = yo[:, :].rearrange("p (a b) -> p a b", b=128)

        def K(c, *a):
            return (c.name,) + a

        def bigR(c):
            return [K(c, "bigR")] if c.alias else []

        def bigW(c):
            return [K(c, "bigW")] if c.alias else []

        wseq = []
        for st in range(NPASS):
            for s in range(9):
                wseq.append((win_d[s], 8))
            for s in range(4):
                wseq.append((wout_d[s], 8))
            for s in range(11):
                wseq.append((wgu_d[s], 8))
            for s in range(6):
                wseq.append((wdn_d[s], 6 if s % 3 == 2 else 8))
        wstate = {"issued": 0, "cur": -1}

        wscr = nc.dram_tensor("wscr", [30, 128, 4096], BF16).ap()

        def wprefetch(upto):
            while wstate["issued"] <= min(upto, len(wseq) - 1):
                n = wstate["issued"]
                src, nk = wseq[n]
                r = n % RING
                slot = n % 30
                if n < 30:
                    DMA("pool", ring[r][:, 0:nk, :], src[:, 0:nk * 512].rearrange("p (k c) -> p k c", c=512),
                        [], [("ring", r)])
                    if NPASS > 1:
                        DMA("sp", wscr[slot][:, 0:nk * 512].rearrange("p (k c) -> p k c", c=512), ring[r][:, 0:nk, :],
                            [("ring", r)], [("wscr", slot)])
                else:
                    DMA("pool", ring[r][:, 0:nk, :], wscr[slot][:, 0:nk * 512].rearrange("p (k c) -> p k c", c=512),
                        [("wscr", slot)], [("ring", r)])
                wstate["issued"] += 1

        def wnext():
            wstate["cur"] += 1
            n = wstate["cur"]
            wprefetch(n + RING - 1)
            return ring[n % RING], ("ring", n % RING)

        def tcols(c, t):
            return slice(t * c.T, (t + 1) * c.T)

        def stats4(c):
            T = c.T
            sk = [("ss4", t) for t in range(c.NT)]
            MS("dve", ss4[:T, 0:c.NT], 0.0, sk)
            for t in range(c.NT):
                ACT(un2[t % 2][:T, :], c.xh[:T, t, :], AF.Square, [K(c, "xh", t)], [("un", t % 2), ("ss4", t)],
                    accum=ss4[:T, t:t + 1])
            TS("dve", vv4[:T, 0:c.NT], ss4[:T, 0:c.NT], 1.0 / D, EPS, ALU.mult, ALU.add, sk, ["vv4"])
            TT("pool", rstd4[:T, 0:c.NT], vv4[:T, 0:c.NT], negh[:T, 0:c.NT], ALU.pow, ["vv4", "negh"], ["rstd4"])

        def norm_stage(c, gT):
            T = c.T
            stats4(c)

            def scale(t):
                TS("dve", un2[t % 2][:T, :], c.xh[:T, t, :], rstd4[:T, t:t + 1], None, ALU.mult, None,
                   [K(c, "xh", t), "rstd4"], [("un", t % 2)])
                bv = bkbf(t % 2)
                for kc in range(8):
                    TR(bv[:, kc * 128:kc * 128 + T], un2[t % 2][:T, kc * 128:(kc + 1) * 128], ident_b[:T, :T],
                       [("un", t % 2), "ident_b"], [BK(t % 2)])

            def evac(t):
                bv = bkbf(t % 2)
                TT("dve", c.uT[:, :, tcols(c, t)], bv[:, 0:1024].rearrange("p (a b) -> p a b", b=128)[:, :, 0:T],
                   gT.unsqueeze(2).broadcast_to([128, 8, T]), ALU.mult, [BK(t % 2), "small"], [K(c, "uT")])

            scale(0)
            for t in range(1, c.NT):
                scale(t)
                evac(t - 1)
            evac(c.NT - 1)

        fb = [0]

        fb_mod = [4]

        def fbank():
            fb[0] = (fb[0] + 1) % fb_mod[0]
            return fb[0]

        def feat_chunk(c, W, wk, col0, dst, dkeys_r, dkeys_w, func=None, bias=None):
            b = fbank()
            S = c.S
            for kc in range(8):
                MM(bank[b][:, 0:S], W[:, kc, col0:col0 + 128], c.uT[:, kc, 0:S], kc == 0, kc == 7,
                   [wk, K(c, "uT")], [BK(b)])
            if func is None:
                EV(dst, bank[b][:, 0:S], [BK(b)] + dkeys_r, dkeys_w)
            else:
                ACT(dst, bank[b][:, 0:S], func, [BK(b)] + dkeys_r, dkeys_w, bias=bias)

        def run_streams(items):
            items = list(items)
            while items:
                for item in list(items):
                    g_, rep = item
                    for _ in range(rep):
                        try:
                            next(g_)
                        except StopIteration:
                            items.remove(item)
                            break

        def in_slot(ctxs, st, s):
            W, wk = wnext()
            for c in ctxs:
                S, T = c.S, c.T
                last = c.sample or st == NPASS - 1
                if s < 2:
                    for cc in range(4):
                        j = 4 * s + cc
                        feat_chunk(c, W, wk, cc * 128, c.qT[:, j, 0:S], [],
                                   [K(c, "qT", t, j // 4) for t in range(c.NT)])
                elif s == 2:
                    for cc in range(2):
                        feat_chunk(c, W, wk, cc * 128, c.kT[:, cc, 128:128 + S], [], [K(c, "kT")])
                    for t in range(c.NT):
                        b = 4 + t % 4
                        for kc in range(8):
                            MM(bank[b][:T, 0:512], c.uT[:, kc, tcols(c, t)], W[:, kc, :], kc == 0, kc == 7,
                               [wk, K(c, "uT")], [BK(b)])
                        CP("dve", c.vx[:T, 1 + t, :], bank[b][:T, 256:512], [BK(b)], [K(c, "vx")])
                        if last and t == c.NT - 1:
                            CP("act", ost[:T, :], bank[b][:T, :], [BK(b)], ["ost"])
                            if c.sample:
                                DMA("sp", ksn_d[c.seq], ost[:T, 0:256], ["ost"], [])
                                DMA("sp", vsn_d[c.seq], ost[:T, 256:512], ["ost"], [])
                            else:
                                DMA("sp", kp_d, ost[:T, 0:256], ["ost"], [])
                                DMA("sp", vp_d, ost[:T, 256:512], ["ost"], [])
                    for t in range(c.NT):
                        b = 4 + t % 4
                        for kc in range(8):
                            MM(bank[b][:T, 0:16], c.uT[:, kc, tcols(c, t)], wdt[:, kc, :], kc == 0, kc == 7,
                               ["wdt", K(c, "uT")], [BK(b)])
                        TT("dve", c.d1[:T, t, :], bank[b][:T, 0:16], dtb[:T, :], ALU.add, [BK(b), "small"], [K(c, "d1")])
                elif s < 5:
                    for cc in range(4):
                        j = 4 * (s - 3) + cc
                        feat_chunk(c, W, wk, cc * 128, c.szT[:, j, 0:S], [],
                                   [K(c, "szT", t) for t in range(c.NT)], func=AF.Silu)
                else:
                    for cc in range(4):
                        ch = 4 * (s - 5) + cc
                        feat_chunk(c, W, wk, cc * 128, c.rawT[:, ch, 3:3 + S], bigR(c),
                                   [K(c, "rawT", ch)] + bigW(c))
                        yield
                    if last:
                        t = c.NT - 1
                        b = fbank()
                        for kc in range(8):
                            MM(bank[b][:T, 0:512], c.uT[:, kc, tcols(c, t)], W[:, kc, :], kc == 0, kc == 7,
                               [wk, K(c, "uT")], [BK(b)])
                        CP("act", ost[:T, :], bank[b][:T, :], [BK(b)], ["ost"])
                        dst = csn_d[c.seq] if c.sample else cp_d
                        DMA("sp", dst[:, (s - 5) * 512:(s - 4) * 512], ost[T - 3:T, :], ["ost"], [])

        def in_proj(ctxs, st):
            for s in range(5):
                for _ in in_slot(ctxs, st, s):
                    pass

            def xbc_stream():
                for s in range(5, 9):
                    yield from in_slot(ctxs, st, s)

            def attn_stream(c):
                for t in range(c.NT):
                    has_prev = c.sample or not (st == 0 and t == 0)
                    yield from attention(c, t, has_prev)

            fb_mod[0] = 2
            run_streams([(xbc_stream(), 2)] + [(attn_stream(c), 1) for c in ctxs])
            fb_mod[0] = 4
            for c in ctxs:
                if not c.sample:
                    CP("dve", c.kT[:, :, 0:128], c.kT[:, :, c.S:c.S + 128], [K(c, "kT")], [K(c, "kT")])
                    CP("dve", c.vx[:, 0, :], c.vx[:, c.NT, :], [K(c, "vx")], [K(c, "vx")])

        def conv_stage(c):
            S = c.S
            if not c.sample:
                CP("dve", c.rawT[:, :, 0:3], c.chist[:, :, :], [K(c, "chist")] + bigR(c),
                   [K(c, "rawT", ch) for ch in range(16)] + bigW(c))
            for ch in range(16):
                b = fbank()
                for i in range(4):
                    MM(bank[b][:, 0:S], diagW[:, i * 16 + ch, :], c.rawT[:, ch, i:i + S], i == 0, i == 3,
                       [("diagW", i * 16 + ch), K(c, "rawT", ch)] + bigR(c), [BK(b)])
                ACT(c.xbcT[:, ch, 0:S], bank[b][:, 0:S], AF.Silu, [BK(b), "small"] + bigR(c),
                    [K(c, "xbcT", ch)] + bigW(c), bias=convb[:, ch:ch + 1])
            if not c.sample:
                CP("dve", c.chist[:, :, :], c.rawT[:, :, S:S + 3],
                   [K(c, "rawT", ch) for ch in range(16)] + bigR(c), [K(c, "chist")])

        def softplus_stage(c):
            T = c.T
            for t in range(c.NT):
                ACT(d2[:T, :], c.d1[:T, t, :], AF.Exp, [K(c, "d1")], ["d2"])
                ACT(c.dtt[:T, t, :], d2[:T, :], AF.Ln, ["d2"], [K(c, "dtt")], bias=1.0)
                TT("dve", c.da[:T, t, :], c.dtt[:T, t, :], aneg[:T, :], ALU.mult, [K(c, "dtt"), "aneg"], [K(c, "da")])

        def attention(c, t, has_prev):
            T = c.T
            N = 4 * T
            qc = tcols(c, t)
            prevc = slice(t * 128, t * 128 + 128)
            curc = slice(128 + t * T, 128 + t * T + T)
            bO, bD = 6, 7

            def pview(ap_, lo, hi):
                return ap_[0:128, 0:4 * T].rearrange("p (a b) -> p a b", b=T)[:, :, lo:hi]

            for kc in range(2):
                j0 = 4 * kc
                qk = K(c, "qT", t, kc)
                gs = (2 * kc, 2 * kc + 1)
                for g in gs:
                    hh = g % 2
                    ps = slice(hh * 64, hh * 64 + 64)
                    bA, bB = (4, 5) if hh == 0 else (2, 3)
                    rq = c.qT[ps, j0:j0 + 4, qc]
                    if has_prev:
                        MM(bank[bA][:, 0:N], c.kT[ps, kc, prevc], rq, True, True, [K(c, "kT"), qk], [BK(bA)])
                    MM(bank[bB][:T, 0:N], c.kT[ps, kc, curc], rq, True, True, [K(c, "kT"), qk], [BK(bB)])
                for g in gs:
                    hh = g % 2
                    bA, bB = (4, 5) if hh == 0 else (2, 3)
                    pa, pb = PT[hh]
                    if has_prev:
                        if c.sample:
                            ACT(pa[:, 0:N], bank[bA][:, 0:N], AF.Exp, [BK(bA)], [("PT", hh, 0, "a"), ("PT", hh, 0, "b")],
                                scale=0.125)
                        else:
                            ACT(pview(pa, 0, 64), pview(bank[bA], 0, 64), AF.Exp, [BK(bA)], [("PT", hh, 0, "a")], scale=0.125)
                            ACT(pview(pa, 64, 128), pview(bank[bA], 64, 128), AF.Exp, [BK(bA), "mbias"],
                                [("PT", hh, 0, "b")], scale=0.125, bias=mbias[:, 0:1])
                    if c.sample:
                        ACT(pb[:T, 0:N], bank[bB][:T, 0:N], AF.Exp, [BK(bB)], [("PT", hh, 1, "a"), ("PT", hh, 1, "b")],
                            scale=0.125)
                    else:
                        ACT(pview(pb, 0, 64), pview(bank[bB], 0, 64), AF.Exp, [BK(bB), "mbias"], [("PT", hh, 1, "a")],
                            scale=0.125, bias=mbias[:, 1:2])
                        ACT(pview(pb, 64, 128), pview(bank[bB], 64, 128), AF.Exp, [BK(bB)], [("PT", hh, 1, "b")], scale=0.125)
                for g in gs:
                    hh = g % 2
                    ps = slice(hh * 64, hh * 64 + 64)
                    pa, pb = PT[hh]
                    vs = slice(g * 64, g * 64 + 64)
                    ka = [("PT", hh, 0, "a"), ("PT", hh, 0, "b")]
                    kb = [("PT", hh, 1, "a"), ("PT", hh, 1, "b")]
                    if has_prev:
                        MM(bank[bO][ps, 0:N], c.vx[:, t, vs], pa[:, 0:N], True, False, [K(c, "vx")] + ka, [BK(bO)])
                    MM(bank[bO][ps, 0:N], c.vx[:T, t + 1, vs], pb[:T, 0:N], not has_prev, True, [K(c, "vx")] + kb, [BK(bO)])
                    if has_prev:
                        MM(bank[bD][ps, 0:N], ones_b[:, 0:64], pa[:, 0:N], True, False, ["ones_b"] + ka, [BK(bD)])
                    MM(bank[bD][ps, 0:N], ones_b[:T, 0:64], pb[:T, 0:N], not has_prev, True, ["ones_b"] + kb, [BK(bD)])
                rdv = rd[:, 0:N].rearrange("p (a b) -> p a b", b=T)
                TT("dve", rdv, bank[bD][:, 0:N].rearrange("p (a b) -> p a b", b=T),
                   esink[:, j0:j0 + 4].unsqueeze(2).broadcast_to([128, 4, T]), ALU.add, [BK(bD), "esink"], ["rd"])
                ACT(rd[:, 0:N], rd[:, 0:N], AF.Ln, ["rd"], ["rd"])
                ACT(rd[:, 0:N], rd[:, 0:N], AF.Exp, ["rd"], ["rd"], scale=-1.0)
                TT("dve", c.qT[:, j0:j0 + 4, qc], bank[bO][:, 0:N].rearrange("p (a b) -> p a b", b=T), rdv,
                   ALU.mult, [BK(bO), "rd"], [qk])
                yield

        def ssd(c, t):
            T, CL = c.T, c.CL
            tc_ = tcols(c, t)
            xk = [K(c, "xbcT", ch) for ch in range(16)]
            dav = c.da[:T, t, :]
            MM(bank[0][:T, 0:16], Uincl[:T, :T], dav, True, True, ["Uincl", K(c, "da")], [BK(0)])
            MM(bank[0][:T, 16:32], Blk[:T, :T], dav, True, True, ["Blk", K(c, "da")], [BK(0)])
            MM(bank[0][:, 32:48], Sel0[:T, :], dav, True, True, ["Sel0", K(c, "da")], [BK(0)])
            if c.nchunk == 2:
                MM(bank[0][:, 48:64], Sel1[:T, :], dav, True, True, ["Sel1", K(c, "da")], [BK(0)])
            MM(bank[0][0:16, 64:64 + T], dav, Uincl[:T, :T], True, True, ["Uincl", K(c, "da")], [BK(0)])
            b1v = bkbf(1)
            b2v = bkbf(2)
            for j in range(8):
                TR(b1v[:T, j * 128:(j + 1) * 128], c.xbcT[:, j, tc_], ident_b[:, :], xk + ["ident_b"], [BK(1)])
            for g in range(4):
                TR(b2v[:T, g * 128:(g + 1) * 128], c.xbcT[:, 8 + g, tc_], ident_b[:, :], xk + ["ident_b"], [BK(2)])
            CP("dve", cs_sb[:T, :], bank[0][:T, 0:16], [BK(0)], ["cs_sb"])
            TT("dve", dd[:T, :], bank[0][:T, 16:32], cs_sb[:T, :], ALU.subtract, [BK(0), "cs_sb"], ["dd"])
            ACT(decay[:T, :], dd[:T, :], AF.Exp, ["dd"], ["decay"])
            ACT(dB[0][:, :], bank[0][:, 32:48], AF.Exp, [BK(0)], [("dB", 0)])
            if c.nchunk == 2:
                ACT(dB[1][:, :], bank[0][:, 48:64], AF.Exp, [BK(0)], [("dB", 1)])
            CP("act", csT_hi[:, :T], bank[0][0:16, 64:64 + T], [BK(0)], ["csT_hi"])
            TT("dve", csT_lo[:, :T], bank[0][0:16, 64:64 + T], csT_hi[:, :T], ALU.subtract, [BK(0), "csT_hi"], ["csT_lo"])
            TT("dve", dtd[:T, :], c.dtt[:T, t, :], decay[:T, :], ALU.mult, [K(c, "dtt"), "decay"], ["dtd"])
            xv = b1v[:T, 0:1024].rearrange("p (a b) -> p a b", b=64)
            TT("dve", xdd[:T, :].rearrange("p (a b) -> p a b", b=64), xv,
               dtd[:T, :].unsqueeze(2).broadcast_to([T, 16, 64]), ALU.mult, [BK(1), "dtd"], ["xdd"])
            CP("act", Btok[:T, :], b2v[:T, 0:512], [BK(2)], ["Btok"])
            TT("dve", xd[:T, :].rearrange("p (a b) -> p a b", b=64), xv,
               c.dtt[:T, t, :].unsqueeze(2).broadcast_to([T, 16, 64]), ALU.mult, [BK(1), K(c, "dtt")], ["xd"])

            def pe_group(g):
                bc, bs = (4, 5) if g % 2 == 0 else (6, 7)
                MM(bank[bc][:T, 0:T], c.xbcT[:, 8 + g, tc_], c.xbcT[:, 12 + g, tc_], True, True, xk, [BK(bc)])
                for r in range(4):
                    h = 4 * g + r
                    MM(bank[bs][:, r * 128:r * 128 + T], ident_b[0:16, h:h + 1].broadcast_to([16, 128]),
                       csT_hi[0:16, :T], True, False, ["ident_b", "csT_hi"], [BK(bs)])
                    MM(bank[bs][:, r * 128:r * 128 + T], ident_b[0:16, h:h + 1].broadcast_to([16, 128]),
                       csT_lo[0:16, :T], False, True, ["ident_b", "csT_lo"], [BK(bs)])

            def stt_group(g):
                bc, bs = (4, 5) if g % 2 == 0 else (6, 7)
                p = g % 2
                for r in range(4):
                    h = 4 * g + r
                    STT("dve", tmp2[p][:T, r * T:(r + 1) * T], bank[bs][:T, r * 128:r * 128 + T], cs_sb[:T, h:h + 1],
                        maskneg[:T, :T], ALU.subtract, ALU.add, [BK(bs), "cs_sb", "maskneg"], [("tmp", p)])

            pe_group(0)
            pe_group(1)
            hb_idx = [c.hb]
            for ci in range(c.nchunk):
                r0 = ci * 64
                for g in range(4):
                    bS = 2 + g // 2
                    MM(bank[bS][:, (g % 2) * 256:(g % 2) * 256 + 256], Btok[r0:r0 + CL, g * 128:(g + 1) * 128],
                       xdd[r0:r0 + CL, g * 256:(g + 1) * 256], True, True, ["Btok", "xdd"], [BK(bS)])
                TT("dve", t1[:, :].rearrange("p (a b) -> p a b", b=64), c.h[:, :].rearrange("p (a b) -> p a b", b=64),
                   dB[ci][:, :].unsqueeze(2).broadcast_to([128, 16, 64]), ALU.mult, [K(c, "h"), ("dB", ci)], ["yf"])
                TT("dve", c.h[:, 0:512], t1[:, 0:512], bank[2][:, :], ALU.add, ["yf", BK(2)], [K(c, "h")])
                TT("dve", c.h[:, 512:1024], t1[:, 512:1024], bank[3][:, :], ALU.add, ["yf", BK(3)], [K(c, "h")])
                nb = (hb_idx[-1] + 1) % c.nhbf
                CP("act", c.hbf[nb][:, :], c.h[:, :], [K(c, "h")], [K(c, "hbf", nb)])
                hb_idx.append(nb)
                if ci == 0:
                    stt_group(0)
            c.hb = hb_idx[-1]
            if c.nchunk == 1:
                pass
            for g in range(4):
                bc, bs = (4, 5) if g % 2 == 0 else (6, 7)
                p = g % 2
                ACT(lm2[p][:T, 0:4 * T], tmp2[p][:T, 0:4 * T], AF.Exp, [("tmp", p)], [("lm", p)])
                ACT(ecs2[p][:, 0:4 * T].rearrange("p (a b) -> p a b", b=T), bk3(bs, 4, T), AF.Exp, [BK(bs)], [("ecs", p)])
                if g + 1 < 4:
                    stt_group(g + 1)
                TT("dve", MT2[p][:T, 0:4 * T].rearrange("p (a b) -> p a b", b=T),
                   lm2[p][:T, 0:4 * T].rearrange("p (a b) -> p a b", b=T),
                   bank[bc][:T, 0:T].unsqueeze(1).broadcast_to([T, 4, T]), ALU.mult, [("lm", p), BK(bc)], [("MT", p)])
                TT("dve", Cs2[p][:, 0:4 * T].rearrange("p (a b) -> p a b", b=T),
                   ecs2[p][:, 0:4 * T].rearrange("p (a b) -> p a b", b=T),
                   c.xbcT[:, 12 + g, tc_].unsqueeze(1).broadcast_to([128, 4, T]), ALU.mult, [("ecs", p)] + xk, [("Cs", p)])
                for j in (2 * g, 2 * g + 1):
                    by = 2 if j < 4 else 3
                    MM(bank[by][:, (j % 4) * 128:(j % 4) * 128 + T], dskD[:, j, :], c.xbcT[:, j, tc_], True, False,
                       ["dskD"] + xk, [BK(by)])
                    for half in range(2):
                        h = 2 * j + half
                        r = h - 4 * g
                        o = bank[by][half * 64:half * 64 + 64, (j % 4) * 128:(j % 4) * 128 + T]
                        MM(o, xd[:T, h * 64:(h + 1) * 64], MT2[p][:T, r * T:(r + 1) * T], False, False,
                           ["xd", ("MT", p)], [BK(by)])
                        for ci in range(c.nchunk):
                            r0 = ci * 64
                            hbk = hb_idx[ci]
                            MM(bank[by][half * 64:half * 64 + 64, (j % 4) * 128 + r0:(j % 4) * 128 + r0 + CL],
                               c.hbf[hbk][:, h * 64:(h + 1) * 64], Cs2[p][:, r * T + r0:r * T + r0 + CL], False,
                               ci == c.nchunk - 1, [K(c, "hbf", hbk), ("Cs", p)], [BK(by)])
                if g + 2 < 4:
                    pe_group(g + 2)
            yv = yf[:, 0:8 * T].rearrange("p (a b) -> p a b", b=T)
            TT("dve", yv[:, 0:4, :], bk3(2, 4, T), c.szT[:, 0:4, tc_], ALU.mult, [BK(2), K(c, "szT", t)], ["yf"])
            TT("dve", yv[:, 4:8, :], bk3(3, 4, T), c.szT[:, 4:8, tc_], ALU.mult, [BK(3), K(c, "szT", t)], ["yf"])
            ACT(sq[:, 0:8 * T], yf[:, 0:8 * T], AF.Square, ["yf"], ["sq"])
            for g in range(4):
                for jj in range(2):
                    MM(bank[0][:, g * 128:g * 128 + T], ones_b[:, :], sq[:, (2 * g + jj) * T:(2 * g + jj + 1) * T],
                       jj == 0, jj == 1, ["ones_b", "sq"], [BK(0)])
            rvv = rv[:, 0:4 * T].rearrange("p (a b) -> p a b", b=T)
            ACT(rvv, bk3(0, 4, T), AF.Ln, [BK(0)], ["rv"], scale=1.0 / 256.0, bias=EPS)
            ACT(rv[:, 0:4 * T], rv[:, 0:4 * T], AF.Exp, ["rv"], ["rv"], scale=-0.5)
            for j in range(8):
                STT("dve", c.szT[:, j, tc_], yv[:, j, :], normwT[:, j:j + 1], rvv[:, j // 2, :], ALU.mult, ALU.mult,
                    ["yf", "rv", "small"], [K(c, "szT", t)])

        def mix_stage(c, st):
            softplus_stage(c)
            for t in range(c.NT):
                ssd(c, t)

        def acc_banks(c, ch):
            if c.sample:
                return [4 * ((ch + 1) % 2)]
            return [4 * (ch % 2) + t for t in range(4)]

        def out_proj(ctxs):
            for ch in range(2):
                for kh in range(2):
                    W, wk = wnext()
                    for c in ctxs:
                        T = c.T
                        bks = acc_banks(c, ch)
                        for t in range(c.NT):
                            b = bks[t]
                            for kc in range(8):
                                if kh == 0:
                                    lhs = c.qT[:, kc, tcols(c, t)]
                                    rk = K(c, "qT", t, kc // 4)
                                else:
                                    lhs = c.szT[:, kc, tcols(c, t)]
                                    rk = K(c, "szT", t)
                                MM(bank[b][:T, :], lhs, W[:, kc, :], kh == 0 and kc == 0, kh == 1 and kc == 7,
                                   [wk, rk], [BK(b)])
                            if kh == 1:
                                xs_ = c.xh[:T, t, ch * 512:(ch + 1) * 512]
                                TT("dve", xs_, bank[b][:T, :], xs_, ALU.add, [BK(b), K(c, "xh", t)], [K(c, "xh", t)])

        def ffn(ctxs):
            gb = [0]
            for s in range(11):
                W, wk = wnext()
                for c in ctxs:
                    S = c.S
                    for mm in range(2):
                        m = 2 * s + mm
                        gb[0] = (gb[0] + 1) % 4
                        bG, bU = 2 * gb[0], 2 * gb[0] + 1
                        for kc in range(8):
                            MM(bank[bG][:, 0:S], W[:, kc, mm * 128:(mm + 1) * 128], c.uT[:, kc, 0:S], kc == 0, kc == 7,
                               [wk, K(c, "uT")], [BK(bG)])
                        for kc in range(8):
                            MM(bank[bU][:, 0:S], W[:, kc, 256 + mm * 128:256 + (mm + 1) * 128], c.uT[:, kc, 0:S],
                               kc == 0, kc == 7, [wk, K(c, "uT")], [BK(bU)])
                        sg = sgt[m % 2]
                        ACT(sg[:, 0:S], bank[bG][:, 0:S], AF.Silu, [BK(bG)], [("sgt", m % 2)])
                        TT("dve", c.hmid[:, m, 0:S], sg[:, 0:S], bank[bU][:, 0:S], ALU.mult,
                           [("sgt", m % 2), BK(bU)], [K(c, "hmid", m)] + ([K(c, "bigR")] if c.alias else []))
            for ch in range(2):
                for kp in range(3):
                    W, wk = wnext()
                    nk = 6 if kp == 2 else 8
                    for c in ctxs:
                        T = c.T
                        bks = acc_banks(c, ch)
                        for t in range(c.NT):
                            b = bks[t]
                            for kc in range(nk):
                                m = kp * 8 + kc
                                MM(bank[b][:T, :], c.hmid[:, m, tcols(c, t)], W[:, kc, :], kp == 0 and kc == 0,
                                   kp == 2 and kc == nk - 1, [wk, K(c, "hmid", m)] + bigW(c), [BK(b)])
                            if kp == 2:
                                xs_ = c.xh[:T, t, ch * 512:(ch + 1) * 512]
                                TT("dve", xs_, bank[b][:T, :], xs_, ALU.add, [BK(b), K(c, "xh", t)], [K(c, "xh", t)])

        def final_stage(c, st):
            T = c.T
            stats4(c)
            for t in range(c.NT):
                ob, okey = (yo, "yo") if t % 2 == 0 else (yf, "yf")
                STT("dve", ob[:T, :], c.xh[:T, t, :], rstd4[:T, t:t + 1], fg[:T, :], ALU.mult, ALU.mult,
                    [K(c, "xh", t), "rstd4", "fg"], [okey])
                if c.sample:
                    DMA("sp", ys_d[c.seq], ob[:T, :], [okey], [])
                else:
                    r0 = (st * 4 + t) * 128
                    DMA("sp", yp_d[r0:r0 + 128, :], ob[:T, :], [okey], [])

        def state_out(c, dst):
            for half in range(2):
                b = 2 + half
                for q in range(4):
                    blk = half * 4 + q
                    TR(bank[b][:, q * 128:(q + 1) * 128], c.h[:, blk * 128:(blk + 1) * 128], ident_f[:, :],
                       [K(c, "h"), "ident_f"], [BK(b)])
                CP("act" if half else "dve", sst[:, half * 4:half * 4 + 4, :],
                   bank[b][:, :].rearrange("p (a b) -> p a b", b=128), [BK(b)], ["yo"])
            DMA("sp", dst.rearrange("(b p) n -> p b n", p=128), sst[:, :, :], ["yo"], [])

        def sample_init(c, seq):
            c.seq = seq
            DMA("sp", c.xh[:16, 0, :], xs_d[seq], [], [K(c, "xh", 0)])
            DMA("pool", xd[:, 0:256], ck_d[seq], [], ["xd"])
            bv = bkbf(0)
            for kc in range(2):
                TR(bv[:, kc * 128:(kc + 1) * 128], xd[:, kc * 128:(kc + 1) * 128], ident_b[:, :], ["xd", "ident_b"], [BK(0)])
            CP("dve", c.kT[:, :, 0:128], bv[:, 0:256].rearrange("p (a b) -> p a b", b=128), [BK(0)], [K(c, "kT")])
            DMA("pool", c.vx[:, 0, :], cv_d[seq], [], [K(c, "vx")])
            DMA("sp", sconv_sb[:, :], sconv_d[seq], [], ["sconv_sb"])
            for ch in range(16):
                TR(bank[1][:, ch * 4:ch * 4 + 3], sconv_sb[0:3, ch * 128:(ch + 1) * 128], ident_f[0:3, 0:3],
                   ["sconv_sb", "ident_f"], [BK(1)])
            CP("dve", c.rawT[:, :, 0:3], bank[1][:, 0:64].rearrange("p (a b) -> p a b", b=4)[:, :, 0:3], [BK(1)],
               [K(c, "rawT", ch) for ch in range(16)])
            DMA("sp", sst[:, :, :], sssm_d[seq].rearrange("(b p) n -> p b n", p=128), [], ["yo"])
            for half in range(2):
                b = 2 + half
                for q in range(4):
                    blk = half * 4 + q
                    TR(bank[b][:, q * 128:(q + 1) * 128], sst[:, blk, :], ident_f[:, :], ["yo", "ident_f"], [BK(b)])
                CP("dve", c.h[:, half * 512:(half + 1) * 512], bank[b][:, :], [BK(b)], [K(c, "h")])
            c.hb = 0
            CP("act", c.hbf[0][:, :], c.h[:, :], [K(c, "h")], [K(c, "hbf", 0)])

        MS("dve", pc.h[:, :], 0.0, [K(pc, "h")])
        pc.hb = 0
        MS("pool", pc.hbf[0][:, :], 0.0, [K(pc, "hbf", 0)])
        MS("pool", pc.chist[:, :, :], 0.0, [K(pc, "chist")])

        for st in range(NPASS):
            ctxs = [pc]
            import os as _os2
            NOS = bool(_os2.environ.get("DEBUG_NOSAMPLE"))
            if st < 2 and not NOS:
                ctxs.append(sc)
                sample_init(sc, st)
            for t in range(4):
                r0 = (st * 4 + t) * 128
                DMA("sp", pc.xh[:, t, :], xp_d[r0:r0 + 128, :], [], [K(pc, "xh", t)])
            LIM = int(_os2.environ.get("STAGE_LIMIT", "99"))
            if LIM >= 1:
                for c in ctxs:
                    norm_stage(c, g1T)
            if LIM >= 2:
                in_proj(ctxs, st)
            if LIM >= 3:
                for c in ctxs:
                    conv_stage(c)
            if LIM >= 4:
                for c in ctxs:
                    mix_stage(c, st)
            if LIM >= 5:
                out_proj(ctxs)
            if LIM >= 6:
                for c in ctxs:
                    norm_stage(c, g2T)
            if LIM >= 7:
                ffn(ctxs)
            if LIM >= 8:
                for c in ctxs:
                    final_stage(c, st)
            if st < 2 and not NOS and LIM >= 9:
                state_out(sc, ssn_d[st])
        if LIM >= 9:
            state_out(pc, sp_d)
        P.emit_all()
    return nc


_CACHE = {}


def kernel(**inputs):
    f = np.float32
    shared = _host_layout(inputs)
    xp = np.asarray(inputs["x_prompt"], f)
    xs = np.asarray(inputs["x_sample"], f)
    ck = np.asarray(inputs["cache_k"], f)[0].reshape(16, 128, 256)
    cv = np.asarray(inputs["cache_v"], f)[0].reshape(16, 128, 256)
    sconv = np.asarray(inputs["state_conv"], f)[0]
    sssm = np.asarray(inputs["state_ssm"], f)[0].reshape(16, 1024, 128)
    in_maps = []
    for i in range(NCORES):
        m = dict(shared)
        m["xp"] = np.ascontiguousarray(xp[i])
        m["xs"] = np.ascontiguousarray(xs[2 * i:2 * i + 2])
        m["ck"] = np.ascontiguousarray(ck[2 * i:2 * i + 2])
        m["cv"] = np.ascontiguousarray(cv[2 * i:2 * i + 2])
        m["sconv"] = np.ascontiguousarray(sconv[2 * i:2 * i + 2])
        m["sssm"] = np.ascontiguousarray(sssm[2 * i:2 * i + 2])
        in_maps.append(m)
    if "nc" not in _CACHE:
        _CACHE["nc"] = build_program()
    nc = _CACHE["nc"]
    res = run_bass_kernel_spmd(nc, in_maps, core_ids=list(range(NCORES)))
    R = res.results
    y_prompt = np.stack([R[i]["yp"] for i in range(NCORES)]).astype(f)
    y_sample = np.concatenate([R[i]["ys"] for i in range(NCORES)]).astype(f)
    k_prompt = np.stack([R[i]["kp"] for i in range(NCORES)]).reshape(1, 8, 128, 4, 64).astype(f)
    v_prompt = np.stack([R[i]["vp"] for i in range(NCORES)]).reshape(1, 8, 128, 4, 64).astype(f)
    conv_prompt = np.stack([R[i]["cp"] for i in range(NCORES)]).reshape(1, 8, 3, 2048).astype(f)
    ssm_prompt = np.stack([R[i]["spo"] for i in range(NCORES)]).reshape(1, 8, 16, 64, 128).astype(f)
    k_sample = np.concatenate([R[i]["ksn"] for i in range(NCORES)]).reshape(1, 16, 16, 4, 64).astype(f)
    v_sample = np.concatenate([R[i]["vsn"] for i in range(NCORES)]).reshape(1, 16, 16, 4, 64).astype(f)
    conv_sample = np.concatenate([R[i]["csn"] for i in range(NCORES)]).reshape(1, 16, 3, 2048).astype(f)
    ssm_sample = np.concatenate([R[i]["ssn"] for i in range(NCORES)]).reshape(1, 16, 16, 64, 128).astype(f)
    return (y_prompt, y_sample, k_prompt, v_prompt, conv_prompt, ssm_prompt,
            k_sample, v_sample, conv_sample, ssm_sample)
```

```python
import contextlib
import numpy as np
import concourse.bass as bass
import concourse.mybir as mybir
from concourse.bass_utils import run_bass_kernel_spmd

F32 = mybir.dt.float32
BF16 = mybir.dt.bfloat16
AF = mybir.ActivationFunctionType
ALU = mybir.AluOpType

ENGS = ("pe", "act", "dve", "pool", "sp")
NCORES = 8
SEQ = 4096
D = 1024
NPASS = 8
SP = 512
EPS = 1e-6
RING = 3


class _Op:
    __slots__ = ("eng", "idx", "emit", "deps", "is_dma", "dma_sem", "dma_val", "signal", "sigval")

    def __init__(self, eng, idx, emit, is_dma):
        self.eng = eng
        self.idx = idx
        self.emit = emit
        self.deps = []
        self.is_dma = is_dma
        self.dma_sem = None
        self.dma_val = 0
        self.signal = False
        self.sigval = 0


class Prog:
    NDMA_SEMS = 8

    def __init__(self, nc):
        self.nc = nc
        self.ops = {e: [] for e in ENGS}
        self.last_w = {}
        self.readers = {}
        self.dma_rr = {e: 0 for e in ENGS}
        self.dma_last = {}
        self.dma_cnt = {}

    def add(self, eng, emit, reads=(), writes=(), dma=False):
        lst = self.ops[eng]
        op = _Op(eng, len(lst), emit, dma)
        lst.append(op)
        deps = []
        bkeys = [k for k in list(reads) + list(writes) if isinstance(k, tuple) and k and k[0] == "bk"]
        if bkeys:
            reads = [k for k in reads if not (isinstance(k, tuple) and k and k[0] == "bk")]
            writes = [k for k in writes if not (isinstance(k, tuple) and k and k[0] == "bk")]
            for k in set(bkeys):
                w = self.last_w.get(k)
                if w is not None and w.eng != eng:
                    deps.append(w)
                self.last_w[k] = op
        for r in reads:
            w = self.last_w.get(r)
            if w is not None:
                deps.append(w)
        for w_ in writes:
            w = self.last_w.get(w_)
            if w is not None:
                deps.append(w)
            deps.extend(self.readers.get(w_, ()))
        if dma:
            j = self.dma_rr[eng] % self.NDMA_SEMS
            self.dma_rr[eng] += 1
            prev = self.dma_last.get((eng, j))
            if prev is not None:
                deps.append(prev)
            self.dma_last[(eng, j)] = op
            c = self.dma_cnt.get((eng, j), 0) + 1
            self.dma_cnt[(eng, j)] = c
            op.dma_sem = (eng, j)
            op.dma_val = 16 * c
        seen = set()
        best = {}
        for d in deps:
            if d is op or id(d) in seen:
                continue
            seen.add(id(d))
            if d.is_dma:
                op.deps.append(d)
                continue
            b = best.get(d.eng)
            if b is None or d.idx > b.idx:
                best[d.eng] = d
        for d in best.values():
            if d.eng == eng and eng == "pe":
                continue
            d.signal = True
            op.deps.append(d)
        for r in reads:
            lst2 = self.readers.setdefault(r, [])
            if not dma:
                for k_ in range(len(lst2)):
                    if (not lst2[k_].is_dma) and lst2[k_].eng == eng:
                        lst2[k_] = op
                        break
                else:
                    lst2.append(op)
            else:
                lst2.append(op)
        for w_ in writes:
            self.last_w[w_] = op
            self.readers[w_] = []
        return op

    def emit_all(self):
        nc = self.nc
        with contextlib.ExitStack() as es:
            esem = {e: es.enter_context(nc.semaphore("s_" + e)) for e in ENGS}
            dsem = {}
            for e in ENGS:
                for j in range(min(self.NDMA_SEMS, self.dma_rr[e])):
                    dsem[(e, j)] = es.enter_context(nc.semaphore("d_%s%d" % (e, j)))
            for e in ENGS:
                c = 0
                for op in self.ops[e]:
                    if op.signal and not op.is_dma:
                        c += 1
                        op.sigval = c
            block = es.enter_context(nc.Block())

            def run(eng_name, eng):
                waited = {}
                for op in self.ops[eng_name]:
                    for d in op.deps:
                        if d.is_dma:
                            key = ("d",) + d.dma_sem
                            sem = dsem[d.dma_sem]
                            val = d.dma_val
                        else:
                            key = ("e", d.eng)
                            sem = esem[d.eng]
                            val = d.sigval
                        if waited.get(key, 0) >= val:
                            continue
                        waited[key] = val
                        eng.wait_ge(sem, val)
                    ins = op.emit(eng)
                    if op.is_dma:
                        ins.then_inc(dsem[op.dma_sem], 16)
                    elif op.signal:
                        ins.then_inc(esem[eng_name], 1)
                for (e, j), lastop in self.dma_last.items():
                    if e == eng_name and waited.get(("d", e, j), 0) < lastop.dma_val:
                        eng.wait_ge(dsem[(e, j)], lastop.dma_val)

            @block.sync
            def _(eng):
                run("sp", eng)

            @block.tensor
            def _(eng):
                run("pe", eng)

            @block.scalar
            def _(eng):
                run("act", eng)

            @block.vector
            def _(eng):
                run("dve", eng)

            @block.gpsimd
            def _(eng):
                run("pool", eng)


def _head_pairs():
    pairs = []
    for j in range(8):
        if j < 4:
            pairs.append((j, 4 + j))
        else:
            pairs.append((8 + (j - 4), 12 + (j - 4)))
    return pairs


def _slotify(w, cols):
    sub = w[:, cols]
    K = sub.shape[0]
    return np.ascontiguousarray(sub.reshape(K // 128, 128, len(cols)).transpose(1, 0, 2).reshape(128, -1))


def _host_layout(inp):
    f = np.float32
    pairs = _head_pairs()
    w_in = np.asarray(inp["w_in"][0], f)
    qcols = []
    for (a, b) in pairs:
        qcols += list(range(a * 64, a * 64 + 64)) + list(range(b * 64, b * 64 + 64))
    qcols = np.array(qcols)
    slots = [qcols[0:512], qcols[512:1024], np.arange(1024, 1536), np.arange(1536, 2048), np.arange(2048, 2560)]
    for s in range(4):
        slots.append(np.arange(2560 + s * 512, 2560 + (s + 1) * 512))
    win = np.stack([_slotify(w_in, c) for c in slots])
    wdt = _slotify(w_in, np.arange(4608, 4624))
    w_out = np.asarray(inp["w_out"][0], f)
    rowperm = np.concatenate([qcols, np.arange(1024, 2048)])
    wo = w_out[rowperm]
    wout = np.stack([_slotify(wo[kh * 1024:(kh + 1) * 1024], np.arange(ch * 512, (ch + 1) * 512))
                     for ch in range(2) for kh in range(2)])
    wg = np.asarray(inp["w_gate"][0], f)
    wu = np.asarray(inp["w_up"][0], f)
    wgu = np.stack([np.concatenate([_slotify(wg, np.arange(s * 256, (s + 1) * 256)).reshape(128, 8, 256),
                                    _slotify(wu, np.arange(s * 256, (s + 1) * 256)).reshape(128, 8, 256)],
                                   axis=2).reshape(128, 4096) for s in range(11)])
    wd = np.asarray(inp["w_down"][0], f)
    wdp = np.zeros((3072, 1024), f)
    wdp[:2816] = wd
    wdn = np.stack([_slotify(wdp[kp * 1024:(kp + 1) * 1024], np.arange(ch * 512, (ch + 1) * 512))
                    for ch in range(2) for kp in range(3)])

    def colT(v):
        v = np.asarray(v, f)
        return np.ascontiguousarray(v.reshape(-1, 128).T)

    def rep(v):
        v = np.asarray(v, f)
        return np.ascontiguousarray(np.broadcast_to(v[None, :], (128, v.shape[0])))

    cw = np.asarray(inp["conv_w"][0], f)
    convw = np.ascontiguousarray(cw.reshape(4, 16, 128).transpose(2, 0, 1).reshape(128, 64))
    dsk = np.asarray(inp["d_skip"][0], f)
    sinks = np.asarray(inp["sinks"][0], f)
    dskT = np.zeros((128, 8), f)
    sinkT = np.zeros((128, 8), f)
    for j in range(8):
        dskT[:64, j] = dsk[2 * j]
        dskT[64:, j] = dsk[2 * j + 1]
        sinkT[:64, j] = sinks[pairs[j][0]]
        sinkT[64:, j] = sinks[pairs[j][1]]
    small = np.concatenate([
        colT(inp["ln1_g"][0]), colT(inp["ln2_g"][0]), convw, colT(inp["conv_b"][0]),
        rep(inp["dt_bias"][0]), rep(inp["a_log"][0]), dskT, colT(inp["ssm_norm_w"][0]), sinkT], axis=1)
    shared = {"win": win, "wdt": wdt, "wout": wout, "wgu": wgu, "wdn": wdn,
              "small": np.ascontiguousarray(small), "fg": rep(inp["final_g"])}
    return shared


class Ctx:
    pass


def build_program():
    nc = bass.Bass("TRN2", target_bir_lowering=False)

    def din(name, shape):
        return nc.dram_tensor(name, shape, F32, kind="ExternalInput").ap()

    def dout(name, shape):
        return nc.dram_tensor(name, shape, F32, kind="ExternalOutput").ap()

    xp_d = din("xp", [SEQ, D])
    xs_d = din("xs", [2, 16, D])
    ck_d = din("ck", [2, 128, 256])
    cv_d = din("cv", [2, 128, 256])
    sconv_d = din("sconv", [2, 3, 2048])
    sssm_d = din("sssm", [2, 1024, 128])
    win_d = din("win", [9, 128, 4096])
    wdt_d = din("wdt", [128, 128])
    wout_d = din("wout", [4, 128, 4096])
    wgu_d = din("wgu", [11, 128, 4096])
    wdn_d = din("wdn", [6, 128, 4096])
    small_d = din("small", [128, 152])
    fg_d = din("fg", [128, 1024])
    yp_d = dout("yp", [SEQ, D])
    ys_d = dout("ys", [2, 16, D])
    kp_d = dout("kp", [128, 256])
    vp_d = dout("vp", [128, 256])
    cp_d = dout("cp", [3, 2048])
    sp_d = dout("spo", [1024, 128])
    ksn_d = dout("ksn", [2, 16, 256])
    vsn_d = dout("vsn", [2, 16, 256])
    csn_d = dout("csn", [2, 3, 2048])
    ssn_d = dout("ssn", [2, 1024, 128])

    es = contextlib.ExitStack()
    with es:
        def sb(name, shape, dt=F32):
            return es.enter_context(nc.sbuf_tensor("sb_" + name, shape, dt))

        P = Prog(nc)
        bank = [es.enter_context(nc.psum_tensor("bk%d" % i, [128, 512], F32)) for i in range(8)]

        def BK(i):
            return ("bk", i)

        def bk3(i, n, T, rows=128, inner=128):
            return bank[i][:, 0:n * inner].rearrange("p (a b) -> p a b", b=inner)[0:rows, :, 0:T]

        def bkbf(i):
            return bank[i][:, :].bitcast(BF16)

        def MM(out, lhsT, rhs, start, stop, reads, writes):
            P.add("pe", lambda e: e.matmul(out, lhsT=lhsT, rhs=rhs, start=start, stop=stop), reads, writes)

        def TR(out, in_, ident, reads, writes):
            P.add("pe", lambda e: e.transpose(out=out, in_=in_, identity=ident), reads, writes)

        def ACT(out, in_, func, reads, writes, bias=None, scale=None, accum=None):
            kw = {}
            if bias is not None:
                kw["bias"] = bias
            if scale is not None:
                kw["scale"] = scale
            if accum is not None:
                kw["accum_out"] = accum
            P.add("act", lambda e: e.activation(out=out, in_=in_, func=func, **kw), reads, writes)

        def TT(eng, out, in0, in1, op, reads, writes):
            P.add(eng, lambda e: e.tensor_tensor(out=out, in0=in0, in1=in1, op=op), reads, writes)

        def TS(eng, out, in0, s1, s2, op0, op1, reads, writes):
            if s2 is None:
                P.add(eng, lambda e: e.tensor_scalar(out=out, in0=in0, scalar1=s1, scalar2=None, op0=op0), reads, writes)
            else:
                P.add(eng, lambda e: e.tensor_scalar(out=out, in0=in0, scalar1=s1, scalar2=s2, op0=op0, op1=op1),
                      reads, writes)

        def STT(eng, out, in0, scalar, in1, op0, op1, reads, writes):
            P.add(eng, lambda e: e.scalar_tensor_tensor(out=out, in0=in0, scalar=scalar, in1=in1, op0=op0, op1=op1),
                  reads, writes)

        def CP(eng, out, in_, reads, writes):
            if eng == "act":
                P.add("act", lambda e: e.copy(out=out, in_=in_), reads, writes)
            else:
                P.add(eng, lambda e: e.tensor_copy(out=out, in_=in_), reads, writes)

        def MS(eng, ap, val, writes, reads=()):
            P.add(eng, lambda e: e.memset(ap, val), reads, writes)

        def DMA(eng, out, in_, reads, writes):
            P.add(eng, lambda e: e.dma_start(out=out, in_=in_), reads, writes, dma=True)

        evac_rr = [0]

        def EV(out, in_, reads, writes):
            evac_rr[0] += 1
            CP("act" if evac_rr[0] % 2 else "dve", out, in_, reads, writes)

        small = sb("small", [128, 152])
        fg = sb("fg", [128, 1024])
        ident_f = sb("ident_f", [128, 128])
        ident_b = sb("ident_b", [128, 128], BF16)
        ones_b = sb("ones_b", [128, 128], BF16)
        Uincl = sb("Uincl", [128, 128])
        Blk = sb("Blk", [128, 128])
        Sel0 = sb("Sel0", [128, 128])
        Sel1 = sb("Sel1", [128, 128])
        maskneg = sb("maskneg", [128, 128])
        negh = sb("negh", [128, 4])
        dskD = sb("dskD", [128, 8, 128], BF16)
        mbias = sb("mbias", [128, 2])
        aneg = sb("aneg", [128, 16])
        esink = sb("esink", [128, 8])
        diagW = sb("diagW", [128, 64, 128], BF16)
        wdt = sb("wdt", [128, 8, 16], BF16)
        g1T = small[:, 0:8]
        g2T = small[:, 8:16]
        convb = small[:, 80:96]
        dtb = small[:, 96:112]
        dskT = small[:, 128:136]
        normwT = small[:, 136:144]

        DMA("sp", small[:, :], small_d, [], ["small"])
        DMA("sp", fg[:, :], fg_d, [], ["fg"])
        DMA("pool", wdt[:, :, :], wdt_d.rearrange("p (k c) -> p k c", c=16), [], ["wdt"])
        MS("pool", ident_f[:, :], 1.0, ["ident_f"])
        P.add("pool", lambda e: e.affine_select(out=ident_f[:, :], in_=ident_f[:, :], pattern=[[-1, 128]],
                                                 compare_op=ALU.is_equal, fill=0.0, base=0, channel_multiplier=1),
              ["ident_f"], ["ident_f"])
        CP("dve", ident_b[:, :], ident_f[:, :], ["ident_f"], ["ident_b"])
        MS("dve", ones_b[:, :], 1.0, ["ones_b"])
        MS("pool", Uincl[:, :], 1.0, ["Uincl"])
        P.add("pool", lambda e: e.affine_select(out=Uincl[:, :], in_=Uincl[:, :], pattern=[[1, 128]],
                                                 compare_op=ALU.is_ge, fill=0.0, base=0, channel_multiplier=-1),
              ["Uincl"], ["Uincl"])
        MS("pool", Uincl[0:64, 64:128], 0.0, ["Uincl"], ["Uincl"])
        MS("dve", Blk[:, :], 0.0, ["Blk"])
        MS("dve", Blk[0:64, 0:64], 1.0, ["Blk"], ["Blk"])
        MS("dve", Blk[64:128, 64:128], 1.0, ["Blk"], ["Blk"])
        MS("dve", Sel0[:, :], 0.0, ["Sel0"])
        MS("dve", Sel0[0:64, :], 1.0, ["Sel0"], ["Sel0"])
        MS("dve", Sel1[:, :], 0.0, ["Sel1"])
        MS("dve", Sel1[64:128, :], 1.0, ["Sel1"], ["Sel1"])
        TS("dve", maskneg[:, :], Uincl[:, :], 1.0, 30000.0, ALU.subtract, ALU.mult, ["Uincl"], ["maskneg"])
        MS("pool", negh[:, :], -0.5, ["negh"])
        for j in range(8):
            TS("dve", dskD[:, j, :], ident_f[:, :], small[:, 128 + j:129 + j], None, ALU.mult, None,
               ["ident_f", "small"], ["dskD"])
        MS("dve", mbias[:, :], 0.0, ["mbias"])
        MS("dve", mbias[0:64, 0:1], -30000.0, ["mbias"], ["mbias"])
        MS("dve", mbias[64:128, 1:2], -30000.0, ["mbias"], ["mbias"])
        ACT(aneg[:, :], small[:, 112:128], AF.Exp, ["small"], ["aneg"])
        TS("dve", aneg[:, :], aneg[:, :], -1.0, None, ALU.mult, None, ["aneg"], ["aneg"])
        ACT(esink[:, :], small[:, 144:152], AF.Exp, ["small"], ["esink"])
        for ic in range(64):
            TS("dve" if ic % 2 else "pool", diagW[:, ic, :], ident_f[:, :], small[:, 16 + ic:17 + ic], None, ALU.mult, None,
               ["ident_f", "small"], [("diagW", ic)])

        un2 = [sb("un%d" % i, [128, 1024], BF16) for i in range(2)]
        ss4 = sb("ss4", [128, 4])
        vv4 = sb("vv4", [128, 4])
        rstd4 = sb("rstd4", [128, 4])
        PT = [[sb("PT%d%d" % (a, b), [128, 512], BF16) for b in range(2)] for a in range(2)]
        rd = sb("rd", [128, 512])
        ost = sb("ost", [128, 512])
        yo = sb("yo", [128, 1024])
        sgt = [sb("sgt%d" % i, [128, 512], BF16) for i in range(2)]
        d2 = sb("d2", [128, 16])
        cs_sb = sb("cs_sb", [128, 16])
        dd = sb("dd", [128, 16])
        decay = sb("decay", [128, 16])
        dtd = sb("dtd", [128, 16])
        dB = [sb("dB%d" % i, [128, 16]) for i in range(2)]
        csT_hi = sb("csT_hi", [16, 128], BF16)
        csT_lo = sb("csT_lo", [16, 128], BF16)
        xd = sb("xd", [128, 1024], BF16)
        xdd = sb("xdd", [128, 1024], BF16)
        Btok = sb("Btok", [128, 512], BF16)
        tmp2 = [sb("tmp%d" % i, [128, 512]) for i in range(2)]
        lm2 = [sb("lm%d" % i, [128, 512], BF16) for i in range(2)]
        MT2 = [sb("MT%d" % i, [128, 512], BF16) for i in range(2)]
        ecs2 = [sb("ecs%d" % i, [128, 512], BF16) for i in range(2)]
        Cs2 = [sb("Cs%d" % i, [128, 512], BF16) for i in range(2)]
        yf = sb("yf", [128, 1024])
        t1 = yf
        sq = sb("sq", [128, 1024], BF16)
        rv = sb("rv", [128, 512])

        ring = [sb("ring%d" % i, [128, 8, 512], BF16) for i in range(RING)]

        def mkctx(name, S, NT, T, nchunk, CL, sample):
            c = Ctx()
            c.name, c.S, c.NT, c.T, c.nchunk, c.CL, c.sample = name, S, NT, T, nchunk, CL, sample
            c.HK = 128
            c.xh = sb(name + "xh", [128, NT, 1024])
            c.uT = sb(name + "uT", [128, 8, S], BF16)
            c.qT = sb(name + "qT", [128, 8, S], BF16)
            c.kT = sb(name + "kT", [128, 2, 128 + S], BF16)
            c.vx = sb(name + "vx", [128, NT + 1, 256], BF16)
            c.szT = sb(name + "szT", [128, 8, S], BF16)
            if sample:
                c.rawT = sb(name + "rawT", [128, 16, S + 3], BF16)
                c.xbcT = sb(name + "xbcT", [128, 16, S], BF16)
                c.hmid = sb(name + "hmid", [128, 22, S], BF16)
            else:
                big = sb(name + "big", [128, 16 * (S + 3) + 16 * S], BF16)
                c.rawT = big[:, 0:16 * (S + 3)].rearrange("p (c n) -> p c n", n=S + 3)
                c.xbcT = big[:, 16 * (S + 3):].rearrange("p (c n) -> p c n", n=S)
                c.hmid = big[:, 0:22 * S].rearrange("p (c n) -> p c n", n=S)
            c.alias = not sample
            c.chist = sb(name + "chist", [128, 16, 3], BF16)
            c.h = sb(name + "h", [128, 1024])
            c.d1 = sb(name + "d1", [128, NT, 16])
            c.dtt = sb(name + "dtt", [128, NT, 16])
            c.da = sb(name + "da", [128, NT, 16])
            c.nhbf = 2 if sample else 3
            c.hbf = [sb(name + "hbf%d" % i, [128, 1024], BF16) for i in range(c.nhbf)]
            c.hb = 0
            return c

        pc = mkctx("p", SP, 4, 128, 2, 64, False)
        sc = mkctx("s", 16, 1, 16, 1, 16, True)
        sconv_sb = sb("sconv_sb", [3, 2048])
        sst = yo[:, :].rearrange("p (a b) -> p a b", b=128)

        def K(c, *a):
            return (c.name,) + a

        def bigR(c):
            return [K(c, "bigR")] if c.alias else []

        def bigW(c):
            return [K(c, "bigW")] if c.alias else []

        wseq = []
        for st in range(NPASS):
            for s in range(9):
                wseq.append((win_d[s], 8))
            for s in range(4):
                wseq.append((wout_d[s], 8))
            for s in range(11):
                wseq.append((wgu_d[s], 8))
            for s in range(6):
                wseq.append((wdn_d[s], 6 if s % 3 == 2 else 8))
        wstate = {"issued": 0, "cur": -1}

        def wprefetch(upto):
            while wstate["issued"] <= min(upto, len(wseq) - 1):
                n = wstate["issued"]
                src, nk = wseq[n]
                r = n % RING
                DMA("pool", ring[r][:, 0:nk, :], src[:, 0:nk * 512].rearrange("p (k c) -> p k c", c=512),
                    [], [("ring", r)])
                wstate["issued"] += 1

        def wnext():
            wstate["cur"] += 1
            n = wstate["cur"]
            wprefetch(n + RING - 1)
            return ring[n % RING], ("ring", n % RING)

        def tcols(c, t):
            return slice(t * c.T, (t + 1) * c.T)

        def stats4(c):
            T = c.T
            sk = [("ss4", t) for t in range(c.NT)]
            MS("dve", ss4[:T, 0:c.NT], 0.0, sk)
            for t in range(c.NT):
                ACT(un2[t % 2][:T, :], c.xh[:T, t, :], AF.Square, [K(c, "xh", t)], [("un", t % 2), ("ss4", t)],
                    accum=ss4[:T, t:t + 1])
            TS("dve", vv4[:T, 0:c.NT], ss4[:T, 0:c.NT], 1.0 / D, EPS, ALU.mult, ALU.add, sk, ["vv4"])
            TT("pool", rstd4[:T, 0:c.NT], vv4[:T, 0:c.NT], negh[:T, 0:c.NT], ALU.pow, ["vv4", "negh"], ["rstd4"])

        def norm_stage(c, gT):
            T = c.T
            stats4(c)

            def scale(t):
                TS("dve", un2[t % 2][:T, :], c.xh[:T, t, :], rstd4[:T, t:t + 1], None, ALU.mult, None,
                   [K(c, "xh", t), "rstd4"], [("un", t % 2)])
                bv = bkbf(t % 2)
                for kc in range(8):
                    TR(bv[:, kc * 128:kc * 128 + T], un2[t % 2][:T, kc * 128:(kc + 1) * 128], ident_b[:T, :T],
                       [("un", t % 2), "ident_b"], [BK(t % 2)])

            def evac(t):
                bv = bkbf(t % 2)
                TT("dve", c.uT[:, :, tcols(c, t)], bv[:, 0:1024].rearrange("p (a b) -> p a b", b=128)[:, :, 0:T],
                   gT.unsqueeze(2).broadcast_to([128, 8, T]), ALU.mult, [BK(t % 2), "small"], [K(c, "uT")])

            scale(0)
            for t in range(1, c.NT):
                scale(t)
                evac(t - 1)
            evac(c.NT - 1)

        fb = [0]

        fb_mod = [4]

        def fbank():
            fb[0] = (fb[0] + 1) % fb_mod[0]
            return fb[0]

        def feat_chunk(c, W, wk, col0, dst, dkeys_r, dkeys_w, func=None, bias=None):
            b = fbank()
            S = c.S
            for kc in range(8):
                MM(bank[b][:, 0:S], W[:, kc, col0:col0 + 128], c.uT[:, kc, 0:S], kc == 0, kc == 7,
                   [wk, K(c, "uT")], [BK(b)])
            if func is None:
                EV(dst, bank[b][:, 0:S], [BK(b)] + dkeys_r, dkeys_w)
            else:
                ACT(dst, bank[b][:, 0:S], func, [BK(b)] + dkeys_r, dkeys_w, bias=bias)

        def run_streams(items):
            items = list(items)
            while items:
                for item in list(items):
                    g_, rep = item
                    for _ in range(rep):
                        try:
                            next(g_)
                        except StopIteration:
                            items.remove(item)
                            break

        def in_slot(ctxs, st, s):
            W, wk = wnext()
            for c in ctxs:
                S, T = c.S, c.T
                last = c.sample or st == NPASS - 1
                if s < 2:
                    for cc in range(4):
                        j = 4 * s + cc
                        feat_chunk(c, W, wk, cc * 128, c.qT[:, j, 0:S], [],
                                   [K(c, "qT", t, j // 4) for t in range(c.NT)])
                elif s == 2:
                    for cc in range(2):
                        feat_chunk(c, W, wk, cc * 128, c.kT[:, cc, 128:128 + S], [], [K(c, "kT")])
                    for t in range(c.NT):
                        b = 4 + t % 4
                        for kc in range(8):
                            MM(bank[b][:T, 0:512], c.uT[:, kc, tcols(c, t)], W[:, kc, :], kc == 0, kc == 7,
                               [wk, K(c, "uT")], [BK(b)])
                        CP("dve", c.vx[:T, 1 + t, :], bank[b][:T, 256:512], [BK(b)], [K(c, "vx")])
                        if last and t == c.NT - 1:
                            CP("act", ost[:T, :], bank[b][:T, :], [BK(b)], ["ost"])
                            if c.sample:
                                DMA("sp", ksn_d[c.seq], ost[:T, 0:256], ["ost"], [])
                                DMA("sp", vsn_d[c.seq], ost[:T, 256:512], ["ost"], [])
                            else:
                                DMA("sp", kp_d, ost[:T, 0:256], ["ost"], [])
                                DMA("sp", vp_d, ost[:T, 256:512], ["ost"], [])
                    for t in range(c.NT):
                        b = 4 + t % 4
                        for kc in range(8):
                            MM(bank[b][:T, 0:16], c.uT[:, kc, tcols(c, t)], wdt[:, kc, :], kc == 0, kc == 7,
                               ["wdt", K(c, "uT")], [BK(b)])
                        TT("dve", c.d1[:T, t, :], bank[b][:T, 0:16], dtb[:T, :], ALU.add, [BK(b), "small"], [K(c, "d1")])
                elif s < 5:
                    for cc in range(4):
                        j = 4 * (s - 3) + cc
                        feat_chunk(c, W, wk, cc * 128, c.szT[:, j, 0:S], [],
                                   [K(c, "szT", t) for t in range(c.NT)], func=AF.Silu)
                else:
                    for cc in range(4):
                        ch = 4 * (s - 5) + cc
                        feat_chunk(c, W, wk, cc * 128, c.rawT[:, ch, 3:3 + S], bigR(c),
                                   [K(c, "rawT", ch)] + bigW(c))
                        yield
                    if last:
                        t = c.NT - 1
                        b = fbank()
                        for kc in range(8):
                            MM(bank[b][:T, 0:512], c.uT[:, kc, tcols(c, t)], W[:, kc, :], kc == 0, kc == 7,
                               [wk, K(c, "uT")], [BK(b)])
                        CP("act", ost[:T, :], bank[b][:T, :], [BK(b)], ["ost"])
                        dst = csn_d[c.seq] if c.sample else cp_d
                        DMA("sp", dst[:, (s - 5) * 512:(s - 4) * 512], ost[T - 3:T, :], ["ost"], [])

        def in_proj(ctxs, st):
            for s in range(5):
                for _ in in_slot(ctxs, st, s):
                    pass

            def xbc_stream():
                for s in range(5, 9):
                    yield from in_slot(ctxs, st, s)

            def attn_stream(c):
                for t in range(c.NT):
                    has_prev = c.sample or not (st == 0 and t == 0)
                    yield from attention(c, t, has_prev)

            fb_mod[0] = 2
            run_streams([(attn_stream(c), 1) for c in ctxs] + [(xbc_stream(), 2)])
            fb_mod[0] = 4
            for c in ctxs:
                if not c.sample:
                    CP("dve", c.kT[:, :, 0:128], c.kT[:, :, c.S:c.S + 128], [K(c, "kT")], [K(c, "kT")])
                    CP("dve", c.vx[:, 0, :], c.vx[:, c.NT, :], [K(c, "vx")], [K(c, "vx")])

        def conv_stage(c):
            S = c.S
            if not c.sample:
                CP("dve", c.rawT[:, :, 0:3], c.chist[:, :, :], [K(c, "chist")] + bigR(c),
                   [K(c, "rawT", ch) for ch in range(16)] + bigW(c))
            for ch in range(16):
                b = fbank()
                for i in range(4):
                    MM(bank[b][:, 0:S], diagW[:, i * 16 + ch, :], c.rawT[:, ch, i:i + S], i == 0, i == 3,
                       [("diagW", i * 16 + ch), K(c, "rawT", ch)] + bigR(c), [BK(b)])
                ACT(c.xbcT[:, ch, 0:S], bank[b][:, 0:S], AF.Silu, [BK(b), "small"] + bigR(c),
                    [K(c, "xbcT", ch)] + bigW(c), bias=convb[:, ch:ch + 1])
            if not c.sample:
                CP("dve", c.chist[:, :, :], c.rawT[:, :, S:S + 3],
                   [K(c, "rawT", ch) for ch in range(16)] + bigR(c), [K(c, "chist")])

        def softplus_stage(c):
            T = c.T
            for t in range(c.NT):
                ACT(d2[:T, :], c.d1[:T, t, :], AF.Exp, [K(c, "d1")], ["d2"])
                ACT(c.dtt[:T, t, :], d2[:T, :], AF.Ln, ["d2"], [K(c, "dtt")], bias=1.0)
                TT("dve", c.da[:T, t, :], c.dtt[:T, t, :], aneg[:T, :], ALU.mult, [K(c, "dtt"), "aneg"], [K(c, "da")])

        def attention(c, t, has_prev):
            T = c.T
            N = 4 * T
            qc = tcols(c, t)
            prevc = slice(t * 128, t * 128 + 128)
            curc = slice(128 + t * T, 128 + t * T + T)
            bO, bD = 6, 7

            def pview(ap_, lo, hi):
                return ap_[0:128, 0:4 * T].rearrange("p (a b) -> p a b", b=T)[:, :, lo:hi]

            for kc in range(2):
                j0 = 4 * kc
                qk = K(c, "qT", t, kc)
                gs = (2 * kc, 2 * kc + 1)
                for g in gs:
                    hh = g % 2
                    ps = slice(hh * 64, hh * 64 + 64)
                    bA, bB = (4, 5) if hh == 0 else (2, 3)
                    rq = c.qT[ps, j0:j0 + 4, qc]
                    if has_prev:
                        MM(bank[bA][:, 0:N], c.kT[ps, kc, prevc], rq, True, True, [K(c, "kT"), qk], [BK(bA)])
                    MM(bank[bB][:T, 0:N], c.kT[ps, kc, curc], rq, True, True, [K(c, "kT"), qk], [BK(bB)])
                for g in gs:
                    hh = g % 2
                    bA, bB = (4, 5) if hh == 0 else (2, 3)
                    pa, pb = PT[hh]
                    if has_prev:
                        if c.sample:
                            ACT(pa[:, 0:N], bank[bA][:, 0:N], AF.Exp, [BK(bA)], [("PT", hh, 0, "a"), ("PT", hh, 0, "b")],
                                scale=0.125)
                        else:
                            ACT(pview(pa, 0, 64), pview(bank[bA], 0, 64), AF.Exp, [BK(bA)], [("PT", hh, 0, "a")], scale=0.125)
                            ACT(pview(pa, 64, 128), pview(bank[bA], 64, 128), AF.Exp, [BK(bA), "mbias"],
                                [("PT", hh, 0, "b")], scale=0.125, bias=mbias[:, 0:1])
                    if c.sample:
                        ACT(pb[:T, 0:N], bank[bB][:T, 0:N], AF.Exp, [BK(bB)], [("PT", hh, 1, "a"), ("PT", hh, 1, "b")],
                            scale=0.125)
                    else:
                        ACT(pview(pb, 0, 64), pview(bank[bB], 0, 64), AF.Exp, [BK(bB), "mbias"], [("PT", hh, 1, "a")],
                            scale=0.125, bias=mbias[:, 1:2])
                        ACT(pview(pb, 64, 128), pview(bank[bB], 64, 128), AF.Exp, [BK(bB)], [("PT", hh, 1, "b")], scale=0.125)
                for g in gs:
                    hh = g % 2
                    ps = slice(hh * 64, hh * 64 + 64)
                    pa, pb = PT[hh]
                    vs = slice(g * 64, g * 64 + 64)
                    ka = [("PT", hh, 0, "a"), ("PT", hh, 0, "b")]
                    kb = [("PT", hh, 1, "a"), ("PT", hh, 1, "b")]
                    if has_prev:
                        MM(bank[bO][ps, 0:N], c.vx[:, t, vs], pa[:, 0:N], True, False, [K(c, "vx")] + ka, [BK(bO)])
                    MM(bank[bO][ps, 0:N], c.vx[:T, t + 1, vs], pb[:T, 0:N], not has_prev, True, [K(c, "vx")] + kb, [BK(bO)])
                    if has_prev:
                        MM(bank[bD][ps, 0:N], ones_b[:, 0:64], pa[:, 0:N], True, False, ["ones_b"] + ka, [BK(bD)])
                    MM(bank[bD][ps, 0:N], ones_b[:T, 0:64], pb[:T, 0:N], not has_prev, True, ["ones_b"] + kb, [BK(bD)])
                rdv = rd[:, 0:N].rearrange("p (a b) -> p a b", b=T)
                TT("dve", rdv, bank[bD][:, 0:N].rearrange("p (a b) -> p a b", b=T),
                   esink[:, j0:j0 + 4].unsqueeze(2).broadcast_to([128, 4, T]), ALU.add, [BK(bD), "esink"], ["rd"])
                ACT(rd[:, 0:N], rd[:, 0:N], AF.Ln, ["rd"], ["rd"])
                ACT(rd[:, 0:N], rd[:, 0:N], AF.Exp, ["rd"], ["rd"], scale=-1.0)
                TT("dve", c.qT[:, j0:j0 + 4, qc], bank[bO][:, 0:N].rearrange("p (a b) -> p a b", b=T), rdv,
                   ALU.mult, [BK(bO), "rd"], [qk])
                yield

        def ssd(c, t):
            T, CL = c.T, c.CL
            tc_ = tcols(c, t)
            xk = [K(c, "xbcT", ch) for ch in range(16)]
            dav = c.da[:T, t, :]
            MM(bank[0][:T, 0:16], Uincl[:T, :T], dav, True, True, ["Uincl", K(c, "da")], [BK(0)])
            MM(bank[0][:T, 16:32], Blk[:T, :T], dav, True, True, ["Blk", K(c, "da")], [BK(0)])
            MM(bank[0][:, 32:48], Sel0[:T, :], dav, True, True, ["Sel0", K(c, "da")], [BK(0)])
            if c.nchunk == 2:
                MM(bank[0][:, 48:64], Sel1[:T, :], dav, True, True, ["Sel1", K(c, "da")], [BK(0)])
            MM(bank[0][0:16, 64:64 + T], dav, Uincl[:T, :T], True, True, ["Uincl", K(c, "da")], [BK(0)])
            b1v = bkbf(1)
            b2v = bkbf(2)
            for j in range(8):
                TR(b1v[:T, j * 128:(j + 1) * 128], c.xbcT[:, j, tc_], ident_b[:, :], xk + ["ident_b"], [BK(1)])
            for g in range(4):
                TR(b2v[:T, g * 128:(g + 1) * 128], c.xbcT[:, 8 + g, tc_], ident_b[:, :], xk + ["ident_b"], [BK(2)])
            CP("dve", cs_sb[:T, :], bank[0][:T, 0:16], [BK(0)], ["cs_sb"])
            TT("dve", dd[:T, :], bank[0][:T, 16:32], cs_sb[:T, :], ALU.subtract, [BK(0), "cs_sb"], ["dd"])
            ACT(decay[:T, :], dd[:T, :], AF.Exp, ["dd"], ["decay"])
            ACT(dB[0][:, :], bank[0][:, 32:48], AF.Exp, [BK(0)], [("dB", 0)])
            if c.nchunk == 2:
                ACT(dB[1][:, :], bank[0][:, 48:64], AF.Exp, [BK(0)], [("dB", 1)])
            CP("act", csT_hi[:, :T], bank[0][0:16, 64:64 + T], [BK(0)], ["csT_hi"])
            TT("dve", csT_lo[:, :T], bank[0][0:16, 64:64 + T], csT_hi[:, :T], ALU.subtract, [BK(0), "csT_hi"], ["csT_lo"])
            TT("dve", dtd[:T, :], c.dtt[:T, t, :], decay[:T, :], ALU.mult, [K(c, "dtt"), "decay"], ["dtd"])
            xv = b1v[:T, 0:1024].rearrange("p (a b) -> p a b", b=64)
            TT("dve", xdd[:T, :].rearrange("p (a b) -> p a b", b=64), xv,
               dtd[:T, :].unsqueeze(2).broadcast_to([T, 16, 64]), ALU.mult, [BK(1), "dtd"], ["xdd"])
            CP("act", Btok[:T, :], b2v[:T, 0:512], [BK(2)], ["Btok"])
            TT("dve", xd[:T, :].rearrange("p (a b) -> p a b", b=64), xv,
               c.dtt[:T, t, :].unsqueeze(2).broadcast_to([T, 16, 64]), ALU.mult, [BK(1), K(c, "dtt")], ["xd"])

            def pe_group(g):
                bc, bs = (4, 5) if g % 2 == 0 else (6, 7)
                MM(bank[bc][:T, 0:T], c.xbcT[:, 8 + g, tc_], c.xbcT[:, 12 + g, tc_], True, True, xk, [BK(bc)])
                for r in range(4):
                    h = 4 * g + r
                    MM(bank[bs][:, r * 128:r * 128 + T], ident_b[0:16, h:h + 1].broadcast_to([16, 128]),
                       csT_hi[0:16, :T], True, False, ["ident_b", "csT_hi"], [BK(bs)])
                    MM(bank[bs][:, r * 128:r * 128 + T], ident_b[0:16, h:h + 1].broadcast_to([16, 128]),
                       csT_lo[0:16, :T], False, True, ["ident_b", "csT_lo"], [BK(bs)])

            def stt_group(g):
                bc, bs = (4, 5) if g % 2 == 0 else (6, 7)
                p = g % 2
                for r in range(4):
                    h = 4 * g + r
                    STT("dve", tmp2[p][:T, r * T:(r + 1) * T], bank[bs][:T, r * 128:r * 128 + T], cs_sb[:T, h:h + 1],
                        maskneg[:T, :T], ALU.subtract, ALU.add, [BK(bs), "cs_sb", "maskneg"], [("tmp", p)])

            pe_group(0)
            pe_group(1)
            hb_idx = [c.hb]
            for ci in range(c.nchunk):
                r0 = ci * 64
                for g in range(4):
                    bS = 2 + g // 2
                    MM(bank[bS][:, (g % 2) * 256:(g % 2) * 256 + 256], Btok[r0:r0 + CL, g * 128:(g + 1) * 128],
                       xdd[r0:r0 + CL, g * 256:(g + 1) * 256], True, True, ["Btok", "xdd"], [BK(bS)])
                TT("dve", t1[:, :].rearrange("p (a b) -> p a b", b=64), c.h[:, :].rearrange("p (a b) -> p a b", b=64),
                   dB[ci][:, :].unsqueeze(2).broadcast_to([128, 16, 64]), ALU.mult, [K(c, "h"), ("dB", ci)], ["yf"])
                TT("dve", c.h[:, 0:512], t1[:, 0:512], bank[2][:, :], ALU.add, ["yf", BK(2)], [K(c, "h")])
                TT("dve", c.h[:, 512:1024], t1[:, 512:1024], bank[3][:, :], ALU.add, ["yf", BK(3)], [K(c, "h")])
                nb = (hb_idx[-1] + 1) % c.nhbf
                CP("act", c.hbf[nb][:, :], c.h[:, :], [K(c, "h")], [K(c, "hbf", nb)])
                hb_idx.append(nb)
                if ci == 0:
                    stt_group(0)
            c.hb = hb_idx[-1]
            if c.nchunk == 1:
                pass
            for g in range(4):
                bc, bs = (4, 5) if g % 2 == 0 else (6, 7)
                p = g % 2
                ACT(lm2[p][:T, 0:4 * T], tmp2[p][:T, 0:4 * T], AF.Exp, [("tmp", p)], [("lm", p)])
                ACT(ecs2[p][:, 0:4 * T].rearrange("p (a b) -> p a b", b=T), bk3(bs, 4, T), AF.Exp, [BK(bs)], [("ecs", p)])
                if g + 1 < 4:
                    stt_group(g + 1)
                TT("dve", MT2[p][:T, 0:4 * T].rearrange("p (a b) -> p a b", b=T),
                   lm2[p][:T, 0:4 * T].rearrange("p (a b) -> p a b", b=T),
                   bank[bc][:T, 0:T].unsqueeze(1).broadcast_to([T, 4, T]), ALU.mult, [("lm", p), BK(bc)], [("MT", p)])
                TT("dve", Cs2[p][:, 0:4 * T].rearrange("p (a b) -> p a b", b=T),
                   ecs2[p][:, 0:4 * T].rearrange("p (a b) -> p a b", b=T),
                   c.xbcT[:, 12 + g, tc_].unsqueeze(1).broadcast_to([128, 4, T]), ALU.mult, [("ecs", p)] + xk, [("Cs", p)])
                for j in (2 * g, 2 * g + 1):
                    by = 2 if j < 4 else 3
                    MM(bank[by][:, (j % 4) * 128:(j % 4) * 128 + T], dskD[:, j, :], c.xbcT[:, j, tc_], True, False,
                       ["dskD"] + xk, [BK(by)])
                    for half in range(2):
                        h = 2 * j + half
                        r = h - 4 * g
                        o = bank[by][half * 64:half * 64 + 64, (j % 4) * 128:(j % 4) * 128 + T]
                        MM(o, xd[:T, h * 64:(h + 1) * 64], MT2[p][:T, r * T:(r + 1) * T], False, False,
                           ["xd", ("MT", p)], [BK(by)])
                        for ci in range(c.nchunk):
                            r0 = ci * 64
                            hbk = hb_idx[ci]
                            MM(bank[by][half * 64:half * 64 + 64, (j % 4) * 128 + r0:(j % 4) * 128 + r0 + CL],
                               c.hbf[hbk][:, h * 64:(h + 1) * 64], Cs2[p][:, r * T + r0:r * T + r0 + CL], False,
                               ci == c.nchunk - 1, [K(c, "hbf", hbk), ("Cs", p)], [BK(by)])
                if g + 2 < 4:
                    pe_group(g + 2)
            yv = yf[:, 0:8 * T].rearrange("p (a b) -> p a b", b=T)
            TT("dve", yv[:, 0:4, :], bk3(2, 4, T), c.szT[:, 0:4, tc_], ALU.mult, [BK(2), K(c, "szT", t)], ["yf"])
            TT("dve", yv[:, 4:8, :], bk3(3, 4, T), c.szT[:, 4:8, tc_], ALU.mult, [BK(3), K(c, "szT", t)], ["yf"])
            ACT(sq[:, 0:8 * T], yf[:, 0:8 * T], AF.Square, ["yf"], ["sq"])
            for g in range(4):
                for jj in range(2):
                    MM(bank[0][:, g * 128:g * 128 + T], ones_b[:, :], sq[:, (2 * g + jj) * T:(2 * g + jj + 1) * T],
                       jj == 0, jj == 1, ["ones_b", "sq"], [BK(0)])
            rvv = rv[:, 0:4 * T].rearrange("p (a b) -> p a b", b=T)
            ACT(rvv, bk3(0, 4, T), AF.Ln, [BK(0)], ["rv"], scale=1.0 / 256.0, bias=EPS)
            ACT(rv[:, 0:4 * T], rv[:, 0:4 * T], AF.Exp, ["rv"], ["rv"], scale=-0.5)
            for j in range(8):
                STT("dve", c.szT[:, j, tc_], yv[:, j, :], normwT[:, j:j + 1], rvv[:, j // 2, :], ALU.mult, ALU.mult,
                    ["yf", "rv", "small"], [K(c, "szT", t)])

        def mix_stage(c, st):
            softplus_stage(c)
            for t in range(c.NT):
                ssd(c, t)

        def acc_banks(c, ch):
            if c.sample:
                return [4 * ((ch + 1) % 2)]
            return [4 * (ch % 2) + t for t in range(4)]

        def out_proj(ctxs):
            for ch in range(2):
                for kh in range(2):
                    W, wk = wnext()
                    for c in ctxs:
                        T = c.T
                        bks = acc_banks(c, ch)
                        for t in range(c.NT):
                            b = bks[t]
                            for kc in range(8):
                                if kh == 0:
                                    lhs = c.qT[:, kc, tcols(c, t)]
                                    rk = K(c, "qT", t, kc // 4)
                                else:
                                    lhs = c.szT[:, kc, tcols(c, t)]
                                    rk = K(c, "szT", t)
                                MM(bank[b][:T, :], lhs, W[:, kc, :], kh == 0 and kc == 0, kh == 1 and kc == 7,
                                   [wk, rk], [BK(b)])
                            if kh == 1:
                                xs_ = c.xh[:T, t, ch * 512:(ch + 1) * 512]
                                TT("dve", xs_, bank[b][:T, :], xs_, ALU.add, [BK(b), K(c, "xh", t)], [K(c, "xh", t)])

        def ffn(ctxs):
            gb = [0]
            for s in range(11):
                W, wk = wnext()
                for c in ctxs:
                    S = c.S
                    for mm in range(2):
                        m = 2 * s + mm
                        gb[0] = (gb[0] + 1) % 4
                        bG, bU = 2 * gb[0], 2 * gb[0] + 1
                        for kc in range(8):
                            MM(bank[bG][:, 0:S], W[:, kc, mm * 128:(mm + 1) * 128], c.uT[:, kc, 0:S], kc == 0, kc == 7,
                               [wk, K(c, "uT")], [BK(bG)])
                        for kc in range(8):
                            MM(bank[bU][:, 0:S], W[:, kc, 256 + mm * 128:256 + (mm + 1) * 128], c.uT[:, kc, 0:S],
                               kc == 0, kc == 7, [wk, K(c, "uT")], [BK(bU)])
                        sg = sgt[m % 2]
                        ACT(sg[:, 0:S], bank[bG][:, 0:S], AF.Silu, [BK(bG)], [("sgt", m % 2)])
                        TT("dve", c.hmid[:, m, 0:S], sg[:, 0:S], bank[bU][:, 0:S], ALU.mult,
                           [("sgt", m % 2), BK(bU)], [K(c, "hmid", m)] + ([K(c, "bigR")] if c.alias else []))
            for ch in range(2):
                for kp in range(3):
                    W, wk = wnext()
                    nk = 6 if kp == 2 else 8
                    for c in ctxs:
                        T = c.T
                        bks = acc_banks(c, ch)
                        for t in range(c.NT):
                            b = bks[t]
                            for kc in range(nk):
                                m = kp * 8 + kc
                                MM(bank[b][:T, :], c.hmid[:, m, tcols(c, t)], W[:, kc, :], kp == 0 and kc == 0,
                                   kp == 2 and kc == nk - 1, [wk, K(c, "hmid", m)] + bigW(c), [BK(b)])
                            if kp == 2:
                                xs_ = c.xh[:T, t, ch * 512:(ch + 1) * 512]
                                TT("dve", xs_, bank[b][:T, :], xs_, ALU.add, [BK(b), K(c, "xh", t)], [K(c, "xh", t)])

        def final_stage(c, st):
            T = c.T
            stats4(c)
            for t in range(c.NT):
                ob, okey = (yo, "yo") if t % 2 == 0 else (yf, "yf")
                STT("dve", ob[:T, :], c.xh[:T, t, :], rstd4[:T, t:t + 1], fg[:T, :], ALU.mult, ALU.mult,
                    [K(c, "xh", t), "rstd4", "fg"], [okey])
                if c.sample:
                    DMA("sp", ys_d[c.seq], ob[:T, :], [okey], [])
                else:
                    r0 = (st * 4 + t) * 128
                    DMA("sp", yp_d[r0:r0 + 128, :], ob[:T, :], [okey], [])

        def state_out(c, dst):
            for half in range(2):
                b = 2 + half
                for q in range(4):
                    blk = half * 4 + q
                    TR(bank[b][:, q * 128:(q + 1) * 128], c.h[:, blk * 128:(blk + 1) * 128], ident_f[:, :],
                       [K(c, "h"), "ident_f"], [BK(b)])
                CP("act" if half else "dve", sst[:, half * 4:half * 4 + 4, :],
                   bank[b][:, :].rearrange("p (a b) -> p a b", b=128), [BK(b)], ["yo"])
            DMA("sp", dst.rearrange("(b p) n -> p b n", p=128), sst[:, :, :], ["yo"], [])

        def sample_init(c, seq):
            c.seq = seq
            DMA("sp", c.xh[:16, 0, :], xs_d[seq], [], [K(c, "xh", 0)])
            DMA("pool", xd[:, 0:256], ck_d[seq], [], ["xd"])
            bv = bkbf(0)
            for kc in range(2):
                TR(bv[:, kc * 128:(kc + 1) * 128], xd[:, kc * 128:(kc + 1) * 128], ident_b[:, :], ["xd", "ident_b"], [BK(0)])
            CP("dve", c.kT[:, :, 0:128], bv[:, 0:256].rearrange("p (a b) -> p a b", b=128), [BK(0)], [K(c, "kT")])
            DMA("pool", c.vx[:, 0, :], cv_d[seq], [], [K(c, "vx")])
            DMA("sp", sconv_sb[:, :], sconv_d[seq], [], ["sconv_sb"])
            for ch in range(16):
                TR(bank[1][:, ch * 4:ch * 4 + 3], sconv_sb[0:3, ch * 128:(ch + 1) * 128], ident_f[0:3, 0:3],
                   ["sconv_sb", "ident_f"], [BK(1)])
            CP("dve", c.rawT[:, :, 0:3], bank[1][:, 0:64].rearrange("p (a b) -> p a b", b=4)[:, :, 0:3], [BK(1)],
               [K(c, "rawT", ch) for ch in range(16)])
            DMA("sp", sst[:, :, :], sssm_d[seq].rearrange("(b p) n -> p b n", p=128), [], ["yo"])
            for half in range(2):
                b = 2 + half
                for q in range(4):
                    blk = half * 4 + q
                    TR(bank[b][:, q * 128:(q + 1) * 128], sst[:, blk, :], ident_f[:, :], ["yo", "ident_f"], [BK(b)])
                CP("dve", c.h[:, half * 512:(half + 1) * 512], bank[b][:, :], [BK(b)], [K(c, "h")])
            c.hb = 0
            CP("act", c.hbf[0][:, :], c.h[:, :], [K(c, "h")], [K(c, "hbf", 0)])

        MS("dve", pc.h[:, :], 0.0, [K(pc, "h")])
        pc.hb = 0
        MS("pool", pc.hbf[0][:, :], 0.0, [K(pc, "hbf", 0)])
        MS("pool", pc.chist[:, :, :], 0.0, [K(pc, "chist")])

        for st in range(NPASS):
            ctxs = [pc]
            import os as _os2
            NOS = bool(_os2.environ.get("DEBUG_NOSAMPLE"))
            if st < 2 and not NOS:
                ctxs.append(sc)
                sample_init(sc, st)
            for t in range(4):
                r0 = (st * 4 + t) * 128
                DMA("sp", pc.xh[:, t, :], xp_d[r0:r0 + 128, :], [], [K(pc, "xh", t)])
            LIM = int(_os2.environ.get("STAGE_LIMIT", "99"))
            if LIM >= 1:
                for c in ctxs:
                    norm_stage(c, g1T)
            if LIM >= 2:
                in_proj(ctxs, st)
            if LIM >= 3:
                for c in ctxs:
                    conv_stage(c)
            if LIM >= 4:
                for c in ctxs:
                    mix_stage(c, st)
            if LIM >= 5:
                out_proj(ctxs)
            if LIM >= 6:
                for c in ctxs:
                    norm_stage(c, g2T)
            if LIM >= 7:
                ffn(ctxs)
            if LIM >= 8:
                for c in ctxs:
                    final_stage(c, st)
            if st < 2 and not NOS and LIM >= 9:
                state_out(sc, ssn_d[st])
        if LIM >= 9:
            state_out(pc, sp_d)
        P.emit_all()
    return nc


_CACHE = {}


def kernel(**inputs):
    f = np.float32
    shared = _host_layout(inputs)
    xp = np.asarray(inputs["x_prompt"], f)
    xs = np.asarray(inputs["x_sample"], f)
    ck = np.asarray(inputs["cache_k"], f)[0].reshape(16, 128, 256)
    cv = np.asarray(inputs["cache_v"], f)[0].reshape(16, 128, 256)
    sconv = np.asarray(inputs["state_conv"], f)[0]
    sssm = np.asarray(inputs["state_ssm"], f)[0].reshape(16, 1024, 128)
    in_maps = []
    for i in range(NCORES):
        m = dict(shared)
        m["xp"] = np.ascontiguousarray(xp[i])
        m["xs"] = np.ascontiguousarray(xs[2 * i:2 * i + 2])
        m["ck"] = np.ascontiguousarray(ck[2 * i:2 * i + 2])
        m["cv"] = np.ascontiguousarray(cv[2 * i:2 * i + 2])
        m["sconv"] = np.ascontiguousarray(sconv[2 * i:2 * i + 2])
        m["sssm"] = np.ascontiguousarray(sssm[2 * i:2 * i + 2])
        in_maps.append(m)
    if "nc" not in _CACHE:
        _CACHE["nc"] = build_program()
    nc = _CACHE["nc"]
    res = run_bass_kernel_spmd(nc, in_maps, core_ids=list(range(NCORES)))
    R = res.results
    y_prompt = np.stack([R[i]["yp"] for i in range(NCORES)]).astype(f)
    y_sample = np.concatenate([R[i]["ys"] for i in range(NCORES)]).astype(f)
    k_prompt = np.stack([R[i]["kp"] for i in range(NCORES)]).reshape(1, 8, 128, 4, 64).astype(f)
    v_prompt = np.stack([R[i]["vp"] for i in range(NCORES)]).reshape(1, 8, 128, 4, 64).astype(f)
    conv_prompt = np.stack([R[i]["cp"] for i in range(NCORES)]).reshape(1, 8, 3, 2048).astype(f)
    ssm_prompt = np.stack([R[i]["spo"] for i in range(NCORES)]).reshape(1, 8, 16, 64, 128).astype(f)
    k_sample = np.concatenate([R[i]["ksn"] for i in range(NCORES)]).reshape(1, 16, 16, 4, 64).astype(f)
    v_sample = np.concatenate([R[i]["vsn"] for i in range(NCORES)]).reshape(1, 16, 16, 4, 64).astype(f)
    conv_sample = np.concatenate([R[i]["csn"] for i in range(NCORES)]).reshape(1, 16, 3, 2048).astype(f)
    ssm_sample = np.concatenate([R[i]["ssn"] for i in range(NCORES)]).reshape(1, 16, 16, 64, 128).astype(f)
    return (y_prompt, y_sample, k_prompt, v_prompt, conv_prompt, ssm_prompt,
            k_sample, v_sample, conv_sample, ssm_sample)
```
